# Optimizing a Trainium2 kernel written in Bass

```python
import math
import jax, jax.numpy as jnp
from jax import lax
import numpy as np

D_MODEL = 1024
BATCH = 16
SEQ = 256
DEPTH = 1
DEC_BATCH = 8
DEC_SEQ = 1024
PAST_LEN = 256

F32 = jnp.float32
GRID_W = 64
DA_HEADS = 4
DA_HEAD_DIM = 64
DA_WIDTH = DA_HEADS * 2 * DA_HEAD_DIM
ROPE_PAIRS_PER_AXIS = DA_HEAD_DIM // 4
ROPE_BASE = 10000.0
Q_BLOCK = 128
RW_HEADS = 8
RW_HEAD_DIM = 64
RW_WIDTH = RW_HEADS * RW_HEAD_DIM
DECAY_LORA = 64
AAA_LORA = 64
GATE_LORA = 128
RW_COLS = 3 * RW_WIDTH + DECAY_LORA + AAA_LORA + GATE_LORA
RW_LNX_EPS = 64e-5
N_IN = 3 * DA_WIDTH + RW_COLS + 2 * D_MODEL
D_FF = 2816
CONV_W = 3
LN_EPS = 1e-5
ALPHA = (2.0 * DEPTH) ** 0.25
BETA = (8.0 * DEPTH) ** -0.25

kernel_name = 'hybrid_diffattn_birwkv7_prefix_dit_step'


def layer_norm(x, g, b, eps=LN_EPS):
    xf = x.astype(F32)
    mu = jnp.mean(xf, -1, keepdims=True)
    var = jnp.mean(jnp.square(xf - mu), -1, keepdims=True)
    return ((xf - mu) * lax.rsqrt(var + eps) * g + b).astype(x.dtype)


def rms_norm(x, g, eps=LN_EPS):
    xf = x.astype(F32)
    return (xf * lax.rsqrt(jnp.mean(jnp.square(xf), -1, keepdims=True) + eps) * g).astype(x.dtype)


def centred_taps(x):
    xp = jnp.pad(x, ((0, 0), (1, 1), (0, 0)))
    return xp[:, :-2], xp[:, 2:]


def axial_rope_tables(n):
    rows = n // GRID_W
    row = jnp.repeat(jnp.arange(rows, dtype=F32), GRID_W)
    col = jnp.tile(jnp.arange(GRID_W, dtype=F32), rows)
    inv = ROPE_BASE ** (-jnp.arange(ROPE_PAIRS_PER_AXIS, dtype=F32) / ROPE_PAIRS_PER_AXIS)
    ang = jnp.concatenate([row[:, None] * inv, col[:, None] * inv], -1)
    return jnp.cos(ang), jnp.sin(ang)


def apply_rope(x, cos, sin):
    half = x.shape[-1] // 2
    c = cos[None, :, None, None, :]
    s = sin[None, :, None, None, :]
    xf = x.astype(F32)
    x1, x2 = xf[..., :half], xf[..., half:]
    return jnp.concatenate([x1 * c - x2 * s, x2 * c + x1 * s], -1).astype(x.dtype)


def diff_attention(q, k, v, lam):
    B, Tq, H, _, d = q.shape
    nb = Tq // Q_BLOCK
    qb = jnp.moveaxis(q.reshape(B, nb, Q_BLOCK, H, 2, d), 1, 0)
    scale = d ** -0.5

    def block(qblk):
        s = jnp.einsum('bqhmd,bkhmd->bmhqk', qblk, k).astype(F32) * scale
        pr = jax.nn.softmax(s, axis=-1)
        a = pr[:, 0] - lam * pr[:, 1]
        return jnp.einsum('bhqk,bkhe->bqhe', a.astype(v.dtype), v)

    o = lax.map(block, qb)
    return jnp.moveaxis(o, 0, 1).reshape(B, Tq, H, 2 * d)


def rwkv7_scan(s0, r, w, k, v, a, b, reverse):
    xs = tuple(jnp.moveaxis(t.astype(F32), 1, 0) for t in (r, w, k, v, a, b))

    def step(S, inp):
        r_t, w_t, k_t, v_t, a_t, b_t = inp
        sa = jnp.einsum('bhvk,bhk->bhv', S, a_t)
        S = S * w_t[:, :, None, :] + sa[..., None] * b_t[:, :, None, :] + v_t[..., None] * k_t[:, :, None, :]
        return S, jnp.einsum('bhvk,bhk->bhv', S, r_t)

    s_final, ys = lax.scan(step, s0.astype(F32), xs, reverse=reverse)
    return jnp.moveaxis(ys, 0, 1), s_final


def trunk_layer(x, mod, p, layer, ctx=None):
    B, T, _ = x.shape
    sh1, sc1, g1, sh2, sc2, g2 = jnp.split(mod, 6, axis=-1)
    h = x * (1 + sc1) + sh1
    proj = h @ p['w_in']
    q, k, v, rw, gates = jnp.split(proj, [DA_WIDTH, 2 * DA_WIDTH, 3 * DA_WIDTH, 3 * DA_WIDTH + RW_COLS], axis=-1)

    q = q.reshape(B, T, DA_HEADS, 2, DA_HEAD_DIM)
    k = k.reshape(B, T, DA_HEADS, 2, DA_HEAD_DIM)
    v = v.reshape(B, T, DA_HEADS, 2 * DA_HEAD_DIM)
    lam_init = 0.8 - 0.6 * math.exp(-0.3 * layer)
    lq = p['da_lambda'].astype(F32)
    lam = jnp.exp(jnp.sum(lq[0] * lq[1])) - jnp.exp(jnp.sum(lq[2] * lq[3])) + lam_init
    if ctx is None:
        o_att = diff_attention(q, k, v, lam)
    else:
        k_ctx, v_ctx, s_ctx = ctx
        cos, sin = axial_rope_tables(T)
        keys = jnp.concatenate([apply_rope(k, cos, sin), k_ctx.astype(k.dtype)], axis=1)
        vals = jnp.concatenate([v, v_ctx.astype(v.dtype)], axis=1)
        o_att = diff_attention(apply_rope(q, cos, sin), keys, vals, lam)
    o_att = (rms_norm(o_att, p['da_subln_g']) * (1 - lam_init)).reshape(B, T, DA_WIDTH)

    prev, nxt = centred_taps(rw)
    mu = p['rw_mu']
    rw = rw + mu[0] * (prev - rw) + mu[1] * (nxt - rw)
    r, kr, vr, w_lo, a_lo, g_lo = jnp.split(
        rw, [RW_WIDTH, 2 * RW_WIDTH, 3 * RW_WIDTH, 3 * RW_WIDTH + DECAY_LORA, 3 * RW_WIDTH + DECAY_LORA + AAA_LORA], axis=-1)

    def heads(t):
        return t.reshape(B, T, RW_HEADS, RW_HEAD_DIM)

    r_h, v_h = heads(r), heads(vr)
    kk = heads((kr * p['rw_k_k']).astype(F32))
    kk = kk / jnp.maximum(jnp.sqrt(jnp.sum(kk * kk, -1, keepdims=True)), 1e-12)
    ys, bonus, states = [], [], []
    for d in range(2):
        w = -jax.nn.softplus(-(p['rw_w0'][d] + jnp.tanh(w_lo) @ p['rw_w_up'][d])) - 0.5
        decay = jnp.exp(-jnp.exp(w.astype(F32)))
        a = jax.nn.sigmoid(p['rw_a0'][d] + a_lo @ p['rw_a_up'][d])
        k_eff = heads(kr * (1 + (a - 1) * p['rw_k_a']))
        a_h = heads(a).astype(F32)
        if ctx is None:
            s0 = jnp.zeros((B, RW_HEADS, RW_HEAD_DIM, RW_HEAD_DIM), F32)
        else:
            s0 = s_ctx[:, d]
        y_d, s_d = rwkv7_scan(s0, r_h, heads(decay), k_eff, v_h, -kk, kk * a_h, reverse=(d == 1))
        ys.append(y_d)
        states.append(s_d)
        bonus.append(jnp.sum(r_h.astype(F32) * k_eff.astype(F32) * p['rw_r_k'], -1, keepdims=True))
    y_rw = layer_norm(ys[0] + ys[1], p['rw_lnx_g'].reshape(RW_HEADS, RW_HEAD_DIM),
                      p['rw_lnx_b'].reshape(RW_HEADS, RW_HEAD_DIM), eps=RW_LNX_EPS)
    y_rw = (y_rw + (bonus[0] + bonus[1]) * v_h.astype(F32)).reshape(B, T, RW_WIDTH).astype(x.dtype)
    g_rw = jax.nn.sigmoid(g_lo) @ p['rw_g_up']
    o_rw = (y_rw * g_rw) @ p['w_o_rwkv']

    ga, gb = jnp.split(jax.nn.sigmoid(gates), 2, axis=-1)
    mix = (ga * (o_att @ p['w_o_attn']) + gb * o_rw) @ p['w_out']
    x = layer_norm(ALPHA * x + g1 * mix, p['ln1_g'], p['ln1_b'])

    h = x * (1 + sc2) + sh2
    u, val = jnp.split(h @ p['w_up'], 2, axis=-1)
    prev, nxt = centred_taps(u)
    cw = p['conv_w']
    u = prev * cw[0] + u * cw[1] + nxt * cw[2] + p['conv_b']
    f = (jax.nn.gelu(u) * val) @ p['w_down']
    x = layer_norm(ALPHA * x + g2 * f, p['ln2_g'], p['ln2_b'])
    new_ctx = (k, v, jnp.stack(states, axis=1)) if ctx is None else None
    return x, new_ctx


def setup_inputs(seed: int = 0) -> dict:
    key = jax.random.key(seed)
    ks = iter(jax.random.split(key, 48))

    def nrm(shape, scale):
        return scale * jax.random.normal(next(ks), shape, F32)

    L = DEPTH
    D = D_MODEL
    return {
        'x_prompt': nrm((BATCH, SEQ, D), 1.0),
        'x_sample': nrm((DEC_BATCH, DEC_SEQ, D), 1.0),
        'cache_k': nrm((DEC_BATCH, L, PAST_LEN, DA_HEADS, 2, DA_HEAD_DIM), 1.0),
        'cache_v': nrm((DEC_BATCH, L, PAST_LEN, DA_HEADS, 2 * DA_HEAD_DIM), 1.0),
        'state_rwkv': nrm((DEC_BATCH, L, 2, RW_HEADS, RW_HEAD_DIM, RW_HEAD_DIM), 0.1),
        'c': nrm((DEC_BATCH, D), 1.0),
        'c_ctx': nrm((D,), 1.0),
        'w_ada': nrm((L, D, 6 * D), D ** -0.5),
        'b_ada': nrm((L, 6 * D), 0.02),
        'w_in': nrm((L, D, N_IN), D ** -0.5),
        'rw_mu': jax.random.uniform(next(ks), (L, 2, RW_COLS), F32, 0.0, 0.5),
        'rw_w0': nrm((L, 2, RW_WIDTH), 0.5),
        'rw_w_up': nrm((L, 2, DECAY_LORA, RW_WIDTH), 0.5 * DECAY_LORA ** -0.5),
        'rw_a0': nrm((L, 2, RW_WIDTH), 0.3),
        'rw_a_up': nrm((L, 2, AAA_LORA, RW_WIDTH), AAA_LORA ** -0.5),
        'rw_g_up': nrm((L, GATE_LORA, RW_WIDTH), GATE_LORA ** -0.5),
        'rw_k_k': 0.85 + nrm((L, RW_WIDTH), 0.05),
        'rw_k_a': 1.0 + nrm((L, RW_WIDTH), 0.05),
        'rw_r_k': nrm((L, RW_HEADS, RW_HEAD_DIM), 0.1),
        'rw_lnx_g': 1.0 + nrm((L, RW_WIDTH), 0.05),
        'rw_lnx_b': nrm((L, RW_WIDTH), 0.02),
        'da_lambda': nrm((L, 4, DA_HEAD_DIM), 0.1),
        'da_subln_g': 1.0 + nrm((L, 2 * DA_HEAD_DIM), 0.05),
        'w_o_attn': nrm((L, DA_WIDTH, D), DA_WIDTH ** -0.5),
        'w_o_rwkv': nrm((L, RW_WIDTH, D), RW_WIDTH ** -0.5),
        'w_out': nrm((L, D, D), BETA * D ** -0.5),
        'ln1_g': 1.0 + nrm((L, D), 0.05),
        'ln1_b': nrm((L, D), 0.02),
        'w_up': nrm((L, D, 2 * D_FF), D ** -0.5),
        'conv_w': nrm((L, CONV_W, D_FF), CONV_W ** -0.5),
        'conv_b': nrm((L, D_FF), 0.02),
        'w_down': nrm((L, D_FF, D), BETA * D_FF ** -0.5),
        'ln2_g': 1.0 + nrm((L, D), 0.05),
        'ln2_b': nrm((L, D), 0.02),
    }


def reference(x_prompt, x_sample, cache_k, cache_v, state_rwkv, c, c_ctx, w_ada, b_ada, w_in,
              rw_mu, rw_w0, rw_w_up, rw_a0, rw_a_up, rw_g_up, rw_k_k, rw_k_a, rw_r_k,
              rw_lnx_g, rw_lnx_b, da_lambda, da_subln_g, w_o_attn, w_o_rwkv, w_out,
              ln1_g, ln1_b, w_up, conv_w, conv_b, w_down, ln2_g, ln2_b):
    y_prompt = x_prompt
    y_sample = x_sample
    new_k, new_v, new_s = [], [], []
    for l in range(DEPTH):
        p = {
            'w_in': w_in[l], 'rw_mu': rw_mu[l], 'rw_w0': rw_w0[l], 'rw_w_up': rw_w_up[l],
            'rw_a0': rw_a0[l], 'rw_a_up': rw_a_up[l], 'rw_g_up': rw_g_up[l], 'rw_k_k': rw_k_k[l],
            'rw_k_a': rw_k_a[l], 'rw_r_k': rw_r_k[l], 'rw_lnx_g': rw_lnx_g[l], 'rw_lnx_b': rw_lnx_b[l],
            'da_lambda': da_lambda[l], 'da_subln_g': da_subln_g[l], 'w_o_attn': w_o_attn[l],
            'w_o_rwkv': w_o_rwkv[l], 'w_out': w_out[l], 'ln1_g': ln1_g[l], 'ln1_b': ln1_b[l],
            'w_up': w_up[l], 'conv_w': conv_w[l], 'conv_b': conv_b[l], 'w_down': w_down[l],
            'ln2_g': ln2_g[l], 'ln2_b': ln2_b[l],
        }
        mod_ctx = (jax.nn.silu(c_ctx) @ w_ada[l] + b_ada[l])[None, None, :]
        mod_lat = (jax.nn.silu(c) @ w_ada[l] + b_ada[l])[:, None, :]
        y_prompt, ctx_l = trunk_layer(y_prompt, mod_ctx, p, l)
        new_k.append(ctx_l[0])
        new_v.append(ctx_l[1])
        new_s.append(ctx_l[2])
        y_sample, _ = trunk_layer(y_sample, mod_lat, p, l,
                                  ctx=(cache_k[:, l], cache_v[:, l], state_rwkv[:, l]))
    return (y_prompt, y_sample, jnp.stack(new_k, axis=1), jnp.stack(new_v, axis=1), jnp.stack(new_s, axis=1))
```

```python
from contextlib import ExitStack
import numpy as np
import concourse.bass as bass
import concourse.mybir as mybir
from concourse.bass_utils import run_bass_kernel_spmd

F32 = mybir.dt.float32
F32R = mybir.dt.float32r
BF16 = mybir.dt.bfloat16
AF = mybir.ActivationFunctionType
ALU = mybir.AluOpType
AX = mybir.AxisListType


ATTACH_WAIT = True
ACT_FAST_COPY = True
STRICT_SYNC = True


class Tile:
    def __init__(self, name, handle, nslots=1):
        self.name = name
        self.h = handle
        self.nslots = nslots
        self.w = [None] * nslots
        self.r = [dict() for _ in range(nslots)]

    def ap(self):
        return self.h


def _norm(items):
    out = []
    for it in items:
        if isinstance(it, Tile):
            for s in range(it.nslots):
                out.append((it, s))
        else:
            t, sl = it
            if isinstance(sl, int):
                out.append((t, sl))
            else:
                for s in sl:
                    out.append((t, s))
    return out


class KB:
    ENG = ("pe", "act", "dve", "pool", "sp")
    NDS = 8

    def __init__(self, nc):
        self.nc = nc
        self.stack = ExitStack()
        self.ops = {e: [] for e in self.ENG}
        self.waited = {e: {} for e in self.ENG}
        self.sems = {e: self.stack.enter_context(nc.semaphore("s_" + e)) for e in self.ENG}
        self.dsem = {}
        self.dval = {}
        self.dnext = {}
        for q in ("sp", "act", "pool"):
            self.dsem[q] = [self.stack.enter_context(nc.semaphore("d_%s%d" % (q, i))) for i in range(self.NDS)]
            self.dval[q] = [0] * self.NDS
            self.dnext[q] = 0
        self.out_events = []
        self.ntiles = 0
        self.vcs = {}

    ARENA_COLS = 53000

    def _init_arena(self):
        self.arena = self.stack.enter_context(self.nc.sbuf_tensor("arena", [128, self.ARENA_COLS], F32))
        self.free_list = [(0, self.ARENA_COLS)]
        self.grave = []
        self.peak = 0

    def tile(self, name, shape, dtype=F32, nslots=1):
        if not hasattr(self, "arena"):
            self._init_arena()
        shape = list(shape)
        n = 1
        for d in shape[1:]:
            n *= d
        esz = 2 if dtype == BF16 else 4
        cols = (n * esz + 3) // 4
        for i, (lo, hi) in enumerate(self.free_list):
            if hi - lo >= cols:
                break
        else:
            raise RuntimeError("arena full allocating %s (%d cols); free=%s" % (name, cols, self.free_list))
        self.free_list[i:i + 1] = [(lo + cols, hi)] if hi - lo > cols else []
        self.peak = max(self.peak, lo + cols)
        v = self.arena[0:shape[0], lo:lo + cols]
        if dtype != F32:
            v = v.bitcast(dtype)
        if esz == 2 and (n % 2):
            v = v[:, 0:n]
        if len(shape) == 3:
            v = v.rearrange("p (a b) -> p a b", a=shape[1])
        elif len(shape) == 4:
            v = v.rearrange("p (a b c) -> p a b c", a=shape[1], b=shape[2])
        t = Tile(name, v, nslots)
        t.rng = (lo, lo + cols)
        keep = []
        for (glo, ghi, evs) in self.grave:
            if glo < lo + cols and lo < ghi:
                for s in range(nslots):
                    for k, ev in evs.items():
                        old = t.r[s].get(k)
                        if old is None or old[-1] < ev[-1]:
                            t.r[s][k] = ev
            keep.append((glo, ghi, evs))
        self.grave = keep
        return t

    def free(self, *tiles):
        for t in tiles:
            evs = {}
            for s in range(t.nslots):
                cand = list(t.r[s].items())
                if t.w[s] is not None:
                    ev = t.w[s]
                    k = (ev[0], ev[1]) if ev[0] == "e" else (ev[0], ev[1], ev[2])
                    cand.append((k, ev))
                for k, ev in cand:
                    old = evs.get(k)
                    if old is None or old[-1] < ev[-1]:
                        evs[k] = ev
            lo, hi = t.rng
            self.grave.append((lo, hi, evs))
            self.free_list.append((lo, hi))
            self.free_list.sort()
            merged = []
            for (a, b) in self.free_list:
                if merged and merged[-1][1] == a:
                    merged[-1] = (merged[-1][0], b)
                else:
                    merged.append((a, b))
            self.free_list = merged

    def psum_banks(self, n=8):
        out = [Tile("ps%d" % i, self.stack.enter_context(self.nc.psum_tensor("psb%d" % i, [128, 512], F32))[:])
               for i in range(n)]
        for t in out:
            t.excl = True
        return out

    def _need(self, eng, ev, waits):
        if ev is None:
            return
        if ev[0] == "e":
            key = ("e", ev[1])
            val = ev[2]
        else:
            key = ("d", ev[1], ev[2])
            val = ev[3]
        if self.waited[eng].get(key, -1) >= val:
            return
        self.waited[eng][key] = val
        waits[key] = ev
        vc = self.vcs.get(ev)
        if vc:
            w = self.waited[eng]
            for k2, v2 in vc.items():
                if w.get(k2, -1) < v2:
                    w[k2] = v2

    def _record(self, eng, fn, reads, writes, dma_q=None, is_output=False):
        reads = _norm(reads)
        writes = _norm(writes)
        xr = [(t, s_) for (t, s_) in reads if getattr(t, "excl", False)]
        if xr:
            reads = [(t, s_) for (t, s_) in reads if not getattr(t, "excl", False)]
            writes = writes + [x for x in xr if x not in writes]
        waits = {}
        for (t, s) in reads:
            ev = t.w[s]
            if ev is not None:
                if ev[0] == "e" and ev[1] == eng and dma_q is None and eng == "pe":
                    pass
                else:
                    self._need(eng, ev, waits)
        strict = STRICT_SYNC and eng != "pe"
        for (t, s) in writes:
            ev = t.w[s]
            if ev is not None and (strict or not (ev[0] == "e" and ev[1] == eng and dma_q is None)):
                self._need(eng, ev, waits)
            for ev in t.r[s].values():
                if strict or not (ev[0] == "e" and ev[1] == eng and dma_q is None):
                    self._need(eng, ev, waits)
        if dma_q is not None:
            q = dma_q
            k = self.dnext[q]
            self.dnext[q] = (k + 1) % self.NDS
            if self.dval[q][k] > 0:
                self._need(eng, ("d", q, k, self.dval[q][k]), waits)
            self.dval[q][k] += 16
            ev_out = ("d", q, k, self.dval[q][k])
            rec = dict(waits=list(waits.values()), fn=fn, marked=False, dma=(q, k))
            if is_output:
                self.out_events.append(ev_out)
        else:
            ev_out = ("e", eng, len(self.ops[eng]))
            rec = dict(waits=list(waits.values()), fn=fn, marked=False, dma=None)
        self.ops[eng].append(rec)
        snap = dict(self.waited[eng])
        if ev_out[0] == "e":
            if snap.get(("e", eng), -1) < ev_out[2] - 1:
                pass
        self.vcs[ev_out] = snap
        for (t, s) in writes:
            t.w[s] = ev_out
            t.r[s] = {}
        for (t, s) in reads:
            key = (ev_out[0], ev_out[1]) if ev_out[0] == "e" else (ev_out[0], ev_out[1], ev_out[2])
            t.r[s][key] = ev_out
        return ev_out

    def fresh(self, *pss):
        for p in pss:
            p.fresh = True

    def mm(self, ps, out, lhsT, rhs, reads, start=None, stop=None):
        st = bool(getattr(ps, "fresh", True))
        ps.fresh = False
        return self._record("pe", lambda e: e.matmul(out, lhsT, rhs, start=st, stop=True, skip_group_check=True),
                            reads, [ps])

    def transpose(self, ps, out, in_, ident, reads):
        return self._record("pe", lambda e: e.transpose(out, in_, ident), reads, [ps])

    def act(self, out, in_, func, reads, writes, bias=None, scale=None, accum_out=None):
        kw = {}
        if bias is not None:
            kw["bias"] = bias
        if scale is not None:
            kw["scale"] = scale
        if accum_out is not None:
            kw["accum_out"] = accum_out
        if ACT_FAST_COPY and func == AF.Copy and bias is None and accum_out is None:
            if scale is None:
                return self._record("act", lambda e: e.copy(out, in_), reads, writes)
            if isinstance(scale, float):
                return self._record("act", lambda e: e.mul(out, in_, scale), reads, writes)
        return self._record("act", lambda e: e.activation(out, in_, func, **kw), reads, writes)

    def dve(self, fn, reads, writes):
        return self._record("dve", fn, reads, writes)

    def pool(self, fn, reads, writes):
        return self._record("dve", fn, reads, writes)

    def gp(self, fn, reads, writes):
        return self._record("pool", fn, reads, writes)

    def any(self, eng, fn, reads, writes):
        return self._record(eng, fn, reads, writes)

    def dve_copy(self, out, in_, reads, writes):
        return self._record("dve", lambda e: e.tensor_copy(out, in_), reads, writes)

    def dma(self, q, out, in_, reads=(), writes=(), is_output=False, **kw):
        return self._record(q, lambda e: e.dma_start(out=out, in_=in_, **kw), reads, writes, dma_q=q,
                            is_output=is_output)

    def barrier(self):
        for eng in self.ENG:
            waits = {}
            for other in self.ENG:
                if other != eng and self.ops[other]:
                    idx = None
                    for i in range(len(self.ops[other]) - 1, -1, -1):
                        if self.ops[other][i]["dma"] is None and self.ops[other][i]["fn"] is not None:
                            idx = i
                            break
                    if idx is not None:
                        self._need(eng, ("e", other, idx), waits)
            for q in self.dsem:
                for k in range(self.NDS):
                    if self.dval[q][k] > 0:
                        self._need(eng, ("d", q, k, self.dval[q][k]), waits)
            if waits:
                self.ops[eng].append(dict(waits=list(waits.values()), fn=None, marked=False, dma=None))

    def finish(self):
        waits = {}
        for ev in self.out_events:
            self._need("sp", ev, waits)
        self.ops["sp"].append(dict(waits=list(waits.values()), fn=None, marked=False, dma=None))
        for eng in self.ENG:
            for rec in self.ops[eng]:
                for ev in rec["waits"]:
                    if ev[0] == "e":
                        self.ops[ev[1]][ev[2]]["marked"] = True
        val = {}
        for eng in self.ENG:
            c = 0
            for i, rec in enumerate(self.ops[eng]):
                if rec["marked"]:
                    c += 1
                    val[(eng, i)] = c
        self.nmarked = {e: sum(1 for r in self.ops[e] if r["marked"]) for e in self.ENG}
        kb = self

        def replay(engname):
            def run(e):
                for i, rec in enumerate(kb.ops[engname]):
                    ws = [(kb.sems[ev[1]], val[(ev[1], ev[2])]) if ev[0] == "e" else (kb.dsem[ev[1]][ev[2]], ev[3])
                          for ev in rec["waits"]]
                    attach = ws.pop() if (ATTACH_WAIT and ws and rec["fn"] is not None) else None
                    for (sm, vl) in ws:
                        e.wait_ge(sm, vl)
                    if rec["fn"] is None:
                        continue
                    ins = rec["fn"](e)
                    if attach is not None:
                        ins._wait_ge(attach[0], attach[1])
                    if rec["dma"] is not None:
                        q, k = rec["dma"]
                        ins.then_inc(kb.dsem[q][k], 16)
                    elif rec["marked"]:
                        ins.then_inc(kb.sems[engname], 1)
            return run

        with self.nc.Block() as block:
            block.tensor(replay("pe"))
            block.scalar(replay("act"))
            block.vector(replay("dve"))
            block.gpsimd(replay("pool"))
            block.sync(replay("sp"))
        self.stack.close()


D = 1024
NTOK = 1536
N_IN = 5376
D_FF = 2816
ALPHA_C = 2.0 ** 0.25
LAM_INIT = 0.8 - 0.6 * 1.0
CL = -float(np.exp(-0.5))
SCAN_F32 = True
FINE_YIELD = False
COARSE = False
SCAN_OFFSET = 0
NEU_BF16 = True
PASSES = [
    dict(t0=0, nT=1024, seqs=[(0, 1024)], g=0, sample=True),
    dict(t0=1024, nT=512, seqs=[(0, 256), (256, 256)], g=1, sample=False),
]
PV = {}
_o = 0
for _n, _c in [("b_ada", 48), ("mu0", 14), ("mu1", 14), ("a0_0", 4), ("a0_1", 4), ("k_k", 4), ("k_a", 4),
               ("ln1_g", 8), ("ln1_b", 8), ("ln2_g", 8), ("ln2_b", 8), ("cw0", 22), ("cw1", 22), ("cw2", 22),
               ("cb", 22)]:
    PV[_n] = (_o, _c)
    _o += _c
NPV = _o
CST = dict(ident=0, sw=128, blk=256, onesd=384, ti_f=512, te_f=640, ti_b=768, te_b=896,
           lt=1024, le=1152, gt=1280, ge=1408, cos=1536, sin=2560)
NCST = 3584


def make_consts():
    c = np.zeros((128, NCST), np.float32)
    idx = np.arange(128)
    c[:, 0:128] = np.eye(128, dtype=np.float32)
    sw = np.where((idx % 64) < 32, idx + 32, idx - 32)
    c[sw, 128 + idx] = 1.0
    c[:, 256:384] = (idx[:, None] // 64 == idx[None, :] // 64)
    c[:, 384:512] = 1.0 / 1024.0
    i = idx[:, None]
    t = idx[None, :]
    m = 63
    ti_f = ((i > m) & (i <= t)).astype(np.float32) - ((i > t) & (i <= m)).astype(np.float32)
    mb = 64
    ti_b = ((i >= t) & (i < mb)).astype(np.float32) - ((i >= mb) & (i < t)).astype(np.float32)
    eye = np.eye(128, dtype=np.float32)
    c[:, 512:640] = ti_f
    c[:, 640:768] = ti_f - eye
    c[:, 768:896] = ti_b
    c[:, 896:1024] = ti_b - eye
    c[:, 1024:1152] = (i < t)
    c[:, 1152:1280] = (i <= t)
    c[:, 1280:1408] = (i > t)
    c[:, 1408:1536] = (i >= t)
    pos = np.arange(1024)
    row = (pos // 64).astype(np.float32)
    col = (pos % 64).astype(np.float32)
    inv = (10000.0 ** (-np.arange(16, dtype=np.float32) / 16)).astype(np.float32)
    ang = np.concatenate([row[:, None] * inv, col[:, None] * inv], -1).astype(np.float32)
    cos = np.cos(ang).astype(np.float32).T
    sin = np.sin(ang).astype(np.float32).T
    pi = idx % 32
    sgn = np.where((idx % 64) < 32, -1.0, 1.0).astype(np.float32)
    c[:, 1536:2560] = cos[pi, :]
    c[:, 2560:3584] = sin[pi, :] * sgn[:, None]
    return c


def fm(v):
    v = np.asarray(v, np.float32).reshape(-1, 128)
    return np.ascontiguousarray(v.T)


def build_program(stop_after=None, passes=(0, 1), scan_steps=99):
    nc = bass.Bass("TRN2", target_bir_lowering=False)

    def din(name, shape):
        return nc.dram_tensor(name, list(shape), F32, kind="ExternalInput").ap()

    def dout(name, shape):
        return nc.dram_tensor(name, list(shape), F32, kind="ExternalOutput").ap()

    xin = din("xin", [NTOK, D])
    cvec_d = din("cvec", [128, 16])
    w_ada = din("w_ada", [D, 6 * D])
    w_in = din("w_in", [D, N_IN])
    w_oa = din("w_oa", [512, D])
    w_or = din("w_or", [512, D])
    w_out = din("w_out", [D, D])
    w_up = din("w_up", [D, 2 * D_FF])
    w_down = din("w_down", [D_FF, D])
    pvec_d = din("pvec", [128, NPV])
    rowp_d = din("rowp", [128, 1408])
    rkblk_d = din("rkblk", [128, 8])
    wupa_d = din("wupa", [65, 1024])
    aupt_d = din("aupt", [128, 1024])
    gup_d = din("gup", [128, 512])
    cst_d = din("cst", [128, NCST])
    ck_d = din("ck", [256, 512])
    cv_d = din("cv", [256, 512])
    st_d = din("st", [2, 512, 64])
    y_d = dout("y", [NTOK, D])
    nk_d = dout("nk", [512, 512])
    nv_d = dout("nv", [512, 512])
    ns_d = dout("ns", [2, 2, 512, 64])
    dbg_d = dout("dbg", [128, 8192]) if stop_after is not None else None

    kb = KB(nc)
    ps = kb.psum_banks()
    PB = [p.ap() for p in ps]
    PBb = [p.ap().bitcast(BF16) for p in ps]

    rr = {"ew": 0}

    def ew():
        rr["ew"] ^= 1
        return "dve" if rr["ew"] else "pool"

    def TS(out, in0, s1, s2, op0, op1):
        return lambda e: e.tensor_scalar(out, in0, s1, s2, op0, op1)

    def TT(out, a, b, op):
        return lambda e: e.tensor_tensor(out, a, b, op)

    def STT(out, in0, sc, in1, op0, op1):
        return lambda e: e.scalar_tensor_tensor(out, in0, sc, in1, op0, op1)

    def CP(out, in_):
        return lambda e: e.tensor_copy(out, in_)

    def bc(ap, shape):
        return ap.to_broadcast(list(shape))

    cst = kb.tile("cst", [128, 1536])
    cstb = kb.tile("cstb", [128, 1024], BF16)
    pv = kb.tile("pv", [128, NPV + 36])
    rowp = kb.tile("rowp", [128, 1408])
    rkblk = kb.tile("rkblk", [128, 8], BF16)
    wupa = kb.tile("wupa", [65, 1024], BF16)
    aupt = kb.tile("aupt", [128, 1024], BF16)
    gup = kb.tile("gup", [128, 512], BF16)
    modT = kb.tile("modT", [128, 96])
    lam = kb.tile("lam", [128, 4])

    kb.dma("sp", cst.ap(), cst_d[:, 0:1536], writes=[cst])
    kb.dma("sp", pv.ap()[:, 0:NPV], pvec_d, writes=[pv])
    kb.dma("sp", rowp.ap(), rowp_d, writes=[rowp])
    kb.dma("pool", rkblk.ap(), rkblk_d, writes=[rkblk])
    kb.dma("pool", wupa.ap(), wupa_d, writes=[wupa])
    kb.dma("pool", aupt.ap(), aupt_d, writes=[aupt])
    kb.dma("pool", gup.ap(), gup_d, writes=[gup])
    C = cst.ap()
    Cb = cstb.ap()
    kb.dve(CP(Cb[:, 0:384], C[:, 0:384]), [cst], [cstb])
    kb.dve(CP(Cb[:, 384:896], C[:, 512:1024]), [cst], [cstb])
    identF = C[:, 0:128]
    identB = Cb[:, 0:128]
    swB = Cb[:, 128:256]
    blkB = Cb[:, 256:384]
    onesd = C[:, 384:512]
    TIB = {0: Cb[:, 384:512], 1: Cb[:, 640:768]}
    TEB = {0: Cb[:, 512:640], 1: Cb[:, 768:896]}
    MASK = {k: C[:, CST[k]:CST[k] + 128] for k in ("lt", "le", "gt", "ge")}
    P = pv.ap()

    def pvc(name, j=None):
        o, n = PV[name]
        return P[:, o:o + n] if j is None else P[:, o + j:o + j + 1]

    c0o = NPV
    omkao = NPV + 14
    kb.dve(TT(P[:, c0o:c0o + 14], pvc("mu0"), pvc("mu1"), ALU.add), [pv], [pv])
    kb.dve(TS(P[:, c0o:c0o + 14], P[:, c0o:c0o + 14], -1.0, 1.0, ALU.mult, ALU.add), [pv], [pv])
    kb.dve(TS(P[:, omkao:omkao + 4], pvc("k_a"), -1.0, 1.0, ALU.mult, ALU.add), [pv], [pv])
    lt_ = kb.tile("lamtmp", [128, 128])
    R = rowp.ap()
    kb.dve(TT(lt_.ap()[:, 0:64], R[:, 1152:1216], R[:, 1216:1280], ALU.mult), [rowp], [lt_])
    kb.dve(TT(lt_.ap()[:, 64:128], R[:, 1280:1344], R[:, 1344:1408], ALU.mult), [rowp], [lt_])
    kb.dve(lambda e: e.tensor_reduce(lam.ap()[:, 2:4], lt_.ap().rearrange("p (a b) -> p a b", a=2), AX.X, ALU.add),
           [lt_], [lam])
    kb.act(lam.ap()[:, 2:4], lam.ap()[:, 2:4], AF.Exp, [lam], [lam])
    kb.dve(TT(lam.ap()[:, 0:1], lam.ap()[:, 2:3], lam.ap()[:, 3:4], ALU.subtract), [lam], [lam])
    kb.dve(TS(lam.ap()[:, 0:1], lam.ap()[:, 0:1], LAM_INIT, None, ALU.add, ALU.bypass) if False else
           (lambda e: e.tensor_scalar_add(lam.ap()[:, 0:1], lam.ap()[:, 0:1], LAM_INIT)), [lam], [lam])
    kb.dve(lambda e: e.tensor_scalar_mul(lam.ap()[:, 1:2], lam.ap()[:, 0:1], -1.0), [lam], [lam])
    kb.dve(lambda e: e.tensor_scalar_mul(R[:, 1024:1152], R[:, 1024:1152], 1.0 - LAM_INIT), [rowp], [rowp])
    kb.free(lt_)

    def phase0():
        cv32 = kb.tile("cv32", [128, 16])
        scb = kb.tile("scb", [128, 16], BF16)
        kb.dma("sp", cv32.ap(), cvec_d, writes=[cv32])
        kb.act(scb.ap(), cv32.ap(), AF.Silu, [cv32], [scb])
        wab = [kb.tile("wa%d" % i, [128, 8, 512], BF16) for i in range(2)]
        wada_v = w_ada.rearrange("(k p) n -> p k n", p=128)
        kb.fresh(ps[0])
        for cg in range(12):
            wt = wab[cg % 2]
            kb.dma("pool", wt.ap(), wada_v[:, :, cg * 512:(cg + 1) * 512], writes=[wt])
            for j in range(4):
                jj = cg * 4 + j
                for k in range(8):
                    kb.mm(ps[0], PB[0][:, 2 * jj:2 * jj + 2], wt.ap()[:, k, j * 128:(j + 1) * 128],
                          scb.ap()[:, 2 * k:2 * k + 2], [wt, scb], start=(k == 0), stop=(k == 7))
        kb.dve(TT(modT.ap().rearrange("p (j g) -> p j g", g=2), PB[0][:, 0:96].rearrange("p (j g) -> p j g", g=2),
                  bc(pvc("b_ada").unsqueeze(2), [128, 48, 2]), ALU.add), [ps[0], pv], [modT])
        kb.dve(lambda e: e.tensor_scalar_add(modT.ap()[:, 16:32], modT.ap()[:, 16:32], 1.0), [modT], [modT])
        kb.dve(lambda e: e.tensor_scalar_add(modT.ap()[:, 64:80], modT.ap()[:, 64:80], 1.0), [modT], [modT])
        kb.free(cv32, scb, *wab)

    M = modT.ap()

    def mod(kind, j, g):
        base = dict(sh1=0, sc1=8, g1=16, sh2=24, sc2=32, g2=40)[kind]
        col = (base + j) * 2 + g
        return M[:, col:col + 1]

    xin_v = xin
    dbgpos = {"o": 0}

    def dump(ap2d, reads, n):
        t = kb.tile("dbgt%d" % dbgpos["o"], [128, n])
        p = ap2d.shape[0]
        kb.dve(CP(t.ap()[0:p, :], ap2d), reads, [t])
        kb.dma("sp", dbg_d[0:p, dbgpos["o"]:dbgpos["o"] + n], t.ap()[0:p, :], reads=[t], is_output=True)
        print("dump at", dbgpos["o"], n)
        dbgpos["o"] += n

    def load_x_T_gen(ps_, pb, t0, nT, emit):
        xt = [kb.tile("xtok%d" % i, [128, D]) for i in range(2)]
        for tt in range(nT // 128):
            x_ = xt[tt % 2]
            kb.dma("sp", x_.ap(), xin_v[t0 + tt * 128:t0 + (tt + 1) * 128, :], writes=[x_])
            for half in range(2):
                b = (2 * tt + half) % 4
                for cc in range(4):
                    c = half * 4 + cc
                    kb.transpose(ps_[b], pb[b][:, cc * 128:(cc + 1) * 128], x_.ap()[:, c * 128:(c + 1) * 128], identF,
                                 [x_, cst])
                for cc in range(4):
                    c = half * 4 + cc
                    emit(c, tt, pb[b][:, cc * 128:(cc + 1) * 128], ps_[b])
            yield
        kb.free(*xt)

    def load_x_T(ps_, pb, t0, nT, emit):
        for _ in load_x_T_gen(ps_, pb, t0, nT, emit):
            pass

    win_v = w_in.rearrange("(k p) n -> p k n", p=128)
    mod_done = [False]

    def TSM(out, in_, sc):
        return lambda e: e.tensor_scalar_mul(out, in_, sc)

    def TSA(out, in_, sc):
        return lambda e: e.tensor_scalar_add(out, in_, sc)

    def RCP(out, in_):
        return lambda e: e.reciprocal(out, in_)

    def RSUM(out, in_):
        return lambda e: e.tensor_reduce(out, in_, AX.X, ALU.add)

    def MSET(out, v):
        return lambda e: e.memset(out, v)

    ecnt = {"i": 0}

    def evac_copy(out, in_, reads, writes):
        ecnt["i"] += 1
        if ecnt["i"] % 2:
            kb.act(out, in_, AF.Copy, reads, writes)
        else:
            kb.dve(CP(out, in_), reads, writes)


    for pi in passes:
        PS = PASSES[pi]
        t0, nT, seqs, g, sample = PS["t0"], PS["nT"], PS["seqs"], PS["g"], PS["sample"]
        NTT = nT // 128
        NN = nT // 512
        oattT = kb.tile("oattT", [128, 4, nT], BF16)
        hT = kb.tile("hT", [128, 8, nT], BF16, nslots=NTT)
        rT = kb.tile("rT", [128, 4, nT], BF16)
        krT = kb.tile("krT", [128, 4, nT], BF16)
        kkT = kb.tile("kkT", [128, 4, nT])
        SDT = F32 if SCAN_F32 else BF16
        Vt = kb.tile("Vt", [128, NTT, 512], BF16, nslots=NTT)
        tw = kb.tile("tw", [65, nT], BF16)
        alo = kb.tile("alo", [128, nT], BF16)
        sg = kb.tile("sg", [128, nT], BF16)
        yac = kb.tile("yac", [128, NTT, 512], F32, nslots=NTT)
        bon = kb.tile("bon", [128, NTT, 8], F32, nslots=NTT)

        cnt = {"i": 0}
        if not mod_done[0]:
            xT0 = kb.tile("xT0", [128, 8, nT], F32, nslots=NTT)

            def emit_raw(c, tt, pap, pst):
                evac_copy(xT0.ap()[:, c, tt * 128:(tt + 1) * 128], pap, [pst], [(xT0, tt)])

            load_x_T(ps, PB, t0, nT, emit_raw)
            phase0()
            mod_done[0] = True
            for c in range(8):
                for n in range(NN):
                    ns_ = slice(n * 512, (n + 1) * 512)
                    rd = [(xT0, range(4 * n, 4 * n + 4)), modT]
                    wr_ = [(hT, range(4 * n, 4 * n + 4))]
                    cnt["i"] += 1
                    if cnt["i"] % 2:
                        kb.act(hT.ap()[:, c, ns_], xT0.ap()[:, c, ns_], AF.Identity, rd, wr_, bias=mod("sh1", c, g), scale=mod("sc1", c, g))
                    else:
                        kb.dve(TS(hT.ap()[:, c, ns_], xT0.ap()[:, c, ns_], mod("sc1", c, g), mod("sh1", c, g), ALU.mult, ALU.add), rd, wr_)
            kb.free(xT0)
        else:
            def emit_h(c, tt, pap, pst):
                out = hT.ap()[:, c, tt * 128:(tt + 1) * 128]
                cnt["i"] += 1
                if cnt["i"] % 2:
                    kb.act(out, pap, AF.Identity, [pst, modT], [(hT, tt)], bias=mod("sh1", c, g), scale=mod("sc1", c, g))
                else:
                    kb.dve(TS(out, pap, mod("sc1", c, g), mod("sh1", c, g), ALU.mult, ALU.add), [pst, modT], [(hT, tt)])

            load_x_T(ps, PB, t0, nT, emit_h)
        if stop_after == "A":
            dump(modT.ap(), [modT], 96)
            dump(hT.ap()[:, 0, :], [hT], nT)
            dump(hT.ap()[:, 7, :], [hT], nT)
            break

        ropet = stg = None
        if sample:
            ropet = kb.tile("ropet", [128, 2048])
            stg = kb.tile("stg", [128, 1024])
            kb.dma("act", ropet.ap(), cst_d[:, 1536:3584], writes=[ropet])
            cosT = ropet.ap()[:, 0:1024]
            sinT = ropet.ap()[:, 1024:2048]
        qT = kb.tile("qT", [128, 4, nT], BF16)
        kT = kb.tile("kT", [128, 4, nT], BF16)
        NKT = NTT + (2 if sample else 0)
        v1 = kb.tile("v1", [128, NKT, 4, 130], BF16, nslots=NKT)
        kb.pool(MSET(v1.ap()[:, :, :, 128:130], 1.0), [], [v1])
        wq = [kb.tile("wq%d" % i, [128, 8, 512], BF16) for i in range(2)]
        qraw = [kb.tile("qraw%d" % i, [128, 512], BF16) for i in range(2)]
        rt1 = [kb.tile("rt1_%d" % i, [128, 512]) for i in range(2)]
        rt2 = [kb.tile("rt2_%d" % i, [128, 512]) for i in range(2)]
        kf32 = None
        if not sample:
            kf32 = kb.tile("kf32", [128, 4, nT])
        it = 0
        for wi, (dst, col0, scl) in enumerate([(qT, 0, 0.125), (kT, 512, 1.0)]):
            wt = wq[wi % 2]
            kb.dma("pool", wt.ap(), win_v[:, :, col0:col0 + 512], writes=[wt])
            for c in range(4):
                for n in range(NN):
                    b = it % 4
                    kb.fresh(ps[b])
                    for k in range(8):
                        kb.mm(ps[b], PB[b], wt.ap()[:, k, c * 128:(c + 1) * 128], hT.ap()[:, k, n * 512:(n + 1) * 512],
                              [wt, (hT, range(4 * n, 4 * n + 4))], start=(k == 0), stop=(k == 7))
                    outap = dst.ap()[:, c, n * 512:(n + 1) * 512]
                    if sample:
                        qr = qraw[it % 2]
                        kb.act(qr.ap(), PB[b], AF.Copy, [ps[b]], [qr], scale=scl)
                        b2 = 4 + it % 2
                        kb.fresh(ps[b2])
                        kb.mm(ps[b2], PB[b2], swB, qr.ap(), [cstb, qr])
                        a1, a2 = rt1[it % 2], rt2[it % 2]
                        kb.pool(TT(a1.ap(), qr.ap(), cosT[:, n * 512:(n + 1) * 512], ALU.mult), [qr, ropet], [a1])
                        kb.dve(TT(a2.ap(), PB[b2], sinT[:, n * 512:(n + 1) * 512], ALU.mult), [ps[b2], ropet], [a2])
                        kb.dve(TT(outap, a1.ap(), a2.ap(), ALU.add), [a1, a2], [dst])
                    else:
                        kb.act(outap, PB[b], AF.Copy, [ps[b]], [dst], scale=scl)
                        if dst is kT:
                            kb.dve(CP(kf32.ap()[:, c, n * 512:(n + 1) * 512], PB[b]), [ps[b]], [kf32])
                    it += 1
        wt = wq[0]
        kb.dma("pool", wt.ap(), win_v[:, :, 1024:1536], writes=[wt])
        vout = None
        if not sample:
            vout = [kb.tile("vout%d" % i, [128, 512]) for i in range(2)]
        for tt in range(NTT):
            b = tt % 4
            kb.fresh(ps[b])
            for k in range(8):
                kb.mm(ps[b], PB[b], hT.ap()[:, k, tt * 128:(tt + 1) * 128], wt.ap()[:, k, :], [wt, (hT, tt)],
                      start=(k == 0), stop=(k == 7))
            kb.act(v1.ap()[:, tt, :, 0:128], PB[b].rearrange("p (h e) -> p h e", h=4), AF.Copy, [ps[b]], [(v1, tt)])
            if not sample:
                vo = vout[tt % 2]
                kb.dve(CP(vo.ap(), PB[b]), [ps[b]], [vo])
                kb.dma("sp", nv_d[tt * 128:(tt + 1) * 128, :], vo.ap(), reads=[vo], is_output=True)
        if not sample:
            for tt in range(NTT):
                b = 4 + tt % 2
                for c in range(4):
                    kb.transpose(ps[b], PB[b][:, c * 128:(c + 1) * 128], kf32.ap()[:, c, tt * 128:(tt + 1) * 128], identF,
                                 [kf32, cst])
                vo = vout[tt % 2]
                kb.dve(CP(vo.ap(), PB[b]), [ps[b]], [vo])
                kb.dma("sp", nk_d[tt * 128:(tt + 1) * 128, :], vo.ap(), reads=[vo], is_output=True)
            kb.free(kf32, *vout)
        kcT = None
        if sample:
            kcT = kb.tile("kcT", [128, 4, 256], BF16)
            for tl in range(2):
                kb.dma("sp", stg.ap()[:, 0:512], ck_d[tl * 128:(tl + 1) * 128, :], writes=[stg])
                kb.dma("act", stg.ap()[:, 512:1024], cv_d[tl * 128:(tl + 1) * 128, :], writes=[stg])
                b = 4 + tl
                for h in range(4):
                    kb.transpose(ps[b], PB[b][:, h * 128:(h + 1) * 128], stg.ap()[:, h * 128:(h + 1) * 128], identF,
                                 [stg, cst])
                kb.act(kcT.ap()[:, :, tl * 128:(tl + 1) * 128], PB[b].rearrange("p (h e) -> p h e", h=4), AF.Copy,
                       [ps[b]], [kcT])
                kb.dve(CP(v1.ap()[:, NTT + tl, :, 0:128], stg.ap()[:, 512:1024].rearrange("p (h e) -> p h e", h=4)),
                       [stg], [(v1, NTT + tl)])
        kb.free(*wq, *qraw, *rt1, *rt2)
        if sample:
            kb.free(ropet, stg)
        if stop_after == "Bproj":
            break

        otok = kb.tile("otok", [128, NTT, 512], F32, nslots=NTT)
        accs = kb.tile("accs", [128, 8, 130])
        rec = kb.tile("rec", [128, 8])
        exb = [kb.tile("exb%d" % i, [128, 512], BF16) for i in range(3)]
        tmp1 = kb.tile("atmp1", [128, 4, 128])
        ei_box = [0]
        si_box = [0]
        def attn_gen():
            for (s0, L) in seqs:
                QB = 512 if L >= 512 else L
                nqs = QB // 128
                ktiles = [("own", s0 // 128 + i) for i in range(L // 128)]
                if sample:
                    ktiles += [("cache", 0), ("cache", 1)]
                for h in range(4):
                    for qb in range(L // QB):
                        q0 = s0 + qb * QB
                        kb.fresh(ps[4], ps[5], ps[6])
                        def qk(ki, kind, kt, m):
                            sb = si_box[0] % 4
                            si_box[0] += 1
                            if kind == "own":
                                kl = kT.ap()[64 * m:64 * m + 64, h, kt * 128:(kt + 1) * 128]
                                kr_ = [kT]
                            else:
                                kl = kcT.ap()[64 * m:64 * m + 64, h, kt * 128:(kt + 1) * 128]
                                kr_ = [kcT]
                            kb.fresh(ps[sb])
                            kb.mm(ps[sb], PB[sb][:, 0:QB], kl, qT.ap()[64 * m:64 * m + 64, h, q0:q0 + QB], kr_ + [qT])
                            return sb

                        def qk2(i):
                            kind, kt = ktiles[i]
                            return [qk(i, kind, kt, 0), qk(i, kind, kt, 1)]

                        pend = [qk2(0)]
                        for ki, (kind, kt) in enumerate(ktiles):
                            if ki + 1 < len(ktiles):
                                pend.append(qk2(ki + 1))
                            sbs = pend.pop(0)
                            vt = kt if kind == "own" else NTT + kt
                            for m in range(2):
                                sb = sbs[m]
                                ex = exb[ei_box[0] % 3]
                                ei_box[0] += 1
                                kb.act(ex.ap()[:, 0:QB], PB[sb][:, 0:QB], AF.Exp, [ps[sb]], [ex])
                                for qs in range(nqs):
                                    a = m * 4 + qs
                                    ab = 4 + a // 3
                                    ao = (a % 3) * 130
                                    kb.mm(ps[ab], PB[ab][:, ao:ao + 129], ex.ap()[:, qs * 128:(qs + 1) * 128],
                                          v1.ap()[:, vt, h, 0:129], [ex, (v1, vt)])
                        A = accs.ap()
                        for bnk in range(3):
                            na = min(3, 8 - bnk * 3)
                            kb.act(A[:, bnk * 3:bnk * 3 + na, 0:129],
                                   PB[4 + bnk][:, 0:na * 130].rearrange("p (a e) -> p a e", a=na)[:, :, 0:129],
                                   AF.Copy, [ps[4 + bnk]], [accs])
                        kb.dve(RCP(rec.ap(), A[:, :, 128]), [accs], [rec])
                        kb.dve(TSM(rec.ap()[:, 4:8], rec.ap()[:, 4:8], lam.ap()[:, 1:2]), [rec, lam], [rec])
                        T1 = tmp1.ap()[:, 0:nqs, :]
                        kb.dve(TT(T1, A[:, 0:nqs, 0:128], bc(rec.ap()[:, 0:nqs].unsqueeze(2), [128, nqs, 128]), ALU.mult),
                               [accs, rec], [tmp1])
                        kb.pool(TT(A[:, 4:4 + nqs, 0:128], A[:, 4:4 + nqs, 0:128],
                                   bc(rec.ap()[:, 4:4 + nqs].unsqueeze(2), [128, nqs, 128]), ALU.mult), [accs, rec], [accs])
                        tq0 = q0 // 128
                        kb.dve(TT(otok.ap()[:, tq0:tq0 + nqs, h * 128:(h + 1) * 128], T1, A[:, 4:4 + nqs, 0:128], ALU.add),
                               [tmp1, accs], [(otok, range(tq0, tq0 + nqs))])
                        yield

        ATTN = attn_gen()
        rrb = {"i": 0}

        def nbank():
            i = rrb["i"]
            rrb["i"] = (i + 1) % 8
            kb.fresh(ps[i])
            return i

        def P3(b, a=4):
            return PB[b].rearrange("p (a e) -> p a e", a=a)

        def phc_gen():
            RW0 = 1536
            raw = [kb.tile("raw%d" % i, [128, nT]) for i in range(2)]
            wr = [kb.tile("wr%d" % i, [128, 8, 512], BF16) for i in range(2)]
            mixo = kb.tile("mixo", [128, nT])
            mixb = kb.tile("mixb", [128, nT], BF16)
            kb.pool(MSET(tw.ap()[64:65, :], 1.0), [], [tw])
            for c14 in range(14):
                wtile = wr[(c14 // 4) % 2]
                if c14 % 4 == 0:
                    ncol = min(512, 1792 - c14 * 128)
                    kb.dma("pool", wtile.ap()[:, :, 0:ncol], win_v[:, :, RW0 + c14 * 128:RW0 + c14 * 128 + ncol], writes=[wtile])

                class _W:
                    pass
                wt = _W()
                wt.t = wtile
                wt.v = wtile.ap()[:, :, (c14 % 4) * 128:(c14 % 4 + 1) * 128]
                rw_ = raw[c14 % 2]
                for n in range(NN):
                    b = nbank()
                    for k in range(8):
                        kb.mm(ps[b], PB[b], wt.v[:, k, :], hT.ap()[:, k, n * 512:(n + 1) * 512],
                              [wt.t, (hT, range(4 * n, 4 * n + 4))])
                    kb.act(rw_.ap()[:, n * 512:(n + 1) * 512], PB[b], AF.Copy, [ps[b]], [rw_])
                if c14 < 4:
                    dst, dt_ = rT.ap()[:, c14, :], rT
                elif c14 < 8:
                    dst, dt_ = krT.ap()[:, c14 - 4, :], krT
                else:
                    dst, dt_ = mixo.ap(), mixo
                kb.act(dst, rw_.ap(), AF.Copy, [rw_, pv], [dt_], scale=P[:, c0o + c14:c0o + c14 + 1])
                for (s0, L) in seqs:
                    kb.dve(STT(dst[:, s0 + 1:s0 + L], rw_.ap()[:, s0:s0 + L - 1], pvc("mu0", c14), dst[:, s0 + 1:s0 + L],
                               ALU.mult, ALU.add), [rw_, pv, dt_], [dt_])
                    kb.dve(STT(dst[:, s0:s0 + L - 1], rw_.ap()[:, s0 + 1:s0 + L], pvc("mu1", c14), dst[:, s0:s0 + L - 1],
                               ALU.mult, ALU.add), [rw_, pv, dt_], [dt_])
                if 8 <= c14 < 12:
                    vc = c14 - 8
                    kb.pool(CP(mixb.ap(), mixo.ap()), [mixo], [mixb])
                    for t8 in range(0, NTT, 8):
                        nt8 = min(8, NTT - t8)
                        b = nbank()
                        for i in range(nt8):
                            kb.transpose(ps[b], PBb[b][:, i * 128:(i + 1) * 128], mixb.ap()[:, (t8 + i) * 128:(t8 + i + 1) * 128],
                                         identB, [mixb, cstb])
                        kb.act(Vt.ap()[:, t8:t8 + nt8, vc * 128:(vc + 1) * 128],
                               PBb[b][:, 0:nt8 * 128].rearrange("p (a e) -> p a e", a=nt8), AF.Copy, [ps[b]],
                               [(Vt, range(t8, t8 + nt8))])
                elif c14 == 12:
                    kb.act(tw.ap()[0:64, :], mixo.ap()[0:64, :], AF.Tanh, [mixo], [tw])
                    kb.dve(CP(alo.ap()[64:128, :], mixo.ap()[64:128, :]), [mixo], [alo])
                elif c14 == 13:
                    kb.act(sg.ap(), mixo.ap(), AF.Sigmoid, [mixo], [sg])
                yield
            for c in range(4):
                kb.dve(TSM(kkT.ap()[:, c, :], krT.ap()[:, c, :], pvc("k_k", c)), [krT, pv], [kkT])
                kb.pool(TT(mixb.ap(), kkT.ap()[:, c, :], kkT.ap()[:, c, :], ALU.mult), [kkT], [mixb])
                for n in range(NN):
                    b = nbank()
                    kb.mm(ps[b], PB[b], blkB, mixb.ap()[:, n * 512:(n + 1) * 512], [cstb, mixb])
                    r_ = raw[0].ap()[:, n * 512:(n + 1) * 512]
                    kb.act(r_, PB[b], AF.Sqrt, [ps[b]], [raw[0]])
                    kb.dve(lambda e, r_=r_: e.tensor_scalar_max(r_, r_, 1e-12), [raw[0]], [raw[0]])
                    kb.dve(RCP(r_, r_), [raw[0]], [raw[0]])
                    kb.dve(TT(kkT.ap()[:, c, n * 512:(n + 1) * 512], kkT.ap()[:, c, n * 512:(n + 1) * 512], r_, ALU.mult),
                           [kkT, raw[0]], [kkT])
                yield

            kb.free(*raw, *wr, mixo, mixb)

        PHC = phc_gen()
        alive = {"a": ATTN, "c": PHC}
        n_att = sum((L // (512 if L >= 512 else L)) * 4 for (_s0, L) in seqs)
        per = -(-18 // n_att)
        while alive:
            for nm_, cnt_ in (("a", 1), ("c", per)):
                for _ in range(cnt_):
                    if nm_ in alive:
                        try:
                            next(alive[nm_])
                        except StopIteration:
                            del alive[nm_]
        kb.free(accs, rec, tmp1, *exb, qT, kT, v1)
        if kcT is not None:
            kb.free(kcT)
        sq = kb.tile("osq", [128, 512])
        ss = kb.tile("oss", [128, 4])
        for tt in range(NTT):
            O = otok.ap()[:, tt, :]
            O3 = O.rearrange("p (h e) -> p h e", h=4)
            kb.pool(TT(sq.ap(), O, O, ALU.mult), [(otok, tt)], [sq])
            kb.dve(RSUM(ss.ap(), sq.ap().rearrange("p (h e) -> p h e", h=4)), [sq], [ss])
            kb.dve(TS(ss.ap(), ss.ap(), 1.0 / 128.0, 1e-5, ALU.mult, ALU.add), [ss], [ss])
            kb.act(ss.ap(), ss.ap(), AF.Sqrt, [ss], [ss])
            kb.dve(RCP(ss.ap(), ss.ap()), [ss], [ss])
            kb.dve(TT(O3, O3, bc(ss.ap().unsqueeze(2), [128, 4, 128]), ALU.mult), [(otok, tt), ss], [(otok, tt)])
            ob = sq.ap().bitcast(BF16)[:, 0:512]
            kb.dve(TT(ob.rearrange("p (h e) -> p h e", h=4), O3, bc(R[:, 1024:1152].unsqueeze(1), [128, 4, 128]),
                      ALU.mult), [(otok, tt), rowp], [sq])
            b = tt % 2
            for h in range(4):
                kb.transpose(ps[b], PBb[b][:, h * 128:(h + 1) * 128], ob[:, h * 128:(h + 1) * 128], identB, [sq, cstb])
            kb.act(oattT.ap()[:, :, tt * 128:(tt + 1) * 128], PBb[b][:, 0:512].rearrange("p (h e) -> p h e", h=4),
                   AF.Copy, [ps[b]], [oattT])
        kb.free(sq, ss, otok)
        if stop_after == "B":
            for c_ in range(4):
                dump(oattT.ap()[:, c_, 0:512], [oattT], 512)
            dump(lam.ap(), [lam], 4)
            break

        if stop_after == "C":
            dump(rT.ap()[:, 0, 0:512], [rT], 512)
            dump(kkT.ap()[:, 1, 0:512], [kkT], 512)
            dump(Vt.ap()[:, 1, :], [Vt], 512)
            dump(tw.ap()[0:65, 0:512], [tw], 512)
            dump(sg.ap()[:, 0:512], [sg], 512)
            break

        kb.free(hT)

        def f3():
            return [128, 4, 128]

        def v3(t):
            return t.ap().rearrange("p (c e) -> p c e", c=4)

        def mkset(k):
            T_ = {}
            for nm in ["b0", "b1", "b2", "b3", "b4", "b5", "Wic", "Wlm", "Ah", "Bt", "Kt", "Rh", "Btok"]:
                T_[nm] = kb.tile("%s_%d" % (nm, k), [128, 512])
            for nm in ["h0", "h1", "rkp", "Ktok"]:
                T_[nm] = kb.tile("%s_%d" % (nm, k), [128, 512], BF16)
            T_["Gt"] = kb.tile("Gt_%d" % k, [128, 4])
            for nm in ["MrbT", "LabTf"]:
                T_[nm] = kb.tile("%s_%d" % (nm, k), [128, 2, 4, 128])
            for nm in ["LakT", "MrkT", "Pm0", "Pm1", "Ptm0", "Ptm1", "Inv0", "Inv1"]:
                T_[nm] = kb.tile("%s_%d" % (nm, k), [128, 2, 4, 128], BF16)
            return T_

        sets = [mkset(0), mkset(1)]
        ka_b = bc(pvc("k_a").unsqueeze(2), f3())
        omka_b = bc(P[:, omkao:omkao + 4].unsqueeze(2), f3())
        ydone = set()
        bdone = set()

        def unit(T_, S, d, tt):
            aT, keff, bq, Wex, Win, sig = T_["b0"], T_["b1"], T_["b2"], T_["b3"], T_["b4"], T_["b5"]
            Z0f, tmpS, Z0b, Xf, U0f, Ub = aT, keff, bq, Wex, Win, sig
            shi, slo = T_["h0"], T_["h1"]
            Xh, rb = shi, slo
            Wic, Wlm, Ah, Bt, Kt, Rh, Btok, Ktok, rkp, Gt = (T_[k_] for k_ in
                                                             ["Wic", "Wlm", "Ah", "Bt", "Kt", "Rh", "Btok", "Ktok", "rkp", "Gt"])
            MrbT, LabTf, LakT, MrkT = T_["MrbT"], T_["LabTf"], T_["LakT"], T_["MrkT"]
            Pm, Ptm, Inv = [T_["Pm0"], T_["Pm1"]], [T_["Ptm0"], T_["Ptm1"]], [T_["Inv0"], T_["Inv1"]]
            mk = dict(labT="lt", lab="gt", mr="le") if d == 0 else dict(labT="gt", lab="lt", mr="ge")
            tf, tl = (0, 127) if d == 0 else (127, 0)
            o = tt * 128
            sl = slice(o, o + 128)
            b0 = nbank()
            for c in range(4):
                kb.mm(ps[b0], PB[b0][:, c * 128:(c + 1) * 128],
                      aupt.ap()[64:128, d * 512 + c * 128:d * 512 + (c + 1) * 128], alo.ap()[64:128, sl], [aupt, alo])
            kb.dve(TT(v3(aT), P3(b0), bc(pvc("a0_%d" % d).unsqueeze(2), f3()), ALU.add), [ps[b0], pv], [aT])
            kb.act(aT.ap(), aT.ap(), AF.Sigmoid, [aT], [aT])
            bz = nbank()
            kb.mm(ps[bz], PB[bz], tw.ap()[0:65, sl], wupa.ap()[0:65, d * 512:(d + 1) * 512], [tw, wupa])
            kb.act(sig.ap(), PB[bz], AF.Sigmoid, [ps[bz]], [sig])
            yield
            kb.gp(TT(v3(keff), v3(aT), ka_b, ALU.mult), [aT, pv], [keff])
            kb.gp(TT(v3(keff), v3(keff), omka_b, ALU.add), [keff, pv], [keff])
            kb.gp(TT(v3(keff), v3(keff), krT.ap()[:, :, sl], ALU.mult), [keff, krT], [keff])
            kb.gp(TT(v3(bq), kkT.ap()[:, :, sl], v3(aT), ALU.mult), [kkT, aT], [bq])
            kb.gp(TT(v3(rkp), rT.ap()[:, :, sl], v3(keff), ALU.mult), [rT, keff], [rkp])
            bB = nbank()
            for c in range(4):
                kb.mm(ps[bB], PB[bB][:, 2 * c:2 * c + 2], v3(rkp)[:, c, :], rkblk.ap()[:, 2 * c:2 * c + 2], [rkp, rkblk])
            if tt not in bdone:
                bdone.add(tt)
                kb.act(bon.ap()[:, tt, :], PB[bB][:, 0:8], AF.Copy, [ps[bB]], [(bon, tt)])
            else:
                kb.dve(TT(bon.ap()[:, tt, :], PB[bB][:, 0:8], bon.ap()[:, tt, :], ALU.add), [ps[bB], (bon, tt)], [(bon, tt)])
            kb.dve(CP(shi.ap(), sig.ap()), [sig], [shi])
            kb.dve(TT(slo.ap(), sig.ap(), shi.ap(), ALU.subtract), [sig, shi], [slo])
            bci = nbank()
            bce = nbank()
            for c in range(4):
                cs = slice(c * 128, (c + 1) * 128)
                kb.mm(ps[bci], PB[bci][:, cs], shi.ap()[:, cs], TIB[d], [shi, cstb])
                kb.mm(ps[bci], PB[bci][:, cs], slo.ap()[:, cs], TIB[d], [slo, cstb])
            for c in range(4):
                cs = slice(c * 128, (c + 1) * 128)
                kb.mm(ps[bce], PB[bce][:, cs], shi.ap()[:, cs], TEB[d], [shi, cstb])
                kb.mm(ps[bce], PB[bce][:, cs], slo.ap()[:, cs], TEB[d], [slo, cstb])
            kb.act(Wex.ap(), PB[bce], AF.Exp, [ps[bce]], [Wex], scale=CL)
            kb.act(Gt.ap(), P3(bce)[:, :, tf], AF.Exp, [ps[bce]], [Gt], scale=-CL)
            kb.act(Win.ap(), PB[bci], AF.Exp, [ps[bci]], [Win], scale=-CL)
            kb.act(Wic.ap(), PB[bci], AF.Exp, [ps[bci]], [Wic], scale=CL)
            yield
            kb.gp(TT(v3(Wlm), bc(C[:, 256:384].unsqueeze(1), f3()), bc(v3(Wic)[:, :, tl:tl + 1], f3()), ALU.mult),
                  [cst, Wic], [Wlm])
            kb.dve(STT(v3(Ah), kkT.ap()[:, :, sl], -1.0, v3(Wex), ALU.mult, ALU.mult), [kkT, Wex], [Ah])
            kb.dve(TT(Bt.ap(), bq.ap(), Win.ap(), ALU.mult), [bq, Win], [Bt])
            kb.dve(TT(Kt.ap(), keff.ap(), Win.ap(), ALU.mult), [keff, Win], [Kt])
            kb.dve(TT(v3(Rh), rT.ap()[:, :, sl], v3(Wic), ALU.mult), [rT, Wic], [Rh])
            bt_ = nbank()
            for c in range(4):
                kb.transpose(ps[bt_], PB[bt_][:, c * 128:(c + 1) * 128], v3(Bt)[:, c, :], identF, [Bt, cst])
            kb.act(Btok.ap(), PB[bt_], AF.Copy, [ps[bt_]], [Btok])
            bt_ = nbank()
            for c in range(4):
                kb.transpose(ps[bt_], PB[bt_][:, c * 128:(c + 1) * 128], v3(Kt)[:, c, :], identF, [Kt, cst])
            kb.act(Ktok.ap(), PB[bt_], AF.Copy, [ps[bt_]], [Ktok])
            yield

            def Lmat(dstt, lt, rt, mname, eng="dve"):
                for h2 in range(2):
                    b = nbank()
                    r0 = 64 * h2
                    for c in range(4):
                        kb.mm(ps[b], PB[b][:, c * 128:(c + 1) * 128], v3(lt)[r0:r0 + 64, c, :], v3(rt)[r0:r0 + 64, c, :], [lt, rt])
                    kb.dve(TT(dstt.ap()[:, h2, :, :], P3(b), bc(MASK[mname].unsqueeze(1), f3()), ALU.mult), [ps[b], cst], [dstt])
                    if FINE_YIELD:
                        yield

            yield from Lmat(LabTf, Bt, Ah, mk["labT"])
            kb.act(Ptm[0].ap(), LabTf.ap(), AF.Copy, [LabTf], [Ptm[0]])
            kb.dve(TT(LabTf.ap().rearrange("p a c e -> p (a c) e"), LabTf.ap().rearrange("p a c e -> p (a c) e"),
                      bc(identF.unsqueeze(1), [128, 8, 128]), ALU.subtract), [LabTf, cst], [LabTf])
            yield from Lmat(Pm[0], Ah, Bt, mk["lab"])
            yield
            yield from Lmat(LakT, Kt, Ah, mk["labT"])
            yield from Lmat(MrbT, Bt, Rh, mk["mr"])
            yield from Lmat(MrkT, Kt, Rh, mk["mr"])
            kb.dve(TT(Inv[0].ap().rearrange("p a c e -> p (a c) e"), Ptm[0].ap().rearrange("p a c e -> p (a c) e"),
                      bc(identB.unsqueeze(1), [128, 8, 128]), ALU.add), [Ptm[0], cstb], [Inv[0]])
            yield
            cur = 0
            for lvl in range(1, 7):
                nx = 1 - cur
                for hg in range(2):
                    b = nbank()
                    for hh in range(4):
                        kb.mm(ps[b], PB[b][:, hh * 128:(hh + 1) * 128], Ptm[cur].ap()[:, hg, hh, :], Pm[cur].ap()[:, hg, hh, :],
                              [Ptm[cur], Pm[cur]])
                    kb.act(Pm[nx].ap()[:, hg, :, :], P3(b), AF.Copy, [ps[b]], [Pm[nx]])
                    if FINE_YIELD:
                        yield
                if lvl < 6:
                    for hg in range(2):
                        b = nbank()
                        for hh in range(4):
                            kb.mm(ps[b], PB[b][:, hh * 128:(hh + 1) * 128], Pm[cur].ap()[:, hg, hh, :], Ptm[cur].ap()[:, hg, hh, :],
                                  [Ptm[cur], Pm[cur]])
                        if hg == 0:
                            kb.act(Ptm[nx].ap()[:, hg, :, :], P3(b), AF.Copy, [ps[b]], [Ptm[nx]])
                        else:
                            kb.dve(CP(Ptm[nx].ap()[:, hg, :, :], P3(b)), [ps[b]], [Ptm[nx]])
                        if FINE_YIELD:
                            yield
                if not COARSE:
                    yield
                for hg in range(2):
                    b = nbank()
                    for hh in range(4):
                        kb.mm(ps[b], PB[b][:, hh * 128:(hh + 1) * 128], Pm[nx].ap()[:, hg, hh, :], Inv[cur].ap()[:, hg, hh, :],
                              [Pm[nx], Inv[cur]])
                    kb.dve(TT(Inv[nx].ap()[:, hg, :, :], P3(b), Inv[cur].ap()[:, hg, :, :], ALU.add), [ps[b], Inv[cur]], [Inv[nx]])
                    if FINE_YIELD:
                        yield
                cur = nx
                yield
            InvT = Inv[cur]
            kb.dve(TT(v3(Z0f), S.ap(), bc(Gt.ap().unsqueeze(2), f3()), ALU.mult), [S, Gt], [Z0f])
            Z0b = Z0f
            bx = nbank()
            for h in range(8):
                c, h2 = h // 2, h % 2
                hs = slice(h * 64, (h + 1) * 64)
                kb.mm(ps[bx], PB[bx][:, hs], LakT.ap()[:, h2, c, :], Vt.ap()[:, tt, hs], [LakT, (Vt, tt)])
            for c in range(4):
                cs = slice(c * 128, (c + 1) * 128)
                kb.mm(ps[bx], PB[bx][:, cs], v3(Ah)[:, c, :], v3(Z0b)[:, c, :], [Ah, Z0b])
            kb.act(Xf.ap(), PB[bx], AF.Copy, [ps[bx]], [Xf])
            kb.dve(CP(Xh.ap(), PB[bx]), [ps[bx]], [Xh])
            yield
            bu = nbank()
            for h in range(8):
                hs = slice(h * 64, (h + 1) * 64)
                kb.mm(ps[bu], PB[bu][:, hs], InvT.ap()[:, h % 2, h // 2, :], Xh.ap()[:, hs], [InvT, Xh])
            kb.act(U0f.ap(), PB[bu], AF.Copy, [ps[bu]], [U0f])
            yield
            bl = nbank()
            for h in range(8):
                hs = slice(h * 64, (h + 1) * 64)
                kb.mm(ps[bl], PB[bl][:, hs], LabTf.ap()[:, h % 2, h // 2, :], U0f.ap()[:, hs], [LabTf, U0f])
            kb.dve(TT(rb.ap(), PB[bl], Xf.ap(), ALU.add), [ps[bl], Xf], [rb])
            yield
            bd_ = nbank()
            for h in range(8):
                hs = slice(h * 64, (h + 1) * 64)
                kb.mm(ps[bd_], PB[bd_][:, hs], InvT.ap()[:, h % 2, h // 2, :], rb.ap()[:, hs], [InvT, rb])
            kb.dve(TT(Ub.ap(), PB[bd_], U0f.ap(), ALU.add), [ps[bd_], U0f], [Ub])
            yield
            by = nbank()
            for h in range(8):
                c, h2 = h // 2, h % 2
                hs = slice(h * 64, (h + 1) * 64)
                kb.mm(ps[by], PB[by][:, hs], MrkT.ap()[:, h2, c, :], Vt.ap()[:, tt, hs], [MrkT, (Vt, tt)])
            for c in range(4):
                cs = slice(c * 128, (c + 1) * 128)
                kb.mm(ps[by], PB[by][:, cs], v3(Rh)[:, c, :], v3(Z0b)[:, c, :], [Rh, Z0b])
            for h in range(8):
                c, h2 = h // 2, h % 2
                hs = slice(h * 64, (h + 1) * 64)
                kb.mm(ps[by], PB[by][:, hs], MrbT.ap()[:, h2, c, :], Ub.ap()[:, hs], [MrbT, Ub])
            if tt not in ydone:
                ydone.add(tt)
                kb.act(yac.ap()[:, tt, :], PB[by], AF.Copy, [ps[by]], [(yac, tt)])
            else:
                kb.dve(TT(yac.ap()[:, tt, :], PB[by], yac.ap()[:, tt, :], ALU.add), [ps[by], (yac, tt)], [(yac, tt)])
            bs = nbank()
            for c in range(4):
                cs = slice(c * 128, (c + 1) * 128)
                kb.mm(ps[bs], PB[bs][:, cs], Ktok.ap()[:, cs], Vt.ap()[:, tt, cs], [Ktok, (Vt, tt)])
            for c in range(4):
                cs = slice(c * 128, (c + 1) * 128)
                kb.mm(ps[bs], PB[bs][:, cs], Btok.ap()[:, cs], Ub.ap()[:, cs], [Btok, Ub])
            kb.dve(TT(tmpS.ap(), PB[bs], Z0f.ap(), ALU.add), [ps[bs], Z0f], [tmpS])
            kb.dve(TT(S.ap(), v3(tmpS), v3(Wlm), ALU.mult), [tmpS, Wlm], [S])
            yield

        def chain(T_, si, s0, L, d):
            S = kb.tile("S_%d_%d" % (si, d), f3())
            if sample:
                stn = kb.tile("stn%d" % d, [128, 4, 64])
                bd = kb.tile("bd%d" % d, f3())
                kb.dma("sp", stn.ap(), st_d[d].rearrange("(c p) k -> p c k", p=128), writes=[stn])
                kb.dve(MSET(bd.ap(), 0.0), [], [bd])
                kb.dve(CP(bd.ap()[0:64, :, 0:64], stn.ap()[0:64, :, :]), [stn], [bd])
                kb.dve(CP(bd.ap()[64:128, :, 64:128], stn.ap()[64:128, :, :]), [stn], [bd])
                b = nbank()
                for c in range(4):
                    kb.transpose(ps[b], PB[b][:, c * 128:(c + 1) * 128], bd.ap()[:, c, :], identF, [bd, cst])
                kb.dve(CP(S.ap(), P3(b)), [ps[b]], [S])
                kb.free(stn, bd)
            else:
                kb.dve(MSET(S.ap(), 0.0), [], [S])
            tts = list(range(s0 // 128, (s0 + L) // 128))
            if d == 1:
                tts = tts[::-1]
            for tt in tts:
                yield from unit(T_, S, d, tt)
            if not sample:
                bo = nbank()
                for c in range(4):
                    kb.transpose(ps[bo], PB[bo][:, c * 128:(c + 1) * 128], S.ap()[:, c, :], identF, [S, cst])
                so = kb.tile("so%d" % d, [128, 4, 64])
                kb.dve(CP(so.ap()[0:64], P3(bo)[0:64, :, 0:64]), [ps[bo]], [so])
                kb.dve(CP(so.ap()[64:128], P3(bo)[64:128, :, 64:128]), [ps[bo]], [so])
                kb.dma("sp", ns_d[si, d].rearrange("(c p) k -> p c k", p=128), so.ap(), reads=[so], is_output=True)
                kb.free(so)
            kb.free(S)

        for si, (s0, L) in enumerate(seqs):
            gens = [chain(sets[0], si, s0, L, 0), chain(sets[1], si, s0, L, 1)]
            alive = list(gens)
            for _ in range(SCAN_OFFSET):
                next(gens[0])
            while alive:
                for g_ in list(alive):
                    try:
                        next(g_)
                    except StopIteration:
                        alive.remove(g_)
        for T_ in sets:
            kb.free(*T_.values())
        if stop_after == "scan":
            dump(yac.ap()[:, 0, :], [yac], 512)
            dump(yac.ap()[:, NTT - 1, :], [yac], 512)
            dump(bon.ap()[:, 0, :], [bon], 8)
            break

        yrwT = kb.tile("yrwT", [128, 4, nT], BF16)
        st1 = kb.tile("st1", [128, 8])
        st2 = kb.tile("st2", [128, 8])
        msq = kb.tile("msq", [128, 8])
        ysq = kb.tile("ysq", [128, 512])
        ygb = kb.tile("ygb", [128, 512], BF16)
        xT = kb.tile("xT", [128, 8, nT], F32, nslots=NN)
        hT = kb.tile("hT2", [128, 8, nT], BF16, nslots=NTT)
        ecx = {"i": 0}

        def emit_x(c, tt, pap, pst):
            out = xT.ap()[:, c, tt * 128:(tt + 1) * 128]
            outh = hT.ap()[:, c, tt * 128:(tt + 1) * 128]
            kb.act(out, pap, AF.Copy, [pst], [(xT, tt // 4)], scale=ALPHA_C)
            kb.act(outh, pap, AF.Identity, [pst, modT], [(hT, tt)], bias=mod("sh1", c, g), scale=mod("sc1", c, g))

        woa = kb.tile("woa", [128, 4, D], BF16)
        wor = kb.tile("wor", [128, 4, D], BF16)
        wout = kb.tile("wout", [128, 8, D], BF16)
        kb.dma("pool", woa.ap(), w_oa.rearrange("(c p) n -> p c n", p=128), writes=[woa])
        kb.dma("pool", wor.ap(), w_or.rearrange("(c p) n -> p c n", p=128), writes=[wor])
        kb.dma("pool", wout.ap(), w_out.rearrange("(c p) n -> p c n", p=128), writes=[wout])
        def phd_gen():
            for tt in range(NTT):
                o = tt * 128
                Y = yac.ap()[:, tt, :]
                Y3 = Y.rearrange("p (h e) -> p h e", h=8)
                yt = [(yac, tt)]
                kb.dve(RSUM(st1.ap(), Y3), yt, [st1])
                kb.pool(TT(ysq.ap(), Y, Y, ALU.mult), yt, [ysq])
                kb.dve(RSUM(st2.ap(), ysq.ap().rearrange("p (h e) -> p h e", h=8)), [ysq], [st2])
                kb.dve(TSM(st1.ap(), st1.ap(), 1.0 / 64.0), [st1], [st1])
                kb.dve(TT(msq.ap(), st1.ap(), st1.ap(), ALU.mult), [st1], [msq])
                kb.dve(STT(st2.ap(), st2.ap(), 1.0 / 64.0, msq.ap(), ALU.mult, ALU.subtract), [st2, msq], [st2])
                kb.dve(TSA(st2.ap(), st2.ap(), 64e-5), [st2], [st2])
                kb.act(st2.ap(), st2.ap(), AF.Sqrt, [st2], [st2])
                kb.dve(RCP(st2.ap(), st2.ap()), [st2], [st2])
                kb.dve(TT(Y3, Y3, bc(st1.ap().unsqueeze(2), [128, 8, 64]), ALU.subtract), yt + [st1], yt)
                kb.dve(TT(Y3, Y3, bc(st2.ap().unsqueeze(2), [128, 8, 64]), ALU.mult), yt + [st2], yt)
                kb.pool(TT(Y, Y, R[:, 0:512], ALU.mult), yt + [rowp], yt)
                kb.pool(TT(Y, Y, R[:, 512:1024], ALU.add), yt + [rowp], yt)
                kb.dve(TT(ysq.ap().rearrange("p (h e) -> p h e", h=8), Vt.ap()[:, tt, :].rearrange("p (h e) -> p h e", h=8),
                          bc(bon.ap()[:, tt, :].unsqueeze(2), [128, 8, 64]), ALU.mult), [(Vt, tt), (bon, tt)], [ysq])
                kb.pool(TT(Y, Y, ysq.ap(), ALU.add), yt + [ysq], yt)
                bg = nbank()
                kb.mm(ps[bg], PB[bg], sg.ap()[:, o:o + 128], gup.ap(), [sg, gup])
                kb.dve(TT(ygb.ap(), Y, PB[bg], ALU.mult), yt + [ps[bg]], [ygb])
                bt_ = nbank()
                for c in range(4):
                    kb.transpose(ps[bt_], PBb[bt_][:, c * 128:(c + 1) * 128], ygb.ap()[:, c * 128:(c + 1) * 128], identB, [ygb, cstb])
                kb.act(yrwT.ap()[:, :, o:o + 128], PBb[bt_][:, 0:512].rearrange("p (c e) -> p c e", c=4), AF.Copy, [ps[bt_]], [yrwT])
                yield

        XLD = load_x_T_gen(ps, PB, t0, nT, emit_x)
        alive = [phd_gen(), XLD]
        while alive:
            for g_ in list(alive):
                try:
                    next(g_)
                except StopIteration:
                    alive.remove(g_)
        kb.free(st1, st2, msq, ysq, ygb, rT, krT, kkT, Vt, tw, alo, sg, yac, bon)
        if stop_after == "D":
            for c_ in range(4):
                dump(yrwT.ap()[:, c_, 0:512], [yrwT], 512)
            break

        mixpre = kb.tile("mixpre", [128, 8, nT], BF16, nslots=NN)
        wg = [kb.tile("wg%d" % i, [128, 8, 1024], BF16) for i in range(2)]
        gat = [kb.tile("gat%d" % i, [128, 512]) for i in range(4)]
        G0 = 3328
        for j in range(8):
            w_ = wg[(j // 4) % 2]
            jj = j % 4
            if jj == 0:
                kb.dma("pool", w_.ap()[:, :, 0:512], win_v[:, :, G0 + j * 128:G0 + j * 128 + 512], writes=[w_])
                kb.dma("pool", w_.ap()[:, :, 512:1024], win_v[:, :, G0 + 1024 + j * 128:G0 + 1024 + j * 128 + 512], writes=[w_])
            for n in range(NN):
                ns_ = slice(n * 512, (n + 1) * 512)
                hs_ = (hT, range(4 * n, 4 * n + 4))
                b1 = nbank()
                for c in range(4):
                    kb.mm(ps[b1], PB[b1], woa.ap()[:, c, j * 128:(j + 1) * 128], oattT.ap()[:, c, ns_], [woa, oattT])
                b2 = nbank()
                for c in range(4):
                    kb.mm(ps[b2], PB[b2], wor.ap()[:, c, j * 128:(j + 1) * 128], yrwT.ap()[:, c, ns_], [wor, yrwT])
                b3 = nbank()
                for k in range(8):
                    kb.mm(ps[b3], PB[b3], w_.ap()[:, k, jj * 128:(jj + 1) * 128], hT.ap()[:, k, ns_], [w_, hs_])
                b4 = nbank()
                for k in range(8):
                    kb.mm(ps[b4], PB[b4], w_.ap()[:, k, 512 + jj * 128:512 + (jj + 1) * 128], hT.ap()[:, k, ns_], [w_, hs_])
                ga, gb_, m1, m2 = gat
                kb.act(ga.ap(), PB[b3], AF.Sigmoid, [ps[b3]], [ga])
                kb.act(gb_.ap(), PB[b4], AF.Sigmoid, [ps[b4]], [gb_])
                kb.dve(TT(m1.ap(), ga.ap(), PB[b1], ALU.mult), [ga, ps[b1]], [m1])
                kb.dve(TT(m2.ap(), gb_.ap(), PB[b2], ALU.mult), [gb_, ps[b2]], [m2])
                kb.pool(TT(mixpre.ap()[:, j, ns_], m1.ap(), m2.ap(), ALU.add), [m1, m2], [(mixpre, n)])
        kb.free(woa, wor, *wg, *gat, oattT, yrwT)
        for j2 in range(8):
            for n in range(NN):
                ns_ = slice(n * 512, (n + 1) * 512)
                b = nbank()
                for j in range(8):
                    kb.mm(ps[b], PB[b], wout.ap()[:, j, j2 * 128:(j2 + 1) * 128], mixpre.ap()[:, j, ns_], [wout, (mixpre, n)])
                kb.dve(STT(xT.ap()[:, j2, ns_], PB[b], mod("g1", j2, g), xT.ap()[:, j2, ns_], ALU.mult, ALU.add),
                       [ps[b], modT, (xT, n)], [(xT, n)])
        kb.free(wout, mixpre)

        def layer_norm_T(gname, bname):
            means = [kb.tile("lnmean%d" % i, [128, 512]) for i in range(NN)]
            vars_ = [kb.tile("lnvar%d" % i, [128, 512]) for i in range(NN)]
            sqt = [kb.tile("lnsq%d" % i, [128, 512]) for i in range(2)]
            tmp = [kb.tile("lntmp%d" % i, [128, 512]) for i in range(2)]
            for n in range(NN):
                ns_ = slice(n * 512, (n + 1) * 512)
                xs = [(xT, n)]
                mean, var = means[n], vars_[n]
                bm = nbank()
                for j in range(8):
                    kb.mm(ps[bm], PB[bm], onesd, xT.ap()[:, j, ns_], [cst] + xs)
                bq_ = nbank()
                for j in range(8):
                    sq_ = sqt[j % 2]
                    kb.act(sq_.ap(), xT.ap()[:, j, ns_], AF.Square, xs, [sq_])
                    kb.mm(ps[bq_], PB[bq_], onesd, sq_.ap(), [cst, sq_])
                kb.act(mean.ap(), PB[bm], AF.Copy, [ps[bm]], [mean])
                kb.dve(TT(var.ap(), mean.ap(), mean.ap(), ALU.mult), [mean], [var])
                kb.dve(TT(var.ap(), PB[bq_], var.ap(), ALU.subtract), [ps[bq_], var], [var])
                kb.dve(TSA(var.ap(), var.ap(), 1e-5), [var], [var])
                kb.act(var.ap(), var.ap(), AF.Sqrt, [var], [var])
                kb.dve(RCP(var.ap(), var.ap()), [var], [var])
            for n in range(NN):
                ns_ = slice(n * 512, (n + 1) * 512)
                xs = [(xT, n)]
                mean, var = means[n], vars_[n]
                for j in range(8):
                    t_ = tmp[j % 2]
                    kb.dve(TT(t_.ap(), xT.ap()[:, j, ns_], mean.ap(), ALU.subtract), xs + [mean], [t_])
                    kb.dve(TT(t_.ap(), t_.ap(), var.ap(), ALU.mult), [t_, var], [t_])
                    kb.act(xT.ap()[:, j, ns_], t_.ap(), AF.Identity, [t_, pv], xs, bias=pvc(bname, j), scale=pvc(gname, j))
            kb.free(*means, *vars_, *sqt, *tmp)

        wu = [kb.tile("wu%d" % i, [128, 8, 1024], BF16) for i in range(2)]
        wup_v = w_up.rearrange("(k p) n -> p k n", p=128)
        kb.dma("pool", wu[0].ap()[:, :, 0:512], wup_v[:, :, 0:512], writes=[wu[0]])
        kb.dma("pool", wu[0].ap()[:, :, 512:1024], wup_v[:, :, D_FF:D_FF + 512], writes=[wu[0]])
        layer_norm_T("ln1_g", "ln1_b")
        for c in range(8):
            for n in range(NN):
                ns_ = slice(n * 512, (n + 1) * 512)
                kb.act(hT.ap()[:, c, ns_], xT.ap()[:, c, ns_], AF.Identity, [(xT, n), modT], [(hT, range(4 * n, 4 * n + 4))],
                       bias=mod("sh2", c, g), scale=mod("sc2", c, g))
        for c in range(8):
            for n in range(NN):
                ns_ = slice(n * 512, (n + 1) * 512)
                kb.pool(TSM(xT.ap()[:, c, ns_], xT.ap()[:, c, ns_], ALPHA_C), [(xT, n)], [(xT, n)])
        if stop_after == "E":
            dump(xT.ap()[:, 0, 0:512], [xT], 512)
            dump(xT.ap()[:, 7, 0:512], [xT], 512)
            dump(xT.ap()[:, 3, nT - 512:nT], [xT], 512)
            dump(hT.ap()[:, 3, nT - 512:nT], [hT], 512)
            break

        fT = kb.tile("fT", [128, 22, nT], BF16, nslots=NN)
        uraw = [kb.tile("uraw%d" % i, [128, nT]) for i in range(2)]
        uacc = [kb.tile("uacc%d" % i, [128, nT]) for i in range(2)]
        wd0 = kb.tile("wd0", [128, 22, 512], BF16)
        wdn_v = w_down.rearrange("(f p) n -> p f n", p=128)
        kb.dma("pool", wd0.ap()[:, 0:11, :], wdn_v[:, 0:11, 0:512], writes=[wd0])
        kb.dma("pool", wd0.ap()[:, 11:22, :], wdn_v[:, 11:22, 0:512], writes=[wd0])
        for f in range(22):
            w_ = wu[(f // 4) % 2]
            fj = f % 4
            if fj == 0 and f > 0:
                ncol = min(512, D_FF - f * 128)
                kb.dma("pool", w_.ap()[:, :, 0:ncol], wup_v[:, :, f * 128:f * 128 + ncol], writes=[w_])
                kb.dma("pool", w_.ap()[:, :, 512:512 + ncol], wup_v[:, :, D_FF + f * 128:D_FF + f * 128 + ncol], writes=[w_])
            ur, ua = uraw[f % 2], uacc[f % 2]
            bv = []
            for n in range(NN):
                ns_ = slice(n * 512, (n + 1) * 512)
                hs_ = (hT, range(4 * n, 4 * n + 4))
                b = nbank()
                for k in range(8):
                    kb.mm(ps[b], PB[b], w_.ap()[:, k, fj * 128:(fj + 1) * 128], hT.ap()[:, k, ns_], [w_, hs_])
                kb.act(ur.ap()[:, ns_], PB[b], AF.Copy, [ps[b]], [ur])
                b = nbank()
                for k in range(8):
                    kb.mm(ps[b], PB[b], w_.ap()[:, k, 512 + fj * 128:512 + (fj + 1) * 128], hT.ap()[:, k, ns_], [w_, hs_])
                bv.append(b)
            kb.dve(TS(ua.ap(), ur.ap(), pvc("cw1", f), pvc("cb", f), ALU.mult, ALU.add), [ur, pv], [ua])
            for (s0, L) in seqs:
                kb.dve(STT(ua.ap()[:, s0 + 1:s0 + L], ur.ap()[:, s0:s0 + L - 1], pvc("cw0", f), ua.ap()[:, s0 + 1:s0 + L],
                           ALU.mult, ALU.add), [ur, pv, ua], [ua])
                kb.dve(STT(ua.ap()[:, s0:s0 + L - 1], ur.ap()[:, s0 + 1:s0 + L], pvc("cw2", f), ua.ap()[:, s0:s0 + L - 1],
                           ALU.mult, ALU.add), [ur, pv, ua], [ua])
            kb.act(ua.ap(), ua.ap(), AF.Gelu_apprx_tanh, [ua], [ua])
            for n in range(NN):
                ns_ = slice(n * 512, (n + 1) * 512)
                kb.dve(TT(fT.ap()[:, f, ns_], ua.ap()[:, ns_], PB[bv[n]], ALU.mult), [ua, ps[bv[n]]], [(fT, n)])
        if stop_after == "F1":
            dump(fT.ap()[:, 0, 0:512], [fT], 512)
            dump(fT.ap()[:, 21, nT - 512:nT], [fT], 512)
            dump(fT.ap()[:, 10, nT - 512:nT], [fT], 512)
            dump(uacc[1].ap()[:, nT - 512:nT], [uacc[1]], 512)
            break
        kb.free(*wu, *uraw, *uacc, hT)
        wd = [wd0, kb.tile("wd1", [128, 22, 512], BF16)]
        for j in range(8):
            w_ = wd[(j // 4) % 2]
            jj = j % 4
            if jj == 0 and j > 0:
                kb.dma("pool", w_.ap()[:, 0:11, :], wdn_v[:, 0:11, j * 128:j * 128 + 512], writes=[w_])
                kb.dma("pool", w_.ap()[:, 11:22, :], wdn_v[:, 11:22, j * 128:j * 128 + 512], writes=[w_])
            for n in range(NN):
                ns_ = slice(n * 512, (n + 1) * 512)
                b = nbank()
                for f in range(22):
                    kb.mm(ps[b], PB[b], w_.ap()[:, f, jj * 128:(jj + 1) * 128], fT.ap()[:, f, ns_], [w_, (fT, n)])
                kb.dve(STT(xT.ap()[:, j, ns_], PB[b], mod("g2", j, g), xT.ap()[:, j, ns_], ALU.mult, ALU.add),
                       [ps[b], modT, (xT, n)], [(xT, n)])
        kb.free(*wd, fT)
        layer_norm_T("ln2_g", "ln2_b")
        ytok = [kb.tile("ytok%d" % i, [128, D]) for i in range(2)]
        for tt in range(NTT):
            yt_ = ytok[tt % 2]
            for half in range(2):
                b = nbank()
                for cc in range(4):
                    c = half * 4 + cc
                    kb.transpose(ps[b], PB[b][:, cc * 128:(cc + 1) * 128], xT.ap()[:, c, tt * 128:(tt + 1) * 128], identF,
                                 [(xT, tt // 4), cst])
                evac_copy(yt_.ap()[:, half * 512:(half + 1) * 512], PB[b], [ps[b]], [yt_])
            kb.dma("sp", y_d[t0 + tt * 128:t0 + (tt + 1) * 128, :], yt_.ap(), reads=[yt_], is_output=True)
        kb.free(*ytok, xT)
    dbg = {}
    kb.finish()
    return nc, kb


def prep_inputs(inp):
    f = lambda a: np.ascontiguousarray(np.asarray(a, np.float32))
    shared = {}
    for k_, n_ in [("w_ada", "w_ada"), ("w_in", "w_in"), ("w_o_attn", "w_oa"), ("w_o_rwkv", "w_or"), ("w_out", "w_out"),
                   ("w_up", "w_up"), ("w_down", "w_down")]:
        shared[n_] = f(inp[k_][0])
    pvec = np.zeros((128, NPV), np.float32)

    def put(name, v):
        o, n = PV[name]
        pvec[:, o:o + n] = fm(v)

    put("b_ada", inp["b_ada"][0])
    put("mu0", inp["rw_mu"][0, 0])
    put("mu1", inp["rw_mu"][0, 1])
    put("a0_0", inp["rw_a0"][0, 0])
    put("a0_1", inp["rw_a0"][0, 1])
    put("k_k", inp["rw_k_k"][0])
    put("k_a", inp["rw_k_a"][0])
    put("ln1_g", inp["ln1_g"][0])
    put("ln1_b", inp["ln1_b"][0])
    put("ln2_g", inp["ln2_g"][0])
    put("ln2_b", inp["ln2_b"][0])
    put("cw0", inp["conv_w"][0, 0])
    put("cw1", inp["conv_w"][0, 1])
    put("cw2", inp["conv_w"][0, 2])
    put("cb", inp["conv_b"][0])
    shared["pvec"] = pvec
    rowv = np.concatenate([f(inp["rw_lnx_g"][0]), f(inp["rw_lnx_b"][0]), f(inp["da_subln_g"][0]),
                           f(inp["da_lambda"][0]).reshape(-1)])
    shared["rowp"] = np.ascontiguousarray(np.broadcast_to(rowv[None, :], (128, 1408)))
    rk = f(inp["rw_r_k"][0])
    rkblk = np.zeros((128, 4, 2), np.float32)
    for c in range(4):
        for j in range(2):
            rkblk[64 * j:64 * j + 64, c, j] = rk[2 * c + j]
    shared["rkblk"] = rkblk.reshape(128, 8)
    wupa = np.zeros((65, 2, 512), np.float32)
    wupa[0:64] = np.transpose(f(inp["rw_w_up"][0]), (1, 0, 2))
    wupa[64] = f(inp["rw_w0"][0])
    shared["wupa"] = wupa.reshape(65, 1024)
    aupt = np.zeros((128, 2, 512), np.float32)
    aupt[64:128] = np.transpose(f(inp["rw_a_up"][0]), (1, 0, 2))
    shared["aupt"] = aupt.reshape(128, 1024)
    shared["gup"] = f(inp["rw_g_up"][0])
    shared["cst"] = make_consts()
    xs = f(inp["x_sample"])
    xp = f(inp["x_prompt"])
    cctx = f(inp["c_ctx"])
    cc = f(inp["c"])
    maps = []
    for i in range(8):
        m = dict(shared)
        m["xin"] = np.concatenate([xs[i], xp[2 * i], xp[2 * i + 1]], axis=0)
        cvec = np.zeros((128, 8, 2), np.float32)
        cvec[:, :, 0] = fm(cc[i])
        cvec[:, :, 1] = fm(cctx)
        m["cvec"] = cvec.reshape(128, 16)
        m["ck"] = f(inp["cache_k"][i, 0]).reshape(256, 512)
        m["cv"] = f(inp["cache_v"][i, 0]).reshape(256, 512)
        m["st"] = f(inp["state_rwkv"][i, 0]).reshape(2, 512, 64)
        maps.append(m)
    return maps


def assemble(results):
    y_p = np.zeros((16, 256, D), np.float32)
    y_s = np.zeros((8, 1024, D), np.float32)
    nk = np.zeros((16, 1, 256, 4, 2, 64), np.float32)
    nv = np.zeros((16, 1, 256, 4, 128), np.float32)
    ns = np.zeros((16, 1, 2, 8, 64, 64), np.float32)
    for i, r in enumerate(results):
        y = r["y"]
        y_s[i] = y[0:1024]
        y_p[2 * i] = y[1024:1280]
        y_p[2 * i + 1] = y[1280:1536]
        nk[2 * i, 0] = r["nk"][0:256].reshape(256, 4, 2, 64)
        nk[2 * i + 1, 0] = r["nk"][256:512].reshape(256, 4, 2, 64)
        nv[2 * i, 0] = r["nv"][0:256].reshape(256, 4, 128)
        nv[2 * i + 1, 0] = r["nv"][256:512].reshape(256, 4, 128)
        ns[2 * i, 0] = r["ns"][0].reshape(2, 8, 64, 64)
        ns[2 * i + 1, 0] = r["ns"][1].reshape(2, 8, 64, 64)
    return y_p, y_s, nk, nv, ns


_CACHE = {}


def kernel(**inputs):
    if "nc" not in _CACHE:
        _CACHE["nc"] = build_program()[0]
    maps = prep_inputs(inputs)
    res = run_bass_kernel_spmd(_CACHE["nc"], maps, core_ids=list(range(8)))
    return assemble(res.results)
```

```python
from contextlib import ExitStack
import numpy as np
import concourse.bass as bass
import concourse.mybir as mybir
from concourse.bass_utils import run_bass_kernel_spmd

F32 = mybir.dt.float32
F32R = mybir.dt.float32r
BF16 = mybir.dt.bfloat16
AF = mybir.ActivationFunctionType
ALU = mybir.AluOpType
AX = mybir.AxisListType


ATTACH_WAIT = True
ACT_FAST_COPY = True
STRICT_SYNC = False


class Tile:
    def __init__(self, name, handle, nslots=1):
        self.name = name
        self.h = handle
        self.nslots = nslots
        self.w = [None] * nslots
        self.r = [dict() for _ in range(nslots)]

    def ap(self):
        return self.h


def _norm(items):
    out = []
    for it in items:
        if isinstance(it, Tile):
            for s in range(it.nslots):
                out.append((it, s))
        else:
            t, sl = it
            if isinstance(sl, int):
                out.append((t, sl))
            else:
                for s in sl:
                    out.append((t, s))
    return out


class KB:
    ENG = ("pe", "act", "dve", "pool", "sp")
    NDS = 8

    def __init__(self, nc):
        self.nc = nc
        self.stack = ExitStack()
        self.ops = {e: [] for e in self.ENG}
        self.waited = {e: {} for e in self.ENG}
        self.sems = {e: self.stack.enter_context(nc.semaphore("s_" + e)) for e in self.ENG}
        self.dsem = {}
        self.dval = {}
        self.dnext = {}
        for q in ("sp", "act", "pool"):
            self.dsem[q] = [self.stack.enter_context(nc.semaphore("d_%s%d" % (q, i))) for i in range(self.NDS)]
            self.dval[q] = [0] * self.NDS
            self.dnext[q] = 0
        self.out_events = []
        self.ntiles = 0
        self.vcs = {}

    ARENA_COLS = 53000

    def _init_arena(self):
        self.arena = self.stack.enter_context(self.nc.sbuf_tensor("arena", [128, self.ARENA_COLS], F32))
        self.free_list = [(0, self.ARENA_COLS)]
        self.grave = []
        self.peak = 0

    def tile(self, name, shape, dtype=F32, nslots=1):
        if not hasattr(self, "arena"):
            self._init_arena()
        shape = list(shape)
        n = 1
        for d in shape[1:]:
            n *= d
        esz = 2 if dtype == BF16 else 4
        cols = (n * esz + 3) // 4
        for i, (lo, hi) in enumerate(self.free_list):
            if hi - lo >= cols:
                break
        else:
            raise RuntimeError("arena full allocating %s (%d cols); free=%s" % (name, cols, self.free_list))
        self.free_list[i:i + 1] = [(lo + cols, hi)] if hi - lo > cols else []
        self.peak = max(self.peak, lo + cols)
        v = self.arena[0:shape[0], lo:lo + cols]
        if dtype != F32:
            v = v.bitcast(dtype)
        if esz == 2 and (n % 2):
            v = v[:, 0:n]
        if len(shape) == 3:
            v = v.rearrange("p (a b) -> p a b", a=shape[1])
        elif len(shape) == 4:
            v = v.rearrange("p (a b c) -> p a b c", a=shape[1], b=shape[2])
        t = Tile(name, v, nslots)
        t.rng = (lo, lo + cols)
        keep = []
        for (glo, ghi, evs) in self.grave:
            if glo < lo + cols and lo < ghi:
                for s in range(nslots):
                    for k, ev in evs.items():
                        old = t.r[s].get(k)
                        if old is None or old[-1] < ev[-1]:
                            t.r[s][k] = ev
            keep.append((glo, ghi, evs))
        self.grave = keep
        return t

    def free(self, *tiles):
        for t in tiles:
            evs = {}
            for s in range(t.nslots):
                cand = list(t.r[s].items())
                if t.w[s] is not None:
                    ev = t.w[s]
                    k = (ev[0], ev[1]) if ev[0] == "e" else (ev[0], ev[1], ev[2])
                    cand.append((k, ev))
                for k, ev in cand:
                    old = evs.get(k)
                    if old is None or old[-1] < ev[-1]:
                        evs[k] = ev
            lo, hi = t.rng
            self.grave.append((lo, hi, evs))
            self.free_list.append((lo, hi))
            self.free_list.sort()
            merged = []
            for (a, b) in self.free_list:
                if merged and merged[-1][1] == a:
                    merged[-1] = (merged[-1][0], b)
                else:
                    merged.append((a, b))
            self.free_list = merged

    def psum_banks(self, n=8):
        out = [Tile("ps%d" % i, self.stack.enter_context(self.nc.psum_tensor("psb%d" % i, [128, 512], F32))[:])
               for i in range(n)]
        for t in out:
            t.excl = True
        return out

    def _need(self, eng, ev, waits):
        if ev is None:
            return
        if ev[0] == "e":
            key = ("e", ev[1])
            val = ev[2]
        else:
            key = ("d", ev[1], ev[2])
            val = ev[3]
        if self.waited[eng].get(key, -1) >= val:
            return
        self.waited[eng][key] = val
        waits[key] = ev
        vc = self.vcs.get(ev)
        if vc:
            w = self.waited[eng]
            for k2, v2 in vc.items():
                if w.get(k2, -1) < v2:
                    w[k2] = v2

    def _record(self, eng, fn, reads, writes, dma_q=None, is_output=False):
        reads = _norm(reads)
        writes = _norm(writes)
        xr = [(t, s_) for (t, s_) in reads if getattr(t, "excl", False)]
        if xr:
            reads = [(t, s_) for (t, s_) in reads if not getattr(t, "excl", False)]
            writes = writes + [x for x in xr if x not in writes]
        waits = {}
        for (t, s) in reads:
            ev = t.w[s]
            if ev is not None:
                if ev[0] == "e" and ev[1] == eng and dma_q is None and eng == "pe":
                    pass
                else:
                    self._need(eng, ev, waits)
        strict = STRICT_SYNC and eng != "pe"
        for (t, s) in writes:
            ev = t.w[s]
            if ev is not None and (strict or not (ev[0] == "e" and ev[1] == eng and dma_q is None)):
                self._need(eng, ev, waits)
            for ev in t.r[s].values():
                if strict or not (ev[0] == "e" and ev[1] == eng and dma_q is None):
                    self._need(eng, ev, waits)
        if dma_q is not None:
            q = dma_q
            k = self.dnext[q]
            self.dnext[q] = (k + 1) % self.NDS
            if self.dval[q][k] > 0:
                self._need(eng, ("d", q, k, self.dval[q][k]), waits)
            self.dval[q][k] += 16
            ev_out = ("d", q, k, self.dval[q][k])
            rec = dict(waits=list(waits.values()), fn=fn, marked=False, dma=(q, k))
            if is_output:
                self.out_events.append(ev_out)
        else:
            ev_out = ("e", eng, len(self.ops[eng]))
            rec = dict(waits=list(waits.values()), fn=fn, marked=False, dma=None)
        self.ops[eng].append(rec)
        snap = dict(self.waited[eng])
        if ev_out[0] == "e":
            if snap.get(("e", eng), -1) < ev_out[2] - 1:
                pass
        self.vcs[ev_out] = snap
        for (t, s) in writes:
            t.w[s] = ev_out
            t.r[s] = {}
        for (t, s) in reads:
            key = (ev_out[0], ev_out[1]) if ev_out[0] == "e" else (ev_out[0], ev_out[1], ev_out[2])
            t.r[s][key] = ev_out
        return ev_out

    def fresh(self, *pss):
        for p in pss:
            p.fresh = True

    def mm(self, ps, out, lhsT, rhs, reads, start=None, stop=None):
        st = bool(getattr(ps, "fresh", True))
        ps.fresh = False
        return self._record("pe", lambda e: e.matmul(out, lhsT, rhs, start=st, stop=True, skip_group_check=True),
                            reads, [ps])

    def transpose(self, ps, out, in_, ident, reads):
        return self._record("pe", lambda e: e.transpose(out, in_, ident), reads, [ps])

    def act(self, out, in_, func, reads, writes, bias=None, scale=None, accum_out=None):
        kw = {}
        if bias is not None:
            kw["bias"] = bias
        if scale is not None:
            kw["scale"] = scale
        if accum_out is not None:
            kw["accum_out"] = accum_out
        if ACT_FAST_COPY and func == AF.Copy and bias is None and accum_out is None:
            if scale is None:
                return self._record("act", lambda e: e.copy(out, in_), reads, writes)
            if isinstance(scale, float):
                return self._record("act", lambda e: e.mul(out, in_, scale), reads, writes)
        return self._record("act", lambda e: e.activation(out, in_, func, **kw), reads, writes)

    def dve(self, fn, reads, writes):
        return self._record("dve", fn, reads, writes)

    def pool(self, fn, reads, writes):
        return self._record("dve", fn, reads, writes)

    def gp(self, fn, reads, writes):
        return self._record("pool", fn, reads, writes)

    def any(self, eng, fn, reads, writes):
        return self._record(eng, fn, reads, writes)

    def dve_copy(self, out, in_, reads, writes):
        return self._record("dve", lambda e: e.tensor_copy(out, in_), reads, writes)

    def dma(self, q, out, in_, reads=(), writes=(), is_output=False, **kw):
        return self._record(q, lambda e: e.dma_start(out=out, in_=in_, **kw), reads, writes, dma_q=q,
                            is_output=is_output)

    def barrier(self):
        for eng in self.ENG:
            waits = {}
            for other in self.ENG:
                if other != eng and self.ops[other]:
                    idx = None
                    for i in range(len(self.ops[other]) - 1, -1, -1):
                        if self.ops[other][i]["dma"] is None and self.ops[other][i]["fn"] is not None:
                            idx = i
                            break
                    if idx is not None:
                        self._need(eng, ("e", other, idx), waits)
            for q in self.dsem:
                for k in range(self.NDS):
                    if self.dval[q][k] > 0:
                        self._need(eng, ("d", q, k, self.dval[q][k]), waits)
            if waits:
                self.ops[eng].append(dict(waits=list(waits.values()), fn=None, marked=False, dma=None))

    def finish(self):
        waits = {}
        for ev in self.out_events:
            self._need("sp", ev, waits)
        self.ops["sp"].append(dict(waits=list(waits.values()), fn=None, marked=False, dma=None))
        for eng in self.ENG:
            for rec in self.ops[eng]:
                for ev in rec["waits"]:
                    if ev[0] == "e":
                        self.ops[ev[1]][ev[2]]["marked"] = True
        val = {}
        for eng in self.ENG:
            c = 0
            for i, rec in enumerate(self.ops[eng]):
                if rec["marked"]:
                    c += 1
                    val[(eng, i)] = c
        self.nmarked = {e: sum(1 for r in self.ops[e] if r["marked"]) for e in self.ENG}
        kb = self

        def replay(engname):
            def run(e):
                for i, rec in enumerate(kb.ops[engname]):
                    ws = [(kb.sems[ev[1]], val[(ev[1], ev[2])]) if ev[0] == "e" else (kb.dsem[ev[1]][ev[2]], ev[3])
                          for ev in rec["waits"]]
                    attach = ws.pop() if (ATTACH_WAIT and ws and rec["fn"] is not None) else None
                    for (sm, vl) in ws:
                        e.wait_ge(sm, vl)
                    if rec["fn"] is None:
                        continue
                    ins = rec["fn"](e)
                    if attach is not None:
                        ins._wait_ge(attach[0], attach[1])
                    if rec["dma"] is not None:
                        q, k = rec["dma"]
                        ins.then_inc(kb.dsem[q][k], 16)
                    elif rec["marked"]:
                        ins.then_inc(kb.sems[engname], 1)
            return run

        with self.nc.Block() as block:
            block.tensor(replay("pe"))
            block.scalar(replay("act"))
            block.vector(replay("dve"))
            block.gpsimd(replay("pool"))
            block.sync(replay("sp"))
        self.stack.close()


D = 1024
NTOK = 1536
N_IN = 5376
D_FF = 2816
ALPHA_C = 2.0 ** 0.25
LAM_INIT = 0.8 - 0.6 * 1.0
CL = -float(np.exp(-0.5))
SCAN_F32 = True
FINE_YIELD = False
COARSE = False
SCAN_OFFSET = 0
NEU_BF16 = True
PASSES = [
    dict(t0=0, nT=1024, seqs=[(0, 1024)], g=0, sample=True),
    dict(t0=1024, nT=512, seqs=[(0, 256), (256, 256)], g=1, sample=False),
]
PV = {}
_o = 0
for _n, _c in [("b_ada", 48), ("mu0", 14), ("mu1", 14), ("a0_0", 4), ("a0_1", 4), ("k_k", 4), ("k_a", 4),
               ("ln1_g", 8), ("ln1_b", 8), ("ln2_g", 8), ("ln2_b", 8), ("cw0", 22), ("cw1", 22), ("cw2", 22),
               ("cb", 22)]:
    PV[_n] = (_o, _c)
    _o += _c
NPV = _o
CST = dict(ident=0, sw=128, blk=256, onesd=384, ti_f=512, te_f=640, ti_b=768, te_b=896,
           lt=1024, le=1152, gt=1280, ge=1408, cos=1536, sin=2560)
NCST = 3584


def make_consts():
    c = np.zeros((128, NCST), np.float32)
    idx = np.arange(128)
    c[:, 0:128] = np.eye(128, dtype=np.float32)
    sw = np.where((idx % 64) < 32, idx + 32, idx - 32)
    c[sw, 128 + idx] = 1.0
    c[:, 256:384] = (idx[:, None] // 64 == idx[None, :] // 64)
    c[:, 384:512] = 1.0 / 1024.0
    i = idx[:, None]
    t = idx[None, :]
    m = 63
    ti_f = ((i > m) & (i <= t)).astype(np.float32) - ((i > t) & (i <= m)).astype(np.float32)
    mb = 64
    ti_b = ((i >= t) & (i < mb)).astype(np.float32) - ((i >= mb) & (i < t)).astype(np.float32)
    eye = np.eye(128, dtype=np.float32)
    c[:, 512:640] = ti_f
    c[:, 640:768] = ti_f - eye
    c[:, 768:896] = ti_b
    c[:, 896:1024] = ti_b - eye
    c[:, 1024:1152] = (i < t)
    c[:, 1152:1280] = (i <= t)
    c[:, 1280:1408] = (i > t)
    c[:, 1408:1536] = (i >= t)
    pos = np.arange(1024)
    row = (pos // 64).astype(np.float32)
    col = (pos % 64).astype(np.float32)
    inv = (10000.0 ** (-np.arange(16, dtype=np.float32) / 16)).astype(np.float32)
    ang = np.concatenate([row[:, None] * inv, col[:, None] * inv], -1).astype(np.float32)
    cos = np.cos(ang).astype(np.float32).T
    sin = np.sin(ang).astype(np.float32).T
    pi = idx % 32
    sgn = np.where((idx % 64) < 32, -1.0, 1.0).astype(np.float32)
    c[:, 1536:2560] = cos[pi, :]
    c[:, 2560:3584] = sin[pi, :] * sgn[:, None]
    return c


def fm(v):
    v = np.asarray(v, np.float32).reshape(-1, 128)
    return np.ascontiguousarray(v.T)


def build_program(stop_after=None, passes=(0, 1), scan_steps=99):
    nc = bass.Bass("TRN2", target_bir_lowering=False)

    def din(name, shape):
        return nc.dram_tensor(name, list(shape), F32, kind="ExternalInput").ap()

    def dout(name, shape):
        return nc.dram_tensor(name, list(shape), F32, kind="ExternalOutput").ap()

    xin = din("xin", [NTOK, D])
    cvec_d = din("cvec", [128, 16])
    w_ada = din("w_ada", [D, 6 * D])
    w_in = din("w_in", [D, N_IN])
    w_oa = din("w_oa", [512, D])
    w_or = din("w_or", [512, D])
    w_out = din("w_out", [D, D])
    w_up = din("w_up", [D, 2 * D_FF])
    w_down = din("w_down", [D_FF, D])
    pvec_d = din("pvec", [128, NPV])
    rowp_d = din("rowp", [128, 1408])
    rkblk_d = din("rkblk", [128, 8])
    wupa_d = din("wupa", [65, 1024])
    aupt_d = din("aupt", [128, 1024])
    gup_d = din("gup", [128, 512])
    cst_d = din("cst", [128, NCST])
    ck_d = din("ck", [256, 512])
    cv_d = din("cv", [256, 512])
    st_d = din("st", [2, 512, 64])
    y_d = dout("y", [NTOK, D])
    nk_d = dout("nk", [512, 512])
    nv_d = dout("nv", [512, 512])
    ns_d = dout("ns", [2, 2, 512, 64])
    dbg_d = dout("dbg", [128, 8192]) if stop_after is not None else None

    kb = KB(nc)
    ps = kb.psum_banks()
    PB = [p.ap() for p in ps]
    PBb = [p.ap().bitcast(BF16) for p in ps]

    rr = {"ew": 0}

    def ew():
        rr["ew"] ^= 1
        return "dve" if rr["ew"] else "pool"

    def TS(out, in0, s1, s2, op0, op1):
        return lambda e: e.tensor_scalar(out, in0, s1, s2, op0, op1)

    def TT(out, a, b, op):
        return lambda e: e.tensor_tensor(out, a, b, op)

    def STT(out, in0, sc, in1, op0, op1):
        return lambda e: e.scalar_tensor_tensor(out, in0, sc, in1, op0, op1)

    def CP(out, in_):
        return lambda e: e.tensor_copy(out, in_)

    def bc(ap, shape):
        return ap.to_broadcast(list(shape))

    cst = kb.tile("cst", [128, 1536])
    cstb = kb.tile("cstb", [128, 1024], BF16)
    pv = kb.tile("pv", [128, NPV + 36])
    rowp = kb.tile("rowp", [128, 1408])
    rkblk = kb.tile("rkblk", [128, 8], BF16)
    wupa = kb.tile("wupa", [65, 1024], BF16)
    aupt = kb.tile("aupt", [128, 1024], BF16)
    gup = kb.tile("gup", [128, 512], BF16)
    modT = kb.tile("modT", [128, 96])
    lam = kb.tile("lam", [128, 4])

    kb.dma("sp", cst.ap(), cst_d[:, 0:1536], writes=[cst])
    kb.dma("sp", pv.ap()[:, 0:NPV], pvec_d, writes=[pv])
    kb.dma("sp", rowp.ap(), rowp_d, writes=[rowp])
    kb.dma("pool", rkblk.ap(), rkblk_d, writes=[rkblk])
    kb.dma("pool", wupa.ap(), wupa_d, writes=[wupa])
    kb.dma("pool", aupt.ap(), aupt_d, writes=[aupt])
    kb.dma("pool", gup.ap(), gup_d, writes=[gup])
    C = cst.ap()
    Cb = cstb.ap()
    kb.dve(CP(Cb[:, 0:384], C[:, 0:384]), [cst], [cstb])
    kb.dve(CP(Cb[:, 384:896], C[:, 512:1024]), [cst], [cstb])
    identF = C[:, 0:128]
    identB = Cb[:, 0:128]
    swB = Cb[:, 128:256]
    blkB = Cb[:, 256:384]
    onesd = C[:, 384:512]
    TIB = {0: Cb[:, 384:512], 1: Cb[:, 640:768]}
    TEB = {0: Cb[:, 512:640], 1: Cb[:, 768:896]}
    MASK = {k: C[:, CST[k]:CST[k] + 128] for k in ("lt", "le", "gt", "ge")}
    P = pv.ap()

    def pvc(name, j=None):
        o, n = PV[name]
        return P[:, o:o + n] if j is None else P[:, o + j:o + j + 1]

    c0o = NPV
    omkao = NPV + 14
    kb.dve(TT(P[:, c0o:c0o + 14], pvc("mu0"), pvc("mu1"), ALU.add), [pv], [pv])
    kb.dve(TS(P[:, c0o:c0o + 14], P[:, c0o:c0o + 14], -1.0, 1.0, ALU.mult, ALU.add), [pv], [pv])
    kb.dve(TS(P[:, omkao:omkao + 4], pvc("k_a"), -1.0, 1.0, ALU.mult, ALU.add), [pv], [pv])
    lt_ = kb.tile("lamtmp", [128, 128])
    R = rowp.ap()
    kb.dve(TT(lt_.ap()[:, 0:64], R[:, 1152:1216], R[:, 1216:1280], ALU.mult), [rowp], [lt_])
    kb.dve(TT(lt_.ap()[:, 64:128], R[:, 1280:1344], R[:, 1344:1408], ALU.mult), [rowp], [lt_])
    kb.dve(lambda e: e.tensor_reduce(lam.ap()[:, 2:4], lt_.ap().rearrange("p (a b) -> p a b", a=2), AX.X, ALU.add),
           [lt_], [lam])
    kb.act(lam.ap()[:, 2:4], lam.ap()[:, 2:4], AF.Exp, [lam], [lam])
    kb.dve(TT(lam.ap()[:, 0:1], lam.ap()[:, 2:3], lam.ap()[:, 3:4], ALU.subtract), [lam], [lam])
    kb.dve(TS(lam.ap()[:, 0:1], lam.ap()[:, 0:1], LAM_INIT, None, ALU.add, ALU.bypass) if False else
           (lambda e: e.tensor_scalar_add(lam.ap()[:, 0:1], lam.ap()[:, 0:1], LAM_INIT)), [lam], [lam])
    kb.dve(lambda e: e.tensor_scalar_mul(lam.ap()[:, 1:2], lam.ap()[:, 0:1], -1.0), [lam], [lam])
    kb.dve(lambda e: e.tensor_scalar_mul(R[:, 1024:1152], R[:, 1024:1152], 1.0 - LAM_INIT), [rowp], [rowp])
    kb.free(lt_)

    def phase0():
        cv32 = kb.tile("cv32", [128, 16])
        scb = kb.tile("scb", [128, 16], BF16)
        kb.dma("sp", cv32.ap(), cvec_d, writes=[cv32])
        kb.act(scb.ap(), cv32.ap(), AF.Silu, [cv32], [scb])
        wab = [kb.tile("wa%d" % i, [128, 8, 512], BF16) for i in range(2)]
        wada_v = w_ada.rearrange("(k p) n -> p k n", p=128)
        kb.fresh(ps[0])
        for cg in range(12):
            wt = wab[cg % 2]
            kb.dma("pool", wt.ap(), wada_v[:, :, cg * 512:(cg + 1) * 512], writes=[wt])
            for j in range(4):
                jj = cg * 4 + j
                for k in range(8):
                    kb.mm(ps[0], PB[0][:, 2 * jj:2 * jj + 2], wt.ap()[:, k, j * 128:(j + 1) * 128],
                          scb.ap()[:, 2 * k:2 * k + 2], [wt, scb], start=(k == 0), stop=(k == 7))
        kb.dve(TT(modT.ap().rearrange("p (j g) -> p j g", g=2), PB[0][:, 0:96].rearrange("p (j g) -> p j g", g=2),
                  bc(pvc("b_ada").unsqueeze(2), [128, 48, 2]), ALU.add), [ps[0], pv], [modT])
        kb.dve(lambda e: e.tensor_scalar_add(modT.ap()[:, 16:32], modT.ap()[:, 16:32], 1.0), [modT], [modT])
        kb.dve(lambda e: e.tensor_scalar_add(modT.ap()[:, 64:80], modT.ap()[:, 64:80], 1.0), [modT], [modT])
        kb.free(cv32, scb, *wab)

    M = modT.ap()

    def mod(kind, j, g):
        base = dict(sh1=0, sc1=8, g1=16, sh2=24, sc2=32, g2=40)[kind]
        col = (base + j) * 2 + g
        return M[:, col:col + 1]

    xin_v = xin
    dbgpos = {"o": 0}

    def dump(ap2d, reads, n):
        t = kb.tile("dbgt%d" % dbgpos["o"], [128, n])
        p = ap2d.shape[0]
        kb.dve(CP(t.ap()[0:p, :], ap2d), reads, [t])
        kb.dma("sp", dbg_d[0:p, dbgpos["o"]:dbgpos["o"] + n], t.ap()[0:p, :], reads=[t], is_output=True)
        print("dump at", dbgpos["o"], n)
        dbgpos["o"] += n

    def load_x_T_gen(ps_, pb, t0, nT, emit):
        xt = [kb.tile("xtok%d" % i, [128, D]) for i in range(2)]
        for tt in range(nT // 128):
            x_ = xt[tt % 2]
            kb.dma("sp", x_.ap(), xin_v[t0 + tt * 128:t0 + (tt + 1) * 128, :], writes=[x_])
            for half in range(2):
                b = (2 * tt + half) % 4
                for cc in range(4):
                    c = half * 4 + cc
                    kb.transpose(ps_[b], pb[b][:, cc * 128:(cc + 1) * 128], x_.ap()[:, c * 128:(c + 1) * 128], identF,
                                 [x_, cst])
                for cc in range(4):
                    c = half * 4 + cc
                    emit(c, tt, pb[b][:, cc * 128:(cc + 1) * 128], ps_[b])
            yield
        kb.free(*xt)

    def load_x_T(ps_, pb, t0, nT, emit):
        for _ in load_x_T_gen(ps_, pb, t0, nT, emit):
            pass

    win_v = w_in.rearrange("(k p) n -> p k n", p=128)
    mod_done = [False]

    def TSM(out, in_, sc):
        return lambda e: e.tensor_scalar_mul(out, in_, sc)

    def TSA(out, in_, sc):
        return lambda e: e.tensor_scalar_add(out, in_, sc)

    def RCP(out, in_):
        return lambda e: e.reciprocal(out, in_)

    def RSUM(out, in_):
        return lambda e: e.tensor_reduce(out, in_, AX.X, ALU.add)

    def MSET(out, v):
        return lambda e: e.memset(out, v)

    ecnt = {"i": 0}

    def evac_copy(out, in_, reads, writes):
        ecnt["i"] += 1
        if ecnt["i"] % 2:
            kb.act(out, in_, AF.Copy, reads, writes)
        else:
            kb.dve(CP(out, in_), reads, writes)


    for pi in passes:
        PS = PASSES[pi]
        t0, nT, seqs, g, sample = PS["t0"], PS["nT"], PS["seqs"], PS["g"], PS["sample"]
        NTT = nT // 128
        NN = nT // 512
        oattT = kb.tile("oattT", [128, 4, nT], BF16)
        hT = kb.tile("hT", [128, 8, nT], BF16, nslots=NTT)
        rT = kb.tile("rT", [128, 4, nT], BF16)
        krT = kb.tile("krT", [128, 4, nT], BF16)
        kkT = kb.tile("kkT", [128, 4, nT])
        SDT = F32 if SCAN_F32 else BF16
        Vt = kb.tile("Vt", [128, NTT, 512], BF16, nslots=NTT)
        tw = kb.tile("tw", [65, nT], BF16)
        alo = kb.tile("alo", [128, nT], BF16)
        sg = kb.tile("sg", [128, nT], BF16)
        yac = kb.tile("yac", [128, NTT, 512], F32, nslots=NTT)
        bon = kb.tile("bon", [128, NTT, 8], F32, nslots=NTT)

        cnt = {"i": 0}
        if not mod_done[0]:
            xT0 = kb.tile("xT0", [128, 8, nT], F32, nslots=NTT)

            def emit_raw(c, tt, pap, pst):
                evac_copy(xT0.ap()[:, c, tt * 128:(tt + 1) * 128], pap, [pst], [(xT0, tt)])

            load_x_T(ps, PB, t0, nT, emit_raw)
            phase0()
            mod_done[0] = True
            for c in range(8):
                for n in range(NN):
                    ns_ = slice(n * 512, (n + 1) * 512)
                    rd = [(xT0, range(4 * n, 4 * n + 4)), modT]
                    wr_ = [(hT, range(4 * n, 4 * n + 4))]
                    cnt["i"] += 1
                    if cnt["i"] % 2:
                        kb.act(hT.ap()[:, c, ns_], xT0.ap()[:, c, ns_], AF.Identity, rd, wr_, bias=mod("sh1", c, g), scale=mod("sc1", c, g))
                    else:
                        kb.dve(TS(hT.ap()[:, c, ns_], xT0.ap()[:, c, ns_], mod("sc1", c, g), mod("sh1", c, g), ALU.mult, ALU.add), rd, wr_)
            kb.free(xT0)
        else:
            def emit_h(c, tt, pap, pst):
                out = hT.ap()[:, c, tt * 128:(tt + 1) * 128]
                cnt["i"] += 1
                if cnt["i"] % 2:
                    kb.act(out, pap, AF.Identity, [pst, modT], [(hT, tt)], bias=mod("sh1", c, g), scale=mod("sc1", c, g))
                else:
                    kb.dve(TS(out, pap, mod("sc1", c, g), mod("sh1", c, g), ALU.mult, ALU.add), [pst, modT], [(hT, tt)])

            load_x_T(ps, PB, t0, nT, emit_h)
        if stop_after == "A":
            dump(modT.ap(), [modT], 96)
            dump(hT.ap()[:, 0, :], [hT], nT)
            dump(hT.ap()[:, 7, :], [hT], nT)
            break

        ropet = stg = None
        if sample:
            ropet = kb.tile("ropet", [128, 2048])
            stg = kb.tile("stg", [128, 1024])
            kb.dma("act", ropet.ap(), cst_d[:, 1536:3584], writes=[ropet])
            cosT = ropet.ap()[:, 0:1024]
            sinT = ropet.ap()[:, 1024:2048]
        qT = kb.tile("qT", [128, 4, nT], BF16)
        kT = kb.tile("kT", [128, 4, nT], BF16)
        NKT = NTT + (2 if sample else 0)
        v1 = kb.tile("v1", [128, NKT, 4, 130], BF16, nslots=NKT)
        kb.pool(MSET(v1.ap()[:, :, :, 128:130], 1.0), [], [v1])
        wq = [kb.tile("wq%d" % i, [128, 8, 512], BF16) for i in range(2)]
        qraw = [kb.tile("qraw%d" % i, [128, 512], BF16) for i in range(2)]
        rt1 = [kb.tile("rt1_%d" % i, [128, 512]) for i in range(2)]
        rt2 = [kb.tile("rt2_%d" % i, [128, 512]) for i in range(2)]
        kf32 = None
        if not sample:
            kf32 = kb.tile("kf32", [128, 4, nT])
        it = 0
        for wi, (dst, col0, scl) in enumerate([(qT, 0, 0.125), (kT, 512, 1.0)]):
            wt = wq[wi % 2]
            kb.dma("pool", wt.ap(), win_v[:, :, col0:col0 + 512], writes=[wt])
            for c in range(4):
                for n in range(NN):
                    b = it % 4
                    kb.fresh(ps[b])
                    for k in range(8):
                        kb.mm(ps[b], PB[b], wt.ap()[:, k, c * 128:(c + 1) * 128], hT.ap()[:, k, n * 512:(n + 1) * 512],
                              [wt, (hT, range(4 * n, 4 * n + 4))], start=(k == 0), stop=(k == 7))
                    outap = dst.ap()[:, c, n * 512:(n + 1) * 512]
                    if sample:
                        qr = qraw[it % 2]
                        kb.act(qr.ap(), PB[b], AF.Copy, [ps[b]], [qr], scale=scl)
                        b2 = 4 + it % 2
                        kb.fresh(ps[b2])
                        kb.mm(ps[b2], PB[b2], swB, qr.ap(), [cstb, qr])
                        a1, a2 = rt1[it % 2], rt2[it % 2]
                        kb.pool(TT(a1.ap(), qr.ap(), cosT[:, n * 512:(n + 1) * 512], ALU.mult), [qr, ropet], [a1])
                        kb.dve(TT(a2.ap(), PB[b2], sinT[:, n * 512:(n + 1) * 512], ALU.mult), [ps[b2], ropet], [a2])
                        kb.dve(TT(outap, a1.ap(), a2.ap(), ALU.add), [a1, a2], [dst])
                    else:
                        kb.act(outap, PB[b], AF.Copy, [ps[b]], [dst], scale=scl)
                        if dst is kT:
                            kb.dve(CP(kf32.ap()[:, c, n * 512:(n + 1) * 512], PB[b]), [ps[b]], [kf32])
                    it += 1
        wt = wq[0]
        kb.dma("pool", wt.ap(), win_v[:, :, 1024:1536], writes=[wt])
        vout = None
        if not sample:
            vout = [kb.tile("vout%d" % i, [128, 512]) for i in range(2)]
        for tt in range(NTT):
            b = tt % 4
            kb.fresh(ps[b])
            for k in range(8):
                kb.mm(ps[b], PB[b], hT.ap()[:, k, tt * 128:(tt + 1) * 128], wt.ap()[:, k, :], [wt, (hT, tt)],
                      start=(k == 0), stop=(k == 7))
            kb.act(v1.ap()[:, tt, :, 0:128], PB[b].rearrange("p (h e) -> p h e", h=4), AF.Copy, [ps[b]], [(v1, tt)])
            if not sample:
                vo = vout[tt % 2]
                kb.dve(CP(vo.ap(), PB[b]), [ps[b]], [vo])
                kb.dma("sp", nv_d[tt * 128:(tt + 1) * 128, :], vo.ap(), reads=[vo], is_output=True)
        if not sample:
            for tt in range(NTT):
                b = 4 + tt % 2
                for c in range(4):
                    kb.transpose(ps[b], PB[b][:, c * 128:(c + 1) * 128], kf32.ap()[:, c, tt * 128:(tt + 1) * 128], identF,
                                 [kf32, cst])
                vo = vout[tt % 2]
                kb.dve(CP(vo.ap(), PB[b]), [ps[b]], [vo])
                kb.dma("sp", nk_d[tt * 128:(tt + 1) * 128, :], vo.ap(), reads=[vo], is_output=True)
            kb.free(kf32, *vout)
        kcT = None
        if sample:
            kcT = kb.tile("kcT", [128, 4, 256], BF16)
            for tl in range(2):
                kb.dma("sp", stg.ap()[:, 0:512], ck_d[tl * 128:(tl + 1) * 128, :], writes=[stg])
                kb.dma("act", stg.ap()[:, 512:1024], cv_d[tl * 128:(tl + 1) * 128, :], writes=[stg])
                b = 4 + tl
                for h in range(4):
                    kb.transpose(ps[b], PB[b][:, h * 128:(h + 1) * 128], stg.ap()[:, h * 128:(h + 1) * 128], identF,
                                 [stg, cst])
                kb.act(kcT.ap()[:, :, tl * 128:(tl + 1) * 128], PB[b].rearrange("p (h e) -> p h e", h=4), AF.Copy,
                       [ps[b]], [kcT])
                kb.dve(CP(v1.ap()[:, NTT + tl, :, 0:128], stg.ap()[:, 512:1024].rearrange("p (h e) -> p h e", h=4)),
                       [stg], [(v1, NTT + tl)])
        kb.free(*wq, *qraw, *rt1, *rt2)
        if sample:
            kb.free(ropet, stg)
        if stop_after == "Bproj":
            break

        otok = kb.tile("otok", [128, NTT, 512], F32, nslots=NTT)
        accs = kb.tile("accs", [128, 8, 130])
        rec = kb.tile("rec", [128, 8])
        exb = [kb.tile("exb%d" % i, [128, 512], BF16) for i in range(3)]
        tmp1 = kb.tile("atmp1", [128, 4, 128])
        ei_box = [0]
        si_box = [0]
        def attn_gen():
            for (s0, L) in seqs:
                QB = 512 if L >= 512 else L
                nqs = QB // 128
                ktiles = [("own", s0 // 128 + i) for i in range(L // 128)]
                if sample:
                    ktiles += [("cache", 0), ("cache", 1)]
                for h in range(4):
                    for qb in range(L // QB):
                        q0 = s0 + qb * QB
                        kb.fresh(ps[4], ps[5], ps[6])
                        def qk(ki, kind, kt, m):
                            sb = si_box[0] % 4
                            si_box[0] += 1
                            if kind == "own":
                                kl = kT.ap()[64 * m:64 * m + 64, h, kt * 128:(kt + 1) * 128]
                                kr_ = [kT]
                            else:
                                kl = kcT.ap()[64 * m:64 * m + 64, h, kt * 128:(kt + 1) * 128]
                                kr_ = [kcT]
                            kb.fresh(ps[sb])
                            kb.mm(ps[sb], PB[sb][:, 0:QB], kl, qT.ap()[64 * m:64 * m + 64, h, q0:q0 + QB], kr_ + [qT])
                            return sb

                        def qk2(i):
                            kind, kt = ktiles[i]
                            return [qk(i, kind, kt, 0), qk(i, kind, kt, 1)]

                        pend = [qk2(0)]
                        for ki, (kind, kt) in enumerate(ktiles):
                            if ki + 1 < len(ktiles):
                                pend.append(qk2(ki + 1))
                            sbs = pend.pop(0)
                            vt = kt if kind == "own" else NTT + kt
                            for m in range(2):
                                sb = sbs[m]
                                ex = exb[ei_box[0] % 3]
                                ei_box[0] += 1
                                kb.act(ex.ap()[:, 0:QB], PB[sb][:, 0:QB], AF.Exp, [ps[sb]], [ex])
                                for qs in range(nqs):
                                    a = m * 4 + qs
                                    ab = 4 + a // 3
                                    ao = (a % 3) * 130
                                    kb.mm(ps[ab], PB[ab][:, ao:ao + 129], ex.ap()[:, qs * 128:(qs + 1) * 128],
                                          v1.ap()[:, vt, h, 0:129], [ex, (v1, vt)])
                        A = accs.ap()
                        for bnk in range(3):
                            na = min(3, 8 - bnk * 3)
                            kb.act(A[:, bnk * 3:bnk * 3 + na, 0:129],
                                   PB[4 + bnk][:, 0:na * 130].rearrange("p (a e) -> p a e", a=na)[:, :, 0:129],
                                   AF.Copy, [ps[4 + bnk]], [accs])
                        kb.dve(RCP(rec.ap(), A[:, :, 128]), [accs], [rec])
                        kb.dve(TSM(rec.ap()[:, 4:8], rec.ap()[:, 4:8], lam.ap()[:, 1:2]), [rec, lam], [rec])
                        T1 = tmp1.ap()[:, 0:nqs, :]
                        kb.dve(TT(T1, A[:, 0:nqs, 0:128], bc(rec.ap()[:, 0:nqs].unsqueeze(2), [128, nqs, 128]), ALU.mult),
                               [accs, rec], [tmp1])
                        kb.pool(TT(A[:, 4:4 + nqs, 0:128], A[:, 4:4 + nqs, 0:128],
                                   bc(rec.ap()[:, 4:4 + nqs].unsqueeze(2), [128, nqs, 128]), ALU.mult), [accs, rec], [accs])
                        tq0 = q0 // 128
                        kb.dve(TT(otok.ap()[:, tq0:tq0 + nqs, h * 128:(h + 1) * 128], T1, A[:, 4:4 + nqs, 0:128], ALU.add),
                               [tmp1, accs], [(otok, range(tq0, tq0 + nqs))])
                        yield

        ATTN = attn_gen()
        rrb = {"i": 0}

        def nbank():
            i = rrb["i"]
            rrb["i"] = (i + 1) % 8
            kb.fresh(ps[i])
            return i

        def P3(b, a=4):
            return PB[b].rearrange("p (a e) -> p a e", a=a)

        def phc_gen():
            RW0 = 1536
            raw = [kb.tile("raw%d" % i, [128, nT]) for i in range(2)]
            wr = [kb.tile("wr%d" % i, [128, 8, 512], BF16) for i in range(2)]
            mixo = kb.tile("mixo", [128, nT])
            mixb = kb.tile("mixb", [128, nT], BF16)
            kb.pool(MSET(tw.ap()[64:65, :], 1.0), [], [tw])
            for c14 in range(14):
                wtile = wr[(c14 // 4) % 2]
                if c14 % 4 == 0:
                    ncol = min(512, 1792 - c14 * 128)
                    kb.dma("pool", wtile.ap()[:, :, 0:ncol], win_v[:, :, RW0 + c14 * 128:RW0 + c14 * 128 + ncol], writes=[wtile])

                class _W:
                    pass
                wt = _W()
                wt.t = wtile
                wt.v = wtile.ap()[:, :, (c14 % 4) * 128:(c14 % 4 + 1) * 128]
                rw_ = raw[c14 % 2]
                for n in range(NN):
                    b = nbank()
                    for k in range(8):
                        kb.mm(ps[b], PB[b], wt.v[:, k, :], hT.ap()[:, k, n * 512:(n + 1) * 512],
                              [wt.t, (hT, range(4 * n, 4 * n + 4))])
                    kb.act(rw_.ap()[:, n * 512:(n + 1) * 512], PB[b], AF.Copy, [ps[b]], [rw_])
                if c14 < 4:
                    dst, dt_ = rT.ap()[:, c14, :], rT
                elif c14 < 8:
                    dst, dt_ = krT.ap()[:, c14 - 4, :], krT
                else:
                    dst, dt_ = mixo.ap(), mixo
                kb.act(dst, rw_.ap(), AF.Copy, [rw_, pv], [dt_], scale=P[:, c0o + c14:c0o + c14 + 1])
                for (s0, L) in seqs:
                    kb.dve(STT(dst[:, s0 + 1:s0 + L], rw_.ap()[:, s0:s0 + L - 1], pvc("mu0", c14), dst[:, s0 + 1:s0 + L],
                               ALU.mult, ALU.add), [rw_, pv, dt_], [dt_])
                    kb.dve(STT(dst[:, s0:s0 + L - 1], rw_.ap()[:, s0 + 1:s0 + L], pvc("mu1", c14), dst[:, s0:s0 + L - 1],
                               ALU.mult, ALU.add), [rw_, pv, dt_], [dt_])
                if 8 <= c14 < 12:
                    vc = c14 - 8
                    kb.pool(CP(mixb.ap(), mixo.ap()), [mixo], [mixb])
                    for t8 in range(0, NTT, 8):
                        nt8 = min(8, NTT - t8)
                        b = nbank()
                        for i in range(nt8):
                            kb.transpose(ps[b], PBb[b][:, i * 128:(i + 1) * 128], mixb.ap()[:, (t8 + i) * 128:(t8 + i + 1) * 128],
                                         identB, [mixb, cstb])
                        kb.act(Vt.ap()[:, t8:t8 + nt8, vc * 128:(vc + 1) * 128],
                               PBb[b][:, 0:nt8 * 128].rearrange("p (a e) -> p a e", a=nt8), AF.Copy, [ps[b]],
                               [(Vt, range(t8, t8 + nt8))])
                elif c14 == 12:
                    kb.act(tw.ap()[0:64, :], mixo.ap()[0:64, :], AF.Tanh, [mixo], [tw])
                    kb.dve(CP(alo.ap()[64:128, :], mixo.ap()[64:128, :]), [mixo], [alo])
                elif c14 == 13:
                    kb.act(sg.ap(), mixo.ap(), AF.Sigmoid, [mixo], [sg])
                yield
            for c in range(4):
                kb.dve(TSM(kkT.ap()[:, c, :], krT.ap()[:, c, :], pvc("k_k", c)), [krT, pv], [kkT])
                kb.pool(TT(mixb.ap(), kkT.ap()[:, c, :], kkT.ap()[:, c, :], ALU.mult), [kkT], [mixb])
                for n in range(NN):
                    b = nbank()
                    kb.mm(ps[b], PB[b], blkB, mixb.ap()[:, n * 512:(n + 1) * 512], [cstb, mixb])
                    r_ = raw[0].ap()[:, n * 512:(n + 1) * 512]
                    kb.act(r_, PB[b], AF.Sqrt, [ps[b]], [raw[0]])
                    kb.dve(lambda e, r_=r_: e.tensor_scalar_max(r_, r_, 1e-12), [raw[0]], [raw[0]])
                    kb.dve(RCP(r_, r_), [raw[0]], [raw[0]])
                    kb.dve(TT(kkT.ap()[:, c, n * 512:(n + 1) * 512], kkT.ap()[:, c, n * 512:(n + 1) * 512], r_, ALU.mult),
                           [kkT, raw[0]], [kkT])
                yield

            kb.free(*raw, *wr, mixo, mixb)

        PHC = phc_gen()
        alive = {"a": ATTN, "c": PHC}
        n_att = sum((L // (512 if L >= 512 else L)) * 4 for (_s0, L) in seqs)
        per = -(-18 // n_att)
        while alive:
            for nm_, cnt_ in (("a", 1), ("c", per)):
                for _ in range(cnt_):
                    if nm_ in alive:
                        try:
                            next(alive[nm_])
                        except StopIteration:
                            del alive[nm_]
        kb.free(accs, rec, tmp1, *exb, qT, kT, v1)
        if kcT is not None:
            kb.free(kcT)
        sq = kb.tile("osq", [128, 512])
        ss = kb.tile("oss", [128, 4])
        for tt in range(NTT):
            O = otok.ap()[:, tt, :]
            O3 = O.rearrange("p (h e) -> p h e", h=4)
            kb.pool(TT(sq.ap(), O, O, ALU.mult), [(otok, tt)], [sq])
            kb.dve(RSUM(ss.ap(), sq.ap().rearrange("p (h e) -> p h e", h=4)), [sq], [ss])
            kb.dve(TS(ss.ap(), ss.ap(), 1.0 / 128.0, 1e-5, ALU.mult, ALU.add), [ss], [ss])
            kb.act(ss.ap(), ss.ap(), AF.Sqrt, [ss], [ss])
            kb.dve(RCP(ss.ap(), ss.ap()), [ss], [ss])
            kb.dve(TT(O3, O3, bc(ss.ap().unsqueeze(2), [128, 4, 128]), ALU.mult), [(otok, tt), ss], [(otok, tt)])
            ob = sq.ap().bitcast(BF16)[:, 0:512]
            kb.dve(TT(ob.rearrange("p (h e) -> p h e", h=4), O3, bc(R[:, 1024:1152].unsqueeze(1), [128, 4, 128]),
                      ALU.mult), [(otok, tt), rowp], [sq])
            b = tt % 2
            for h in range(4):
                kb.transpose(ps[b], PBb[b][:, h * 128:(h + 1) * 128], ob[:, h * 128:(h + 1) * 128], identB, [sq, cstb])
            kb.act(oattT.ap()[:, :, tt * 128:(tt + 1) * 128], PBb[b][:, 0:512].rearrange("p (h e) -> p h e", h=4),
                   AF.Copy, [ps[b]], [oattT])
        kb.free(sq, ss, otok)
        if stop_after == "B":
            for c_ in range(4):
                dump(oattT.ap()[:, c_, 0:512], [oattT], 512)
            dump(lam.ap(), [lam], 4)
            break

        if stop_after == "C":
            dump(rT.ap()[:, 0, 0:512], [rT], 512)
            dump(kkT.ap()[:, 1, 0:512], [kkT], 512)
            dump(Vt.ap()[:, 1, :], [Vt], 512)
            dump(tw.ap()[0:65, 0:512], [tw], 512)
            dump(sg.ap()[:, 0:512], [sg], 512)
            break

        kb.free(hT)

        def f3():
            return [128, 4, 128]

        def v3(t):
            return t.ap().rearrange("p (c e) -> p c e", c=4)

        def mkset(k):
            T_ = {}
            for nm in ["b0", "b1", "b2", "b3", "b4", "b5", "Wic", "Wlm", "Ah", "Bt", "Kt", "Rh", "Btok"]:
                T_[nm] = kb.tile("%s_%d" % (nm, k), [128, 512])
            for nm in ["h0", "h1", "rkp", "Ktok"]:
                T_[nm] = kb.tile("%s_%d" % (nm, k), [128, 512], BF16)
            T_["Gt"] = kb.tile("Gt_%d" % k, [128, 4])
            for nm in ["MrbT", "LabTf"]:
                T_[nm] = kb.tile("%s_%d" % (nm, k), [128, 2, 4, 128])
            for nm in ["LakT", "MrkT", "Pm0", "Pm1", "Ptm0", "Ptm1", "Inv0", "Inv1"]:
                T_[nm] = kb.tile("%s_%d" % (nm, k), [128, 2, 4, 128], BF16)
            return T_

        sets = [mkset(0), mkset(1)]
        ka_b = bc(pvc("k_a").unsqueeze(2), f3())
        omka_b = bc(P[:, omkao:omkao + 4].unsqueeze(2), f3())
        ydone = set()
        bdone = set()

        def unit(T_, S, d, tt):
            aT, keff, bq, Wex, Win, sig = T_["b0"], T_["b1"], T_["b2"], T_["b3"], T_["b4"], T_["b5"]
            Z0f, tmpS, Z0b, Xf, U0f, Ub = aT, keff, bq, Wex, Win, sig
            shi, slo = T_["h0"], T_["h1"]
            Xh, rb = shi, slo
            Wic, Wlm, Ah, Bt, Kt, Rh, Btok, Ktok, rkp, Gt = (T_[k_] for k_ in
                                                             ["Wic", "Wlm", "Ah", "Bt", "Kt", "Rh", "Btok", "Ktok", "rkp", "Gt"])
            MrbT, LabTf, LakT, MrkT = T_["MrbT"], T_["LabTf"], T_["LakT"], T_["MrkT"]
            Pm, Ptm, Inv = [T_["Pm0"], T_["Pm1"]], [T_["Ptm0"], T_["Ptm1"]], [T_["Inv0"], T_["Inv1"]]
            mk = dict(labT="lt", lab="gt", mr="le") if d == 0 else dict(labT="gt", lab="lt", mr="ge")
            tf, tl = (0, 127) if d == 0 else (127, 0)
            o = tt * 128
            sl = slice(o, o + 128)
            b0 = nbank()
            for c in range(4):
                kb.mm(ps[b0], PB[b0][:, c * 128:(c + 1) * 128],
                      aupt.ap()[64:128, d * 512 + c * 128:d * 512 + (c + 1) * 128], alo.ap()[64:128, sl], [aupt, alo])
            kb.dve(TT(v3(aT), P3(b0), bc(pvc("a0_%d" % d).unsqueeze(2), f3()), ALU.add), [ps[b0], pv], [aT])
            kb.act(aT.ap(), aT.ap(), AF.Sigmoid, [aT], [aT])
            bz = nbank()
            kb.mm(ps[bz], PB[bz], tw.ap()[0:65, sl], wupa.ap()[0:65, d * 512:(d + 1) * 512], [tw, wupa])
            kb.act(sig.ap(), PB[bz], AF.Sigmoid, [ps[bz]], [sig])
            yield
            kb.gp(TT(v3(keff), v3(aT), ka_b, ALU.mult), [aT, pv], [keff])
            kb.gp(TT(v3(keff), v3(keff), omka_b, ALU.add), [keff, pv], [keff])
            kb.gp(TT(v3(keff), v3(keff), krT.ap()[:, :, sl], ALU.mult), [keff, krT], [keff])
            kb.gp(TT(v3(bq), kkT.ap()[:, :, sl], v3(aT), ALU.mult), [kkT, aT], [bq])
            kb.gp(TT(v3(rkp), rT.ap()[:, :, sl], v3(keff), ALU.mult), [rT, keff], [rkp])
            bB = nbank()
            for c in range(4):
                kb.mm(ps[bB], PB[bB][:, 2 * c:2 * c + 2], v3(rkp)[:, c, :], rkblk.ap()[:, 2 * c:2 * c + 2], [rkp, rkblk])
            if tt not in bdone:
                bdone.add(tt)
                kb.act(bon.ap()[:, tt, :], PB[bB][:, 0:8], AF.Copy, [ps[bB]], [(bon, tt)])
            else:
                kb.dve(TT(bon.ap()[:, tt, :], PB[bB][:, 0:8], bon.ap()[:, tt, :], ALU.add), [ps[bB], (bon, tt)], [(bon, tt)])
            kb.act(shi.ap(), sig.ap(), AF.Copy, [sig], [shi])
            kb.dve(TT(slo.ap(), sig.ap(), shi.ap(), ALU.subtract), [sig, shi], [slo])
            bci = nbank()
            bce = nbank()
            for c in range(4):
                cs = slice(c * 128, (c + 1) * 128)
                kb.mm(ps[bci], PB[bci][:, cs], shi.ap()[:, cs], TIB[d], [shi, cstb])
                kb.mm(ps[bci], PB[bci][:, cs], slo.ap()[:, cs], TIB[d], [slo, cstb])
            for c in range(4):
                cs = slice(c * 128, (c + 1) * 128)
                kb.mm(ps[bce], PB[bce][:, cs], shi.ap()[:, cs], TEB[d], [shi, cstb])
                kb.mm(ps[bce], PB[bce][:, cs], slo.ap()[:, cs], TEB[d], [slo, cstb])
            kb.act(Wex.ap(), PB[bce], AF.Exp, [ps[bce]], [Wex], scale=CL)
            kb.act(Gt.ap(), P3(bce)[:, :, tf], AF.Exp, [ps[bce]], [Gt], scale=-CL)
            kb.act(Win.ap(), PB[bci], AF.Exp, [ps[bci]], [Win], scale=-CL)
            kb.act(Wic.ap(), PB[bci], AF.Exp, [ps[bci]], [Wic], scale=CL)
            yield
            kb.gp(TT(v3(Wlm), bc(C[:, 256:384].unsqueeze(1), f3()), bc(v3(Wic)[:, :, tl:tl + 1], f3()), ALU.mult),
                  [cst, Wic], [Wlm])
            kb.dve(STT(v3(Ah), kkT.ap()[:, :, sl], -1.0, v3(Wex), ALU.mult, ALU.mult), [kkT, Wex], [Ah])
            kb.dve(TT(Bt.ap(), bq.ap(), Win.ap(), ALU.mult), [bq, Win], [Bt])
            kb.dve(TT(Kt.ap(), keff.ap(), Win.ap(), ALU.mult), [keff, Win], [Kt])
            kb.dve(TT(v3(Rh), rT.ap()[:, :, sl], v3(Wic), ALU.mult), [rT, Wic], [Rh])
            bt_ = nbank()
            for c in range(4):
                kb.transpose(ps[bt_], PB[bt_][:, c * 128:(c + 1) * 128], v3(Bt)[:, c, :], identF, [Bt, cst])
            kb.act(Btok.ap(), PB[bt_], AF.Copy, [ps[bt_]], [Btok])
            bt_ = nbank()
            for c in range(4):
                kb.transpose(ps[bt_], PB[bt_][:, c * 128:(c + 1) * 128], v3(Kt)[:, c, :], identF, [Kt, cst])
            kb.act(Ktok.ap(), PB[bt_], AF.Copy, [ps[bt_]], [Ktok])
            yield

            def Lmat(dstt, lt, rt, mname, eng="dve"):
                for h2 in range(2):
                    b = nbank()
                    r0 = 64 * h2
                    for c in range(4):
                        kb.mm(ps[b], PB[b][:, c * 128:(c + 1) * 128], v3(lt)[r0:r0 + 64, c, :], v3(rt)[r0:r0 + 64, c, :], [lt, rt])
                    kb.dve(TT(dstt.ap()[:, h2, :, :], P3(b), bc(MASK[mname].unsqueeze(1), f3()), ALU.mult), [ps[b], cst], [dstt])
                    if FINE_YIELD:
                        yield

            yield from Lmat(LabTf, Bt, Ah, mk["labT"])
            kb.act(Ptm[0].ap(), LabTf.ap(), AF.Copy, [LabTf], [Ptm[0]])
            kb.dve(TT(LabTf.ap().rearrange("p a c e -> p (a c) e"), LabTf.ap().rearrange("p a c e -> p (a c) e"),
                      bc(identF.unsqueeze(1), [128, 8, 128]), ALU.subtract), [LabTf, cst], [LabTf])
            yield from Lmat(Pm[0], Ah, Bt, mk["lab"])
            yield
            yield from Lmat(LakT, Kt, Ah, mk["labT"])
            yield from Lmat(MrbT, Bt, Rh, mk["mr"])
            yield from Lmat(MrkT, Kt, Rh, mk["mr"])
            kb.dve(TT(Inv[0].ap().rearrange("p a c e -> p (a c) e"), Ptm[0].ap().rearrange("p a c e -> p (a c) e"),
                      bc(identB.unsqueeze(1), [128, 8, 128]), ALU.add), [Ptm[0], cstb], [Inv[0]])
            yield
            cur = 0
            for lvl in range(1, 7):
                nx = 1 - cur
                for hg in range(2):
                    b = nbank()
                    for hh in range(4):
                        kb.mm(ps[b], PB[b][:, hh * 128:(hh + 1) * 128], Ptm[cur].ap()[:, hg, hh, :], Pm[cur].ap()[:, hg, hh, :],
                              [Ptm[cur], Pm[cur]])
                    kb.act(Pm[nx].ap()[:, hg, :, :], P3(b), AF.Copy, [ps[b]], [Pm[nx]])
                    if FINE_YIELD:
                        yield
                if lvl < 6:
                    for hg in range(2):
                        b = nbank()
                        for hh in range(4):
                            kb.mm(ps[b], PB[b][:, hh * 128:(hh + 1) * 128], Pm[cur].ap()[:, hg, hh, :], Ptm[cur].ap()[:, hg, hh, :],
                                  [Ptm[cur], Pm[cur]])
                        kb.act(Ptm[nx].ap()[:, hg, :, :], P3(b), AF.Copy, [ps[b]], [Ptm[nx]])
                        if FINE_YIELD:
                            yield
                if not COARSE:
                    yield
                for hg in range(2):
                    b = nbank()
                    for hh in range(4):
                        kb.mm(ps[b], PB[b][:, hh * 128:(hh + 1) * 128], Pm[nx].ap()[:, hg, hh, :], Inv[cur].ap()[:, hg, hh, :],
                              [Pm[nx], Inv[cur]])
                    kb.dve(TT(Inv[nx].ap()[:, hg, :, :], P3(b), Inv[cur].ap()[:, hg, :, :], ALU.add), [ps[b], Inv[cur]], [Inv[nx]])
                    if FINE_YIELD:
                        yield
                cur = nx
                yield
            InvT = Inv[cur]
            kb.dve(TT(v3(Z0f), S.ap(), bc(Gt.ap().unsqueeze(2), f3()), ALU.mult), [S, Gt], [Z0f])
            Z0b = Z0f
            bx = nbank()
            for h in range(8):
                c, h2 = h // 2, h % 2
                hs = slice(h * 64, (h + 1) * 64)
                kb.mm(ps[bx], PB[bx][:, hs], LakT.ap()[:, h2, c, :], Vt.ap()[:, tt, hs], [LakT, (Vt, tt)])
            for c in range(4):
                cs = slice(c * 128, (c + 1) * 128)
                kb.mm(ps[bx], PB[bx][:, cs], v3(Ah)[:, c, :], v3(Z0b)[:, c, :], [Ah, Z0b])
            kb.act(Xf.ap(), PB[bx], AF.Copy, [ps[bx]], [Xf])
            kb.act(Xh.ap(), PB[bx], AF.Copy, [ps[bx]], [Xh])
            yield
            bu = nbank()
            for h in range(8):
                hs = slice(h * 64, (h + 1) * 64)
                kb.mm(ps[bu], PB[bu][:, hs], InvT.ap()[:, h % 2, h // 2, :], Xh.ap()[:, hs], [InvT, Xh])
            kb.act(U0f.ap(), PB[bu], AF.Copy, [ps[bu]], [U0f])
            yield
            bl = nbank()
            for h in range(8):
                hs = slice(h * 64, (h + 1) * 64)
                kb.mm(ps[bl], PB[bl][:, hs], LabTf.ap()[:, h % 2, h // 2, :], U0f.ap()[:, hs], [LabTf, U0f])
            kb.dve(TT(rb.ap(), PB[bl], Xf.ap(), ALU.add), [ps[bl], Xf], [rb])
            yield
            bd_ = nbank()
            for h in range(8):
                hs = slice(h * 64, (h + 1) * 64)
                kb.mm(ps[bd_], PB[bd_][:, hs], InvT.ap()[:, h % 2, h // 2, :], rb.ap()[:, hs], [InvT, rb])
            kb.dve(TT(Ub.ap(), PB[bd_], U0f.ap(), ALU.add), [ps[bd_], U0f], [Ub])
            yield
            by = nbank()
            for h in range(8):
                c, h2 = h // 2, h % 2
                hs = slice(h * 64, (h + 1) * 64)
                kb.mm(ps[by], PB[by][:, hs], MrkT.ap()[:, h2, c, :], Vt.ap()[:, tt, hs], [MrkT, (Vt, tt)])
            for c in range(4):
                cs = slice(c * 128, (c + 1) * 128)
                kb.mm(ps[by], PB[by][:, cs], v3(Rh)[:, c, :], v3(Z0b)[:, c, :], [Rh, Z0b])
            for h in range(8):
                c, h2 = h // 2, h % 2
                hs = slice(h * 64, (h + 1) * 64)
                kb.mm(ps[by], PB[by][:, hs], MrbT.ap()[:, h2, c, :], Ub.ap()[:, hs], [MrbT, Ub])
            if tt not in ydone:
                ydone.add(tt)
                kb.act(yac.ap()[:, tt, :], PB[by], AF.Copy, [ps[by]], [(yac, tt)])
            else:
                kb.dve(TT(yac.ap()[:, tt, :], PB[by], yac.ap()[:, tt, :], ALU.add), [ps[by], (yac, tt)], [(yac, tt)])
            bs = nbank()
            for c in range(4):
                cs = slice(c * 128, (c + 1) * 128)
                kb.mm(ps[bs], PB[bs][:, cs], Ktok.ap()[:, cs], Vt.ap()[:, tt, cs], [Ktok, (Vt, tt)])
            for c in range(4):
                cs = slice(c * 128, (c + 1) * 128)
                kb.mm(ps[bs], PB[bs][:, cs], Btok.ap()[:, cs], Ub.ap()[:, cs], [Btok, Ub])
            kb.dve(TT(tmpS.ap(), PB[bs], Z0f.ap(), ALU.add), [ps[bs], Z0f], [tmpS])
            kb.dve(TT(S.ap(), v3(tmpS), v3(Wlm), ALU.mult), [tmpS, Wlm], [S])
            yield

        def chain(T_, si, s0, L, d):
            S = kb.tile("S_%d_%d" % (si, d), f3())
            if sample:
                stn = kb.tile("stn%d" % d, [128, 4, 64])
                bd = kb.tile("bd%d" % d, f3())
                kb.dma("sp", stn.ap(), st_d[d].rearrange("(c p) k -> p c k", p=128), writes=[stn])
                kb.dve(MSET(bd.ap(), 0.0), [], [bd])
                kb.dve(CP(bd.ap()[0:64, :, 0:64], stn.ap()[0:64, :, :]), [stn], [bd])
                kb.dve(CP(bd.ap()[64:128, :, 64:128], stn.ap()[64:128, :, :]), [stn], [bd])
                b = nbank()
                for c in range(4):
                    kb.transpose(ps[b], PB[b][:, c * 128:(c + 1) * 128], bd.ap()[:, c, :], identF, [bd, cst])
                kb.dve(CP(S.ap(), P3(b)), [ps[b]], [S])
                kb.free(stn, bd)
            else:
                kb.dve(MSET(S.ap(), 0.0), [], [S])
            tts = list(range(s0 // 128, (s0 + L) // 128))
            if d == 1:
                tts = tts[::-1]
            for tt in tts:
                yield from unit(T_, S, d, tt)
            if not sample:
                bo = nbank()
                for c in range(4):
                    kb.transpose(ps[bo], PB[bo][:, c * 128:(c + 1) * 128], S.ap()[:, c, :], identF, [S, cst])
                so = kb.tile("so%d" % d, [128, 4, 64])
                kb.dve(CP(so.ap()[0:64], P3(bo)[0:64, :, 0:64]), [ps[bo]], [so])
                kb.dve(CP(so.ap()[64:128], P3(bo)[64:128, :, 64:128]), [ps[bo]], [so])
                kb.dma("sp", ns_d[si, d].rearrange("(c p) k -> p c k", p=128), so.ap(), reads=[so], is_output=True)
                kb.free(so)
            kb.free(S)

        for si, (s0, L) in enumerate(seqs):
            gens = [chain(sets[0], si, s0, L, 0), chain(sets[1], si, s0, L, 1)]
            alive = list(gens)
            for _ in range(SCAN_OFFSET):
                next(gens[0])
            while alive:
                for g_ in list(alive):
                    try:
                        next(g_)
                    except StopIteration:
                        alive.remove(g_)
        for T_ in sets:
            kb.free(*T_.values())
        if stop_after == "scan":
            dump(yac.ap()[:, 0, :], [yac], 512)
            dump(yac.ap()[:, NTT - 1, :], [yac], 512)
            dump(bon.ap()[:, 0, :], [bon], 8)
            break

        yrwT = kb.tile("yrwT", [128, 4, nT], BF16)
        st1 = kb.tile("st1", [128, 8])
        st2 = kb.tile("st2", [128, 8])
        msq = kb.tile("msq", [128, 8])
        ysq = kb.tile("ysq", [128, 512])
        ygb = kb.tile("ygb", [128, 512], BF16)
        xT = kb.tile("xT", [128, 8, nT], F32, nslots=NN)
        hT = kb.tile("hT2", [128, 8, nT], BF16, nslots=NTT)
        ecx = {"i": 0}

        def emit_x(c, tt, pap, pst):
            out = xT.ap()[:, c, tt * 128:(tt + 1) * 128]
            outh = hT.ap()[:, c, tt * 128:(tt + 1) * 128]
            kb.act(out, pap, AF.Copy, [pst], [(xT, tt // 4)], scale=ALPHA_C)
            kb.act(outh, pap, AF.Identity, [pst, modT], [(hT, tt)], bias=mod("sh1", c, g), scale=mod("sc1", c, g))

        woa = kb.tile("woa", [128, 4, D], BF16)
        wor = kb.tile("wor", [128, 4, D], BF16)
        wout = kb.tile("wout", [128, 8, D], BF16)
        kb.dma("pool", woa.ap(), w_oa.rearrange("(c p) n -> p c n", p=128), writes=[woa])
        kb.dma("pool", wor.ap(), w_or.rearrange("(c p) n -> p c n", p=128), writes=[wor])
        kb.dma("pool", wout.ap(), w_out.rearrange("(c p) n -> p c n", p=128), writes=[wout])
        def phd_gen():
            for tt in range(NTT):
                o = tt * 128
                Y = yac.ap()[:, tt, :]
                Y3 = Y.rearrange("p (h e) -> p h e", h=8)
                yt = [(yac, tt)]
                kb.dve(RSUM(st1.ap(), Y3), yt, [st1])
                kb.pool(TT(ysq.ap(), Y, Y, ALU.mult), yt, [ysq])
                kb.dve(RSUM(st2.ap(), ysq.ap().rearrange("p (h e) -> p h e", h=8)), [ysq], [st2])
                kb.dve(TSM(st1.ap(), st1.ap(), 1.0 / 64.0), [st1], [st1])
                kb.dve(TT(msq.ap(), st1.ap(), st1.ap(), ALU.mult), [st1], [msq])
                kb.dve(STT(st2.ap(), st2.ap(), 1.0 / 64.0, msq.ap(), ALU.mult, ALU.subtract), [st2, msq], [st2])
                kb.dve(TSA(st2.ap(), st2.ap(), 64e-5), [st2], [st2])
                kb.act(st2.ap(), st2.ap(), AF.Sqrt, [st2], [st2])
                kb.dve(RCP(st2.ap(), st2.ap()), [st2], [st2])
                kb.dve(TT(Y3, Y3, bc(st1.ap().unsqueeze(2), [128, 8, 64]), ALU.subtract), yt + [st1], yt)
                kb.dve(TT(Y3, Y3, bc(st2.ap().unsqueeze(2), [128, 8, 64]), ALU.mult), yt + [st2], yt)
                kb.pool(TT(Y, Y, R[:, 0:512], ALU.mult), yt + [rowp], yt)
                kb.pool(TT(Y, Y, R[:, 512:1024], ALU.add), yt + [rowp], yt)
                kb.dve(TT(ysq.ap().rearrange("p (h e) -> p h e", h=8), Vt.ap()[:, tt, :].rearrange("p (h e) -> p h e", h=8),
                          bc(bon.ap()[:, tt, :].unsqueeze(2), [128, 8, 64]), ALU.mult), [(Vt, tt), (bon, tt)], [ysq])
                kb.pool(TT(Y, Y, ysq.ap(), ALU.add), yt + [ysq], yt)
                bg = nbank()
                kb.mm(ps[bg], PB[bg], sg.ap()[:, o:o + 128], gup.ap(), [sg, gup])
                kb.dve(TT(ygb.ap(), Y, PB[bg], ALU.mult), yt + [ps[bg]], [ygb])
                bt_ = nbank()
                for c in range(4):
                    kb.transpose(ps[bt_], PBb[bt_][:, c * 128:(c + 1) * 128], ygb.ap()[:, c * 128:(c + 1) * 128], identB, [ygb, cstb])
                kb.act(yrwT.ap()[:, :, o:o + 128], PBb[bt_][:, 0:512].rearrange("p (c e) -> p c e", c=4), AF.Copy, [ps[bt_]], [yrwT])
                yield

        XLD = load_x_T_gen(ps, PB, t0, nT, emit_x)
        alive = [phd_gen(), XLD]
        while alive:
            for g_ in list(alive):
                try:
                    next(g_)
                except StopIteration:
                    alive.remove(g_)
        kb.free(st1, st2, msq, ysq, ygb, rT, krT, kkT, Vt, tw, alo, sg, yac, bon)
        if stop_after == "D":
            for c_ in range(4):
                dump(yrwT.ap()[:, c_, 0:512], [yrwT], 512)
            break

        mixpre = kb.tile("mixpre", [128, 8, nT], BF16, nslots=NN)
        wg = [kb.tile("wg%d" % i, [128, 8, 1024], BF16) for i in range(2)]
        gat = [kb.tile("gat%d" % i, [128, 512]) for i in range(4)]
        G0 = 3328
        for j in range(8):
            w_ = wg[(j // 4) % 2]
            jj = j % 4
            if jj == 0:
                kb.dma("pool", w_.ap()[:, :, 0:512], win_v[:, :, G0 + j * 128:G0 + j * 128 + 512], writes=[w_])
                kb.dma("pool", w_.ap()[:, :, 512:1024], win_v[:, :, G0 + 1024 + j * 128:G0 + 1024 + j * 128 + 512], writes=[w_])
            for n in range(NN):
                ns_ = slice(n * 512, (n + 1) * 512)
                hs_ = (hT, range(4 * n, 4 * n + 4))
                b1 = nbank()
                for c in range(4):
                    kb.mm(ps[b1], PB[b1], woa.ap()[:, c, j * 128:(j + 1) * 128], oattT.ap()[:, c, ns_], [woa, oattT])
                b2 = nbank()
                for c in range(4):
                    kb.mm(ps[b2], PB[b2], wor.ap()[:, c, j * 128:(j + 1) * 128], yrwT.ap()[:, c, ns_], [wor, yrwT])
                b3 = nbank()
                for k in range(8):
                    kb.mm(ps[b3], PB[b3], w_.ap()[:, k, jj * 128:(jj + 1) * 128], hT.ap()[:, k, ns_], [w_, hs_])
                b4 = nbank()
                for k in range(8):
                    kb.mm(ps[b4], PB[b4], w_.ap()[:, k, 512 + jj * 128:512 + (jj + 1) * 128], hT.ap()[:, k, ns_], [w_, hs_])
                ga, gb_, m1, m2 = gat
                kb.act(ga.ap(), PB[b3], AF.Sigmoid, [ps[b3]], [ga])
                kb.act(gb_.ap(), PB[b4], AF.Sigmoid, [ps[b4]], [gb_])
                kb.dve(TT(m1.ap(), ga.ap(), PB[b1], ALU.mult), [ga, ps[b1]], [m1])
                kb.dve(TT(m2.ap(), gb_.ap(), PB[b2], ALU.mult), [gb_, ps[b2]], [m2])
                kb.pool(TT(mixpre.ap()[:, j, ns_], m1.ap(), m2.ap(), ALU.add), [m1, m2], [(mixpre, n)])
        kb.free(woa, wor, *wg, *gat, oattT, yrwT)
        for j2 in range(8):
            for n in range(NN):
                ns_ = slice(n * 512, (n + 1) * 512)
                b = nbank()
                for j in range(8):
                    kb.mm(ps[b], PB[b], wout.ap()[:, j, j2 * 128:(j2 + 1) * 128], mixpre.ap()[:, j, ns_], [wout, (mixpre, n)])
                kb.dve(STT(xT.ap()[:, j2, ns_], PB[b], mod("g1", j2, g), xT.ap()[:, j2, ns_], ALU.mult, ALU.add),
                       [ps[b], modT, (xT, n)], [(xT, n)])
        kb.free(wout, mixpre)

        def layer_norm_T(gname, bname):
            means = [kb.tile("lnmean%d" % i, [128, 512]) for i in range(NN)]
            vars_ = [kb.tile("lnvar%d" % i, [128, 512]) for i in range(NN)]
            sqt = [kb.tile("lnsq%d" % i, [128, 512]) for i in range(2)]
            tmp = [kb.tile("lntmp%d" % i, [128, 512]) for i in range(2)]
            for n in range(NN):
                ns_ = slice(n * 512, (n + 1) * 512)
                xs = [(xT, n)]
                mean, var = means[n], vars_[n]
                bm = nbank()
                for j in range(8):
                    kb.mm(ps[bm], PB[bm], onesd, xT.ap()[:, j, ns_], [cst] + xs)
                bq_ = nbank()
                for j in range(8):
                    sq_ = sqt[j % 2]
                    kb.act(sq_.ap(), xT.ap()[:, j, ns_], AF.Square, xs, [sq_])
                    kb.mm(ps[bq_], PB[bq_], onesd, sq_.ap(), [cst, sq_])
                kb.act(mean.ap(), PB[bm], AF.Copy, [ps[bm]], [mean])
                kb.dve(TT(var.ap(), mean.ap(), mean.ap(), ALU.mult), [mean], [var])
                kb.dve(TT(var.ap(), PB[bq_], var.ap(), ALU.subtract), [ps[bq_], var], [var])
                kb.dve(TSA(var.ap(), var.ap(), 1e-5), [var], [var])
                kb.act(var.ap(), var.ap(), AF.Sqrt, [var], [var])
                kb.dve(RCP(var.ap(), var.ap()), [var], [var])
            for n in range(NN):
                ns_ = slice(n * 512, (n + 1) * 512)
                xs = [(xT, n)]
                mean, var = means[n], vars_[n]
                for j in range(8):
                    t_ = tmp[j % 2]
                    kb.dve(TT(t_.ap(), xT.ap()[:, j, ns_], mean.ap(), ALU.subtract), xs + [mean], [t_])
                    kb.dve(TT(t_.ap(), t_.ap(), var.ap(), ALU.mult), [t_, var], [t_])
                    kb.act(xT.ap()[:, j, ns_], t_.ap(), AF.Identity, [t_, pv], xs, bias=pvc(bname, j), scale=pvc(gname, j))
            kb.free(*means, *vars_, *sqt, *tmp)

        wu = [kb.tile("wu%d" % i, [128, 8, 1024], BF16) for i in range(2)]
        wup_v = w_up.rearrange("(k p) n -> p k n", p=128)
        kb.dma("pool", wu[0].ap()[:, :, 0:512], wup_v[:, :, 0:512], writes=[wu[0]])
        kb.dma("pool", wu[0].ap()[:, :, 512:1024], wup_v[:, :, D_FF:D_FF + 512], writes=[wu[0]])
        layer_norm_T("ln1_g", "ln1_b")
        for c in range(8):
            for n in range(NN):
                ns_ = slice(n * 512, (n + 1) * 512)
                kb.act(hT.ap()[:, c, ns_], xT.ap()[:, c, ns_], AF.Identity, [(xT, n), modT], [(hT, range(4 * n, 4 * n + 4))],
                       bias=mod("sh2", c, g), scale=mod("sc2", c, g))
        for c in range(8):
            for n in range(NN):
                ns_ = slice(n * 512, (n + 1) * 512)
                kb.pool(TSM(xT.ap()[:, c, ns_], xT.ap()[:, c, ns_], ALPHA_C), [(xT, n)], [(xT, n)])
        if stop_after == "E":
            dump(xT.ap()[:, 0, 0:512], [xT], 512)
            dump(xT.ap()[:, 7, 0:512], [xT], 512)
            dump(xT.ap()[:, 3, nT - 512:nT], [xT], 512)
            dump(hT.ap()[:, 3, nT - 512:nT], [hT], 512)
            break

        fT = kb.tile("fT", [128, 22, nT], BF16, nslots=NN)
        uraw = [kb.tile("uraw%d" % i, [128, nT]) for i in range(2)]
        uacc = [kb.tile("uacc%d" % i, [128, nT]) for i in range(2)]
        wd0 = kb.tile("wd0", [128, 22, 512], BF16)
        wdn_v = w_down.rearrange("(f p) n -> p f n", p=128)
        kb.dma("pool", wd0.ap()[:, 0:11, :], wdn_v[:, 0:11, 0:512], writes=[wd0])
        kb.dma("pool", wd0.ap()[:, 11:22, :], wdn_v[:, 11:22, 0:512], writes=[wd0])
        for f in range(22):
            w_ = wu[(f // 4) % 2]
            fj = f % 4
            if fj == 0 and f > 0:
                ncol = min(512, D_FF - f * 128)
                kb.dma("pool", w_.ap()[:, :, 0:ncol], wup_v[:, :, f * 128:f * 128 + ncol], writes=[w_])
                kb.dma("pool", w_.ap()[:, :, 512:512 + ncol], wup_v[:, :, D_FF + f * 128:D_FF + f * 128 + ncol], writes=[w_])
            ur, ua = uraw[f % 2], uacc[f % 2]
            bv = []
            for n in range(NN):
                ns_ = slice(n * 512, (n + 1) * 512)
                hs_ = (hT, range(4 * n, 4 * n + 4))
                b = nbank()
                for k in range(8):
                    kb.mm(ps[b], PB[b], w_.ap()[:, k, fj * 128:(fj + 1) * 128], hT.ap()[:, k, ns_], [w_, hs_])
                kb.act(ur.ap()[:, ns_], PB[b], AF.Copy, [ps[b]], [ur])
                b = nbank()
                for k in range(8):
                    kb.mm(ps[b], PB[b], w_.ap()[:, k, 512 + fj * 128:512 + (fj + 1) * 128], hT.ap()[:, k, ns_], [w_, hs_])
                bv.append(b)
            kb.dve(TS(ua.ap(), ur.ap(), pvc("cw1", f), pvc("cb", f), ALU.mult, ALU.add), [ur, pv], [ua])
            for (s0, L) in seqs:
                kb.dve(STT(ua.ap()[:, s0 + 1:s0 + L], ur.ap()[:, s0:s0 + L - 1], pvc("cw0", f), ua.ap()[:, s0 + 1:s0 + L],
                           ALU.mult, ALU.add), [ur, pv, ua], [ua])
                kb.dve(STT(ua.ap()[:, s0:s0 + L - 1], ur.ap()[:, s0 + 1:s0 + L], pvc("cw2", f), ua.ap()[:, s0:s0 + L - 1],
                           ALU.mult, ALU.add), [ur, pv, ua], [ua])
            kb.act(ua.ap(), ua.ap(), AF.Gelu_apprx_tanh, [ua], [ua])
            for n in range(NN):
                ns_ = slice(n * 512, (n + 1) * 512)
                kb.dve(TT(fT.ap()[:, f, ns_], ua.ap()[:, ns_], PB[bv[n]], ALU.mult), [ua, ps[bv[n]]], [(fT, n)])
        if stop_after == "F1":
            dump(fT.ap()[:, 0, 0:512], [fT], 512)
            dump(fT.ap()[:, 21, nT - 512:nT], [fT], 512)
            dump(fT.ap()[:, 10, nT - 512:nT], [fT], 512)
            dump(uacc[1].ap()[:, nT - 512:nT], [uacc[1]], 512)
            break
        kb.free(*wu, *uraw, *uacc, hT)
        wd = [wd0, kb.tile("wd1", [128, 22, 512], BF16)]
        for j in range(8):
            w_ = wd[(j // 4) % 2]
            jj = j % 4
            if jj == 0 and j > 0:
                kb.dma("pool", w_.ap()[:, 0:11, :], wdn_v[:, 0:11, j * 128:j * 128 + 512], writes=[w_])
                kb.dma("pool", w_.ap()[:, 11:22, :], wdn_v[:, 11:22, j * 128:j * 128 + 512], writes=[w_])
            for n in range(NN):
                ns_ = slice(n * 512, (n + 1) * 512)
                b = nbank()
                for f in range(22):
                    kb.mm(ps[b], PB[b], w_.ap()[:, f, jj * 128:(jj + 1) * 128], fT.ap()[:, f, ns_], [w_, (fT, n)])
                kb.dve(STT(xT.ap()[:, j, ns_], PB[b], mod("g2", j, g), xT.ap()[:, j, ns_], ALU.mult, ALU.add),
                       [ps[b], modT, (xT, n)], [(xT, n)])
        kb.free(*wd, fT)
        layer_norm_T("ln2_g", "ln2_b")
        ytok = [kb.tile("ytok%d" % i, [128, D]) for i in range(2)]
        for tt in range(NTT):
            yt_ = ytok[tt % 2]
            for half in range(2):
                b = nbank()
                for cc in range(4):
                    c = half * 4 + cc
                    kb.transpose(ps[b], PB[b][:, cc * 128:(cc + 1) * 128], xT.ap()[:, c, tt * 128:(tt + 1) * 128], identF,
                                 [(xT, tt // 4), cst])
                evac_copy(yt_.ap()[:, half * 512:(half + 1) * 512], PB[b], [ps[b]], [yt_])
            kb.dma("sp", y_d[t0 + tt * 128:t0 + (tt + 1) * 128, :], yt_.ap(), reads=[yt_], is_output=True)
        kb.free(*ytok, xT)
    dbg = {}
    kb.finish()
    return nc, kb


def prep_inputs(inp):
    f = lambda a: np.ascontiguousarray(np.asarray(a, np.float32))
    shared = {}
    for k_, n_ in [("w_ada", "w_ada"), ("w_in", "w_in"), ("w_o_attn", "w_oa"), ("w_o_rwkv", "w_or"), ("w_out", "w_out"),
                   ("w_up", "w_up"), ("w_down", "w_down")]:
        shared[n_] = f(inp[k_][0])
    pvec = np.zeros((128, NPV), np.float32)

    def put(name, v):
        o, n = PV[name]
        pvec[:, o:o + n] = fm(v)

    put("b_ada", inp["b_ada"][0])
    put("mu0", inp["rw_mu"][0, 0])
    put("mu1", inp["rw_mu"][0, 1])
    put("a0_0", inp["rw_a0"][0, 0])
    put("a0_1", inp["rw_a0"][0, 1])
    put("k_k", inp["rw_k_k"][0])
    put("k_a", inp["rw_k_a"][0])
    put("ln1_g", inp["ln1_g"][0])
    put("ln1_b", inp["ln1_b"][0])
    put("ln2_g", inp["ln2_g"][0])
    put("ln2_b", inp["ln2_b"][0])
    put("cw0", inp["conv_w"][0, 0])
    put("cw1", inp["conv_w"][0, 1])
    put("cw2", inp["conv_w"][0, 2])
    put("cb", inp["conv_b"][0])
    shared["pvec"] = pvec
    rowv = np.concatenate([f(inp["rw_lnx_g"][0]), f(inp["rw_lnx_b"][0]), f(inp["da_subln_g"][0]),
                           f(inp["da_lambda"][0]).reshape(-1)])
    shared["rowp"] = np.ascontiguousarray(np.broadcast_to(rowv[None, :], (128, 1408)))
    rk = f(inp["rw_r_k"][0])
    rkblk = np.zeros((128, 4, 2), np.float32)
    for c in range(4):
        for j in range(2):
            rkblk[64 * j:64 * j + 64, c, j] = rk[2 * c + j]
    shared["rkblk"] = rkblk.reshape(128, 8)
    wupa = np.zeros((65, 2, 512), np.float32)
    wupa[0:64] = np.transpose(f(inp["rw_w_up"][0]), (1, 0, 2))
    wupa[64] = f(inp["rw_w0"][0])
    shared["wupa"] = wupa.reshape(65, 1024)
    aupt = np.zeros((128, 2, 512), np.float32)
    aupt[64:128] = np.transpose(f(inp["rw_a_up"][0]), (1, 0, 2))
    shared["aupt"] = aupt.reshape(128, 1024)
    shared["gup"] = f(inp["rw_g_up"][0])
    shared["cst"] = make_consts()
    xs = f(inp["x_sample"])
    xp = f(inp["x_prompt"])
    cctx = f(inp["c_ctx"])
    cc = f(inp["c"])
    maps = []
    for i in range(8):
        m = dict(shared)
        m["xin"] = np.concatenate([xs[i], xp[2 * i], xp[2 * i + 1]], axis=0)
        cvec = np.zeros((128, 8, 2), np.float32)
        cvec[:, :, 0] = fm(cc[i])
        cvec[:, :, 1] = fm(cctx)
        m["cvec"] = cvec.reshape(128, 16)
        m["ck"] = f(inp["cache_k"][i, 0]).reshape(256, 512)
        m["cv"] = f(inp["cache_v"][i, 0]).reshape(256, 512)
        m["st"] = f(inp["state_rwkv"][i, 0]).reshape(2, 512, 64)
        maps.append(m)
    return maps


def assemble(results):
    y_p = np.zeros((16, 256, D), np.float32)
    y_s = np.zeros((8, 1024, D), np.float32)
    nk = np.zeros((16, 1, 256, 4, 2, 64), np.float32)
    nv = np.zeros((16, 1, 256, 4, 128), np.float32)
    ns = np.zeros((16, 1, 2, 8, 64, 64), np.float32)
    for i, r in enumerate(results):
        y = r["y"]
        y_s[i] = y[0:1024]
        y_p[2 * i] = y[1024:1280]
        y_p[2 * i + 1] = y[1280:1536]
        nk[2 * i, 0] = r["nk"][0:256].reshape(256, 4, 2, 64)
        nk[2 * i + 1, 0] = r["nk"][256:512].reshape(256, 4, 2, 64)
        nv[2 * i, 0] = r["nv"][0:256].reshape(256, 4, 128)
        nv[2 * i + 1, 0] = r["nv"][256:512].reshape(256, 4, 128)
        ns[2 * i, 0] = r["ns"][0].reshape(2, 8, 64, 64)
        ns[2 * i + 1, 0] = r["ns"][1].reshape(2, 8, 64, 64)
    return y_p, y_s, nk, nv, ns


_CACHE = {}


def kernel(**inputs):
    if "nc" not in _CACHE:
        _CACHE["nc"] = build_program()[0]
    maps = prep_inputs(inputs)
    res = run_bass_kernel_spmd(_CACHE["nc"], maps, core_ids=list(range(8)))
    return assemble(res.results)
```

```python
from contextlib import ExitStack
import numpy as np
import concourse.bass as bass
import concourse.mybir as mybir
from concourse.bass_utils import run_bass_kernel_spmd

F32 = mybir.dt.float32
F32R = mybir.dt.float32r
BF16 = mybir.dt.bfloat16
AF = mybir.ActivationFunctionType
ALU = mybir.AluOpType
AX = mybir.AxisListType


ATTACH_WAIT = True
ACT_FAST_COPY = True
STRICT_SYNC = False


class Tile:
    def __init__(self, name, handle, nslots=1):
        self.name = name
        self.h = handle
        self.nslots = nslots
        self.w = [None] * nslots
        self.r = [dict() for _ in range(nslots)]

    def ap(self):
        return self.h


def _norm(items):
    out = []
    for it in items:
        if isinstance(it, Tile):
            for s in range(it.nslots):
                out.append((it, s))
        else:
            t, sl = it
            if isinstance(sl, int):
                out.append((t, sl))
            else:
                for s in sl:
                    out.append((t, s))
    return out


class KB:
    ENG = ("pe", "act", "dve", "pool", "sp")
    NDS = 8

    def __init__(self, nc):
        self.nc = nc
        self.stack = ExitStack()
        self.ops = {e: [] for e in self.ENG}
        self.waited = {e: {} for e in self.ENG}
        self.sems = {e: self.stack.enter_context(nc.semaphore("s_" + e)) for e in self.ENG}
        self.dsem = {}
        self.dval = {}
        self.dnext = {}
        for q in ("sp", "act", "pool"):
            self.dsem[q] = [self.stack.enter_context(nc.semaphore("d_%s%d" % (q, i))) for i in range(self.NDS)]
            self.dval[q] = [0] * self.NDS
            self.dnext[q] = 0
        self.out_events = []
        self.ntiles = 0
        self.vcs = {}

    ARENA_COLS = 53000

    def _init_arena(self):
        self.arena = self.stack.enter_context(self.nc.sbuf_tensor("arena", [128, self.ARENA_COLS], F32))
        self.free_list = [(0, self.ARENA_COLS)]
        self.grave = []
        self.peak = 0

    def tile(self, name, shape, dtype=F32, nslots=1):
        if not hasattr(self, "arena"):
            self._init_arena()
        shape = list(shape)
        n = 1
        for d in shape[1:]:
            n *= d
        esz = 2 if dtype == BF16 else 4
        cols = (n * esz + 3) // 4
        for i, (lo, hi) in enumerate(self.free_list):
            if hi - lo >= cols:
                break
        else:
            raise RuntimeError("arena full allocating %s (%d cols); free=%s" % (name, cols, self.free_list))
        self.free_list[i:i + 1] = [(lo + cols, hi)] if hi - lo > cols else []
        self.peak = max(self.peak, lo + cols)
        v = self.arena[0:shape[0], lo:lo + cols]
        if dtype != F32:
            v = v.bitcast(dtype)
        if esz == 2 and (n % 2):
            v = v[:, 0:n]
        if len(shape) == 3:
            v = v.rearrange("p (a b) -> p a b", a=shape[1])
        elif len(shape) == 4:
            v = v.rearrange("p (a b c) -> p a b c", a=shape[1], b=shape[2])
        t = Tile(name, v, nslots)
        t.rng = (lo, lo + cols)
        keep = []
        for (glo, ghi, evs) in self.grave:
            if glo < lo + cols and lo < ghi:
                for s in range(nslots):
                    for k, ev in evs.items():
                        old = t.r[s].get(k)
                        if old is None or old[-1] < ev[-1]:
                            t.r[s][k] = ev
            keep.append((glo, ghi, evs))
        self.grave = keep
        return t

    def free(self, *tiles):
        for t in tiles:
            evs = {}
            for s in range(t.nslots):
                cand = list(t.r[s].items())
                if t.w[s] is not None:
                    ev = t.w[s]
                    k = (ev[0], ev[1]) if ev[0] == "e" else (ev[0], ev[1], ev[2])
                    cand.append((k, ev))
                for k, ev in cand:
                    old = evs.get(k)
                    if old is None or old[-1] < ev[-1]:
                        evs[k] = ev
            lo, hi = t.rng
            self.grave.append((lo, hi, evs))
            self.free_list.append((lo, hi))
            self.free_list.sort()
            merged = []
            for (a, b) in self.free_list:
                if merged and merged[-1][1] == a:
                    merged[-1] = (merged[-1][0], b)
                else:
                    merged.append((a, b))
            self.free_list = merged

    def psum_banks(self, n=8):
        out = [Tile("ps%d" % i, self.stack.enter_context(self.nc.psum_tensor("psb%d" % i, [128, 512], F32))[:])
               for i in range(n)]
        for t in out:
            t.excl = True
        return out

    def _need(self, eng, ev, waits):
        if ev is None:
            return
        if ev[0] == "e":
            key = ("e", ev[1])
            val = ev[2]
        else:
            key = ("d", ev[1], ev[2])
            val = ev[3]
        if self.waited[eng].get(key, -1) >= val:
            return
        self.waited[eng][key] = val
        waits[key] = ev
        vc = self.vcs.get(ev)
        if vc:
            w = self.waited[eng]
            for k2, v2 in vc.items():
                if w.get(k2, -1) < v2:
                    w[k2] = v2

    def _record(self, eng, fn, reads, writes, dma_q=None, is_output=False):
        reads = _norm(reads)
        writes = _norm(writes)
        xr = [(t, s_) for (t, s_) in reads if getattr(t, "excl", False)]
        if xr:
            reads = [(t, s_) for (t, s_) in reads if not getattr(t, "excl", False)]
            writes = writes + [x for x in xr if x not in writes]
        waits = {}
        for (t, s) in reads:
            ev = t.w[s]
            if ev is not None:
                if ev[0] == "e" and ev[1] == eng and dma_q is None and eng == "pe":
                    pass
                else:
                    self._need(eng, ev, waits)
        strict = STRICT_SYNC and eng != "pe"
        for (t, s) in writes:
            ev = t.w[s]
            if ev is not None and (strict or not (ev[0] == "e" and ev[1] == eng and dma_q is None)):
                self._need(eng, ev, waits)
            for ev in t.r[s].values():
                if strict or not (ev[0] == "e" and ev[1] == eng and dma_q is None):
                    self._need(eng, ev, waits)
        if dma_q is not None:
            q = dma_q
            k = self.dnext[q]
            self.dnext[q] = (k + 1) % self.NDS
            if self.dval[q][k] > 0:
                self._need(eng, ("d", q, k, self.dval[q][k]), waits)
            self.dval[q][k] += 16
            ev_out = ("d", q, k, self.dval[q][k])
            rec = dict(waits=list(waits.values()), fn=fn, marked=False, dma=(q, k))
            if is_output:
                self.out_events.append(ev_out)
        else:
            ev_out = ("e", eng, len(self.ops[eng]))
            rec = dict(waits=list(waits.values()), fn=fn, marked=False, dma=None)
        self.ops[eng].append(rec)
        snap = dict(self.waited[eng])
        if ev_out[0] == "e":
            if snap.get(("e", eng), -1) < ev_out[2] - 1:
                pass
        self.vcs[ev_out] = snap
        for (t, s) in writes:
            t.w[s] = ev_out
            t.r[s] = {}
        for (t, s) in reads:
            key = (ev_out[0], ev_out[1]) if ev_out[0] == "e" else (ev_out[0], ev_out[1], ev_out[2])
            t.r[s][key] = ev_out
        return ev_out

    def fresh(self, *pss):
        for p in pss:
            p.fresh = True

    def mm(self, ps, out, lhsT, rhs, reads, start=None, stop=None):
        st = bool(getattr(ps, "fresh", True))
        ps.fresh = False
        return self._record("pe", lambda e: e.matmul(out, lhsT, rhs, start=st, stop=True, skip_group_check=True),
                            reads, [ps])

    def transpose(self, ps, out, in_, ident, reads):
        return self._record("pe", lambda e: e.transpose(out, in_, ident), reads, [ps])

    def act(self, out, in_, func, reads, writes, bias=None, scale=None, accum_out=None):
        kw = {}
        if bias is not None:
            kw["bias"] = bias
        if scale is not None:
            kw["scale"] = scale
        if accum_out is not None:
            kw["accum_out"] = accum_out
        if ACT_FAST_COPY and func == AF.Copy and bias is None and accum_out is None:
            if scale is None:
                return self._record("act", lambda e: e.copy(out, in_), reads, writes)
            if isinstance(scale, float):
                return self._record("act", lambda e: e.mul(out, in_, scale), reads, writes)
        return self._record("act", lambda e: e.activation(out, in_, func, **kw), reads, writes)

    def dve(self, fn, reads, writes):
        return self._record("dve", fn, reads, writes)

    def pool(self, fn, reads, writes):
        return self._record("dve", fn, reads, writes)

    def gp(self, fn, reads, writes):
        return self._record("pool", fn, reads, writes)

    def any(self, eng, fn, reads, writes):
        return self._record(eng, fn, reads, writes)

    def dve_copy(self, out, in_, reads, writes):
        return self._record("dve", lambda e: e.tensor_copy(out, in_), reads, writes)

    def dma(self, q, out, in_, reads=(), writes=(), is_output=False, **kw):
        return self._record(q, lambda e: e.dma_start(out=out, in_=in_, **kw), reads, writes, dma_q=q,
                            is_output=is_output)

    def barrier(self):
        for eng in self.ENG:
            waits = {}
            for other in self.ENG:
                if other != eng and self.ops[other]:
                    idx = None
                    for i in range(len(self.ops[other]) - 1, -1, -1):
                        if self.ops[other][i]["dma"] is None and self.ops[other][i]["fn"] is not None:
                            idx = i
                            break
                    if idx is not None:
                        self._need(eng, ("e", other, idx), waits)
            for q in self.dsem:
                for k in range(self.NDS):
                    if self.dval[q][k] > 0:
                        self._need(eng, ("d", q, k, self.dval[q][k]), waits)
            if waits:
                self.ops[eng].append(dict(waits=list(waits.values()), fn=None, marked=False, dma=None))

    def finish(self):
        waits = {}
        for ev in self.out_events:
            self._need("sp", ev, waits)
        self.ops["sp"].append(dict(waits=list(waits.values()), fn=None, marked=False, dma=None))
        for eng in self.ENG:
            for rec in self.ops[eng]:
                for ev in rec["waits"]:
                    if ev[0] == "e":
                        self.ops[ev[1]][ev[2]]["marked"] = True
        val = {}
        for eng in self.ENG:
            c = 0
            for i, rec in enumerate(self.ops[eng]):
                if rec["marked"]:
                    c += 1
                    val[(eng, i)] = c
        self.nmarked = {e: sum(1 for r in self.ops[e] if r["marked"]) for e in self.ENG}
        kb = self

        def replay(engname):
            def run(e):
                for i, rec in enumerate(kb.ops[engname]):
                    ws = [(kb.sems[ev[1]], val[(ev[1], ev[2])]) if ev[0] == "e" else (kb.dsem[ev[1]][ev[2]], ev[3])
                          for ev in rec["waits"]]
                    attach = ws.pop() if (ATTACH_WAIT and ws and rec["fn"] is not None) else None
                    for (sm, vl) in ws:
                        e.wait_ge(sm, vl)
                    if rec["fn"] is None:
                        continue
                    ins = rec["fn"](e)
                    if attach is not None:
                        ins._wait_ge(attach[0], attach[1])
                    if rec["dma"] is not None:
                        q, k = rec["dma"]
                        ins.then_inc(kb.dsem[q][k], 16)
                    elif rec["marked"]:
                        ins.then_inc(kb.sems[engname], 1)
            return run

        with self.nc.Block() as block:
            block.tensor(replay("pe"))
            block.scalar(replay("act"))
            block.vector(replay("dve"))
            block.gpsimd(replay("pool"))
            block.sync(replay("sp"))
        self.stack.close()


D = 1024
NTOK = 1536
N_IN = 5376
D_FF = 2816
ALPHA_C = 2.0 ** 0.25
LAM_INIT = 0.8 - 0.6 * 1.0
CL = -float(np.exp(-0.5))
SCAN_F32 = True
FINE_YIELD = False
COARSE = False
SCAN_OFFSET = 0
NEU_BF16 = True
PASSES = [
    dict(t0=0, nT=1024, seqs=[(0, 1024)], g=0, sample=True),
    dict(t0=1024, nT=512, seqs=[(0, 256), (256, 256)], g=1, sample=False),
]
PV = {}
_o = 0
for _n, _c in [("b_ada", 48), ("mu0", 14), ("mu1", 14), ("a0_0", 4), ("a0_1", 4), ("k_k", 4), ("k_a", 4),
               ("ln1_g", 8), ("ln1_b", 8), ("ln2_g", 8), ("ln2_b", 8), ("cw0", 22), ("cw1", 22), ("cw2", 22),
               ("cb", 22)]:
    PV[_n] = (_o, _c)
    _o += _c
NPV = _o
CST = dict(ident=0, sw=128, blk=256, onesd=384, ti_f=512, te_f=640, ti_b=768, te_b=896,
           lt=1024, le=1152, gt=1280, ge=1408, cos=1536, sin=2560)
NCST = 3584


def make_consts():
    c = np.zeros((128, NCST), np.float32)
    idx = np.arange(128)
    c[:, 0:128] = np.eye(128, dtype=np.float32)
    sw = np.where((idx % 64) < 32, idx + 32, idx - 32)
    c[sw, 128 + idx] = 1.0
    c[:, 256:384] = (idx[:, None] // 64 == idx[None, :] // 64)
    c[:, 384:512] = 1.0 / 1024.0
    i = idx[:, None]
    t = idx[None, :]
    m = 63
    ti_f = ((i > m) & (i <= t)).astype(np.float32) - ((i > t) & (i <= m)).astype(np.float32)
    mb = 64
    ti_b = ((i >= t) & (i < mb)).astype(np.float32) - ((i >= mb) & (i < t)).astype(np.float32)
    eye = np.eye(128, dtype=np.float32)
    c[:, 512:640] = ti_f
    c[:, 640:768] = ti_f - eye
    c[:, 768:896] = ti_b
    c[:, 896:1024] = ti_b - eye
    c[:, 1024:1152] = (i < t)
    c[:, 1152:1280] = (i <= t)
    c[:, 1280:1408] = (i > t)
    c[:, 1408:1536] = (i >= t)
    pos = np.arange(1024)
    row = (pos // 64).astype(np.float32)
    col = (pos % 64).astype(np.float32)
    inv = (10000.0 ** (-np.arange(16, dtype=np.float32) / 16)).astype(np.float32)
    ang = np.concatenate([row[:, None] * inv, col[:, None] * inv], -1).astype(np.float32)
    cos = np.cos(ang).astype(np.float32).T
    sin = np.sin(ang).astype(np.float32).T
    pi = idx % 32
    sgn = np.where((idx % 64) < 32, -1.0, 1.0).astype(np.float32)
    c[:, 1536:2560] = cos[pi, :]
    c[:, 2560:3584] = sin[pi, :] * sgn[:, None]
    return c


def fm(v):
    v = np.asarray(v, np.float32).reshape(-1, 128)
    return np.ascontiguousarray(v.T)


def build_program(stop_after=None, passes=(0, 1), scan_steps=99):
    nc = bass.Bass("TRN2", target_bir_lowering=False)

    def din(name, shape):
        return nc.dram_tensor(name, list(shape), F32, kind="ExternalInput").ap()

    def dout(name, shape):
        return nc.dram_tensor(name, list(shape), F32, kind="ExternalOutput").ap()

    xin = din("xin", [NTOK, D])
    cvec_d = din("cvec", [128, 16])
    w_ada = din("w_ada", [D, 6 * D])
    w_in = din("w_in", [D, N_IN])
    w_oa = din("w_oa", [512, D])
    w_or = din("w_or", [512, D])
    w_out = din("w_out", [D, D])
    w_up = din("w_up", [D, 2 * D_FF])
    w_down = din("w_down", [D_FF, D])
    pvec_d = din("pvec", [128, NPV])
    rowp_d = din("rowp", [128, 1408])
    rkblk_d = din("rkblk", [128, 8])
    wupa_d = din("wupa", [65, 1024])
    aupt_d = din("aupt", [128, 1024])
    gup_d = din("gup", [128, 512])
    cst_d = din("cst", [128, NCST])
    ck_d = din("ck", [256, 512])
    cv_d = din("cv", [256, 512])
    st_d = din("st", [2, 512, 64])
    y_d = dout("y", [NTOK, D])
    nk_d = dout("nk", [512, 512])
    nv_d = dout("nv", [512, 512])
    ns_d = dout("ns", [2, 2, 512, 64])
    dbg_d = dout("dbg", [128, 8192]) if stop_after is not None else None

    kb = KB(nc)
    ps = kb.psum_banks()
    PB = [p.ap() for p in ps]
    PBb = [p.ap().bitcast(BF16) for p in ps]

    rr = {"ew": 0}

    def ew():
        rr["ew"] ^= 1
        return "dve" if rr["ew"] else "pool"

    def TS(out, in0, s1, s2, op0, op1):
        return lambda e: e.tensor_scalar(out, in0, s1, s2, op0, op1)

    def TT(out, a, b, op):
        return lambda e: e.tensor_tensor(out, a, b, op)

    def STT(out, in0, sc, in1, op0, op1):
        return lambda e: e.scalar_tensor_tensor(out, in0, sc, in1, op0, op1)

    def CP(out, in_):
        return lambda e: e.tensor_copy(out, in_)

    def bc(ap, shape):
        return ap.to_broadcast(list(shape))

    cst = kb.tile("cst", [128, 1536])
    cstb = kb.tile("cstb", [128, 1024], BF16)
    pv = kb.tile("pv", [128, NPV + 36])
    rowp = kb.tile("rowp", [128, 1408])
    rkblk = kb.tile("rkblk", [128, 8], BF16)
    wupa = kb.tile("wupa", [65, 1024], BF16)
    aupt = kb.tile("aupt", [128, 1024], BF16)
    gup = kb.tile("gup", [128, 512], BF16)
    modT = kb.tile("modT", [128, 96])
    lam = kb.tile("lam", [128, 4])

    kb.dma("sp", cst.ap(), cst_d[:, 0:1536], writes=[cst])
    kb.dma("sp", pv.ap()[:, 0:NPV], pvec_d, writes=[pv])
    kb.dma("sp", rowp.ap(), rowp_d, writes=[rowp])
    kb.dma("pool", rkblk.ap(), rkblk_d, writes=[rkblk])
    kb.dma("pool", wupa.ap(), wupa_d, writes=[wupa])
    kb.dma("pool", aupt.ap(), aupt_d, writes=[aupt])
    kb.dma("pool", gup.ap(), gup_d, writes=[gup])
    C = cst.ap()
    Cb = cstb.ap()
    kb.dve(CP(Cb[:, 0:384], C[:, 0:384]), [cst], [cstb])
    kb.dve(CP(Cb[:, 384:896], C[:, 512:1024]), [cst], [cstb])
    identF = C[:, 0:128]
    identB = Cb[:, 0:128]
    swB = Cb[:, 128:256]
    blkB = Cb[:, 256:384]
    onesd = C[:, 384:512]
    TIB = {0: Cb[:, 384:512], 1: Cb[:, 640:768]}
    TEB = {0: Cb[:, 512:640], 1: Cb[:, 768:896]}
    MASK = {k: C[:, CST[k]:CST[k] + 128] for k in ("lt", "le", "gt", "ge")}
    P = pv.ap()

    def pvc(name, j=None):
        o, n = PV[name]
        return P[:, o:o + n] if j is None else P[:, o + j:o + j + 1]

    c0o = NPV
    omkao = NPV + 14
    kb.dve(TT(P[:, c0o:c0o + 14], pvc("mu0"), pvc("mu1"), ALU.add), [pv], [pv])
    kb.dve(TS(P[:, c0o:c0o + 14], P[:, c0o:c0o + 14], -1.0, 1.0, ALU.mult, ALU.add), [pv], [pv])
    kb.dve(TS(P[:, omkao:omkao + 4], pvc("k_a"), -1.0, 1.0, ALU.mult, ALU.add), [pv], [pv])
    lt_ = kb.tile("lamtmp", [128, 128])
    R = rowp.ap()
    kb.dve(TT(lt_.ap()[:, 0:64], R[:, 1152:1216], R[:, 1216:1280], ALU.mult), [rowp], [lt_])
    kb.dve(TT(lt_.ap()[:, 64:128], R[:, 1280:1344], R[:, 1344:1408], ALU.mult), [rowp], [lt_])
    kb.dve(lambda e: e.tensor_reduce(lam.ap()[:, 2:4], lt_.ap().rearrange("p (a b) -> p a b", a=2), AX.X, ALU.add),
           [lt_], [lam])
    kb.act(lam.ap()[:, 2:4], lam.ap()[:, 2:4], AF.Exp, [lam], [lam])
    kb.dve(TT(lam.ap()[:, 0:1], lam.ap()[:, 2:3], lam.ap()[:, 3:4], ALU.subtract), [lam], [lam])
    kb.dve(TS(lam.ap()[:, 0:1], lam.ap()[:, 0:1], LAM_INIT, None, ALU.add, ALU.bypass) if False else
           (lambda e: e.tensor_scalar_add(lam.ap()[:, 0:1], lam.ap()[:, 0:1], LAM_INIT)), [lam], [lam])
    kb.dve(lambda e: e.tensor_scalar_mul(lam.ap()[:, 1:2], lam.ap()[:, 0:1], -1.0), [lam], [lam])
    kb.dve(lambda e: e.tensor_scalar_mul(R[:, 1024:1152], R[:, 1024:1152], 1.0 - LAM_INIT), [rowp], [rowp])
    kb.free(lt_)

    def phase0():
        cv32 = kb.tile("cv32", [128, 16])
        scb = kb.tile("scb", [128, 16], BF16)
        kb.dma("sp", cv32.ap(), cvec_d, writes=[cv32])
        kb.act(scb.ap(), cv32.ap(), AF.Silu, [cv32], [scb])
        wab = [kb.tile("wa%d" % i, [128, 8, 512], BF16) for i in range(2)]
        wada_v = w_ada.rearrange("(k p) n -> p k n", p=128)
        kb.fresh(ps[0])
        for cg in range(12):
            wt = wab[cg % 2]
            kb.dma("pool", wt.ap(), wada_v[:, :, cg * 512:(cg + 1) * 512], writes=[wt])
            for j in range(4):
                jj = cg * 4 + j
                for k in range(8):
                    kb.mm(ps[0], PB[0][:, 2 * jj:2 * jj + 2], wt.ap()[:, k, j * 128:(j + 1) * 128],
                          scb.ap()[:, 2 * k:2 * k + 2], [wt, scb], start=(k == 0), stop=(k == 7))
        kb.dve(TT(modT.ap().rearrange("p (j g) -> p j g", g=2), PB[0][:, 0:96].rearrange("p (j g) -> p j g", g=2),
                  bc(pvc("b_ada").unsqueeze(2), [128, 48, 2]), ALU.add), [ps[0], pv], [modT])
        kb.dve(lambda e: e.tensor_scalar_add(modT.ap()[:, 16:32], modT.ap()[:, 16:32], 1.0), [modT], [modT])
        kb.dve(lambda e: e.tensor_scalar_add(modT.ap()[:, 64:80], modT.ap()[:, 64:80], 1.0), [modT], [modT])
        kb.free(cv32, scb, *wab)

    M = modT.ap()

    def mod(kind, j, g):
        base = dict(sh1=0, sc1=8, g1=16, sh2=24, sc2=32, g2=40)[kind]
        col = (base + j) * 2 + g
        return M[:, col:col + 1]

    xin_v = xin
    dbgpos = {"o": 0}

    def dump(ap2d, reads, n):
        t = kb.tile("dbgt%d" % dbgpos["o"], [128, n])
        p = ap2d.shape[0]
        kb.dve(CP(t.ap()[0:p, :], ap2d), reads, [t])
        kb.dma("sp", dbg_d[0:p, dbgpos["o"]:dbgpos["o"] + n], t.ap()[0:p, :], reads=[t], is_output=True)
        print("dump at", dbgpos["o"], n)
        dbgpos["o"] += n

    def load_x_T_gen(ps_, pb, t0, nT, emit):
        xt = [kb.tile("xtok%d" % i, [128, D]) for i in range(2)]
        for tt in range(nT // 128):
            x_ = xt[tt % 2]
            kb.dma("sp", x_.ap(), xin_v[t0 + tt * 128:t0 + (tt + 1) * 128, :], writes=[x_])
            for half in range(2):
                b = (2 * tt + half) % 4
                for cc in range(4):
                    c = half * 4 + cc
                    kb.transpose(ps_[b], pb[b][:, cc * 128:(cc + 1) * 128], x_.ap()[:, c * 128:(c + 1) * 128], identF,
                                 [x_, cst])
                for cc in range(4):
                    c = half * 4 + cc
                    emit(c, tt, pb[b][:, cc * 128:(cc + 1) * 128], ps_[b])
            yield
        kb.free(*xt)

    def load_x_T(ps_, pb, t0, nT, emit):
        for _ in load_x_T_gen(ps_, pb, t0, nT, emit):
            pass

    win_v = w_in.rearrange("(k p) n -> p k n", p=128)
    mod_done = [False]

    def TSM(out, in_, sc):
        return lambda e: e.tensor_scalar_mul(out, in_, sc)

    def TSA(out, in_, sc):
        return lambda e: e.tensor_scalar_add(out, in_, sc)

    def RCP(out, in_):
        return lambda e: e.reciprocal(out, in_)

    def RSUM(out, in_):
        return lambda e: e.tensor_reduce(out, in_, AX.X, ALU.add)

    def MSET(out, v):
        return lambda e: e.memset(out, v)

    ecnt = {"i": 0}

    def evac_copy(out, in_, reads, writes):
        ecnt["i"] += 1
        if ecnt["i"] % 2:
            kb.act(out, in_, AF.Copy, reads, writes)
        else:
            kb.dve(CP(out, in_), reads, writes)


    for pi in passes:
        PS = PASSES[pi]
        t0, nT, seqs, g, sample = PS["t0"], PS["nT"], PS["seqs"], PS["g"], PS["sample"]
        NTT = nT // 128
        NN = nT // 512
        oattT = kb.tile("oattT", [128, 4, nT], BF16)
        hT = kb.tile("hT", [128, 8, nT], BF16, nslots=NTT)
        rT = kb.tile("rT", [128, 4, nT], BF16)
        krT = kb.tile("krT", [128, 4, nT], BF16)
        kkT = kb.tile("kkT", [128, 4, nT])
        SDT = F32 if SCAN_F32 else BF16
        Vt = kb.tile("Vt", [128, NTT, 512], BF16, nslots=NTT)
        tw = kb.tile("tw", [65, nT], BF16)
        alo = kb.tile("alo", [128, nT], BF16)
        sg = kb.tile("sg", [128, nT], BF16)
        yac = kb.tile("yac", [128, NTT, 512], F32, nslots=NTT)
        bon = kb.tile("bon", [128, NTT, 8], F32, nslots=NTT)

        cnt = {"i": 0}
        if not mod_done[0]:
            xT0 = kb.tile("xT0", [128, 8, nT], F32, nslots=NTT)

            def emit_raw(c, tt, pap, pst):
                evac_copy(xT0.ap()[:, c, tt * 128:(tt + 1) * 128], pap, [pst], [(xT0, tt)])

            load_x_T(ps, PB, t0, nT, emit_raw)
            phase0()
            mod_done[0] = True
            for c in range(8):
                for n in range(NN):
                    ns_ = slice(n * 512, (n + 1) * 512)
                    rd = [(xT0, range(4 * n, 4 * n + 4)), modT]
                    wr_ = [(hT, range(4 * n, 4 * n + 4))]
                    cnt["i"] += 1
                    if cnt["i"] % 2:
                        kb.act(hT.ap()[:, c, ns_], xT0.ap()[:, c, ns_], AF.Identity, rd, wr_, bias=mod("sh1", c, g), scale=mod("sc1", c, g))
                    else:
                        kb.dve(TS(hT.ap()[:, c, ns_], xT0.ap()[:, c, ns_], mod("sc1", c, g), mod("sh1", c, g), ALU.mult, ALU.add), rd, wr_)
            kb.free(xT0)
        else:
            def emit_h(c, tt, pap, pst):
                out = hT.ap()[:, c, tt * 128:(tt + 1) * 128]
                cnt["i"] += 1
                if cnt["i"] % 2:
                    kb.act(out, pap, AF.Identity, [pst, modT], [(hT, tt)], bias=mod("sh1", c, g), scale=mod("sc1", c, g))
                else:
                    kb.dve(TS(out, pap, mod("sc1", c, g), mod("sh1", c, g), ALU.mult, ALU.add), [pst, modT], [(hT, tt)])

            load_x_T(ps, PB, t0, nT, emit_h)
        if stop_after == "A":
            dump(modT.ap(), [modT], 96)
            dump(hT.ap()[:, 0, :], [hT], nT)
            dump(hT.ap()[:, 7, :], [hT], nT)
            break

        ropet = stg = None
        if sample:
            ropet = kb.tile("ropet", [128, 2048])
            stg = kb.tile("stg", [128, 1024])
            kb.dma("act", ropet.ap(), cst_d[:, 1536:3584], writes=[ropet])
            cosT = ropet.ap()[:, 0:1024]
            sinT = ropet.ap()[:, 1024:2048]
        qT = kb.tile("qT", [128, 4, nT], BF16)
        kT = kb.tile("kT", [128, 4, nT], BF16)
        NKT = NTT + (2 if sample else 0)
        v1 = kb.tile("v1", [128, NKT, 4, 130], BF16, nslots=NKT)
        kb.pool(MSET(v1.ap()[:, :, :, 128:130], 1.0), [], [v1])
        wq = [kb.tile("wq%d" % i, [128, 8, 512], BF16) for i in range(2)]
        qraw = [kb.tile("qraw%d" % i, [128, 512], BF16) for i in range(2)]
        rt1 = [kb.tile("rt1_%d" % i, [128, 512]) for i in range(2)]
        rt2 = [kb.tile("rt2_%d" % i, [128, 512]) for i in range(2)]
        kf32 = None
        if not sample:
            kf32 = kb.tile("kf32", [128, 4, nT])
        it = 0
        for wi, (dst, col0, scl) in enumerate([(qT, 0, 0.125), (kT, 512, 1.0)]):
            wt = wq[wi % 2]
            kb.dma("pool", wt.ap(), win_v[:, :, col0:col0 + 512], writes=[wt])
            for c in range(4):
                for n in range(NN):
                    b = it % 4
                    kb.fresh(ps[b])
                    for k in range(8):
                        kb.mm(ps[b], PB[b], wt.ap()[:, k, c * 128:(c + 1) * 128], hT.ap()[:, k, n * 512:(n + 1) * 512],
                              [wt, (hT, range(4 * n, 4 * n + 4))], start=(k == 0), stop=(k == 7))
                    outap = dst.ap()[:, c, n * 512:(n + 1) * 512]
                    if sample:
                        qr = qraw[it % 2]
                        kb.act(qr.ap(), PB[b], AF.Copy, [ps[b]], [qr], scale=scl)
                        b2 = 4 + it % 2
                        kb.fresh(ps[b2])
                        kb.mm(ps[b2], PB[b2], swB, qr.ap(), [cstb, qr])
                        a1, a2 = rt1[it % 2], rt2[it % 2]
                        kb.pool(TT(a1.ap(), qr.ap(), cosT[:, n * 512:(n + 1) * 512], ALU.mult), [qr, ropet], [a1])
                        kb.dve(TT(a2.ap(), PB[b2], sinT[:, n * 512:(n + 1) * 512], ALU.mult), [ps[b2], ropet], [a2])
                        kb.dve(TT(outap, a1.ap(), a2.ap(), ALU.add), [a1, a2], [dst])
                    else:
                        kb.act(outap, PB[b], AF.Copy, [ps[b]], [dst], scale=scl)
                        if dst is kT:
                            kb.dve(CP(kf32.ap()[:, c, n * 512:(n + 1) * 512], PB[b]), [ps[b]], [kf32])
                    it += 1
        wt = wq[0]
        kb.dma("pool", wt.ap(), win_v[:, :, 1024:1536], writes=[wt])
        vout = None
        if not sample:
            vout = [kb.tile("vout%d" % i, [128, 512]) for i in range(2)]
        for tt in range(NTT):
            b = tt % 4
            kb.fresh(ps[b])
            for k in range(8):
                kb.mm(ps[b], PB[b], hT.ap()[:, k, tt * 128:(tt + 1) * 128], wt.ap()[:, k, :], [wt, (hT, tt)],
                      start=(k == 0), stop=(k == 7))
            kb.act(v1.ap()[:, tt, :, 0:128], PB[b].rearrange("p (h e) -> p h e", h=4), AF.Copy, [ps[b]], [(v1, tt)])
            if not sample:
                vo = vout[tt % 2]
                kb.dve(CP(vo.ap(), PB[b]), [ps[b]], [vo])
                kb.dma("sp", nv_d[tt * 128:(tt + 1) * 128, :], vo.ap(), reads=[vo], is_output=True)
        if not sample:
            for tt in range(NTT):
                b = 4 + tt % 2
                for c in range(4):
                    kb.transpose(ps[b], PB[b][:, c * 128:(c + 1) * 128], kf32.ap()[:, c, tt * 128:(tt + 1) * 128], identF,
                                 [kf32, cst])
                vo = vout[tt % 2]
                kb.dve(CP(vo.ap(), PB[b]), [ps[b]], [vo])
                kb.dma("sp", nk_d[tt * 128:(tt + 1) * 128, :], vo.ap(), reads=[vo], is_output=True)
            kb.free(kf32, *vout)
        kcT = None
        if sample:
            kcT = kb.tile("kcT", [128, 4, 256], BF16)
            for tl in range(2):
                kb.dma("sp", stg.ap()[:, 0:512], ck_d[tl * 128:(tl + 1) * 128, :], writes=[stg])
                kb.dma("act", stg.ap()[:, 512:1024], cv_d[tl * 128:(tl + 1) * 128, :], writes=[stg])
                b = 4 + tl
                for h in range(4):
                    kb.transpose(ps[b], PB[b][:, h * 128:(h + 1) * 128], stg.ap()[:, h * 128:(h + 1) * 128], identF,
                                 [stg, cst])
                kb.act(kcT.ap()[:, :, tl * 128:(tl + 1) * 128], PB[b].rearrange("p (h e) -> p h e", h=4), AF.Copy,
                       [ps[b]], [kcT])
                kb.dve(CP(v1.ap()[:, NTT + tl, :, 0:128], stg.ap()[:, 512:1024].rearrange("p (h e) -> p h e", h=4)),
                       [stg], [(v1, NTT + tl)])
        kb.free(*wq, *qraw, *rt1, *rt2)
        if sample:
            kb.free(ropet, stg)
        if stop_after == "Bproj":
            break

        otok = kb.tile("otok", [128, NTT, 512], F32, nslots=NTT)
        accs = kb.tile("accs", [128, 8, 130])
        rec = kb.tile("rec", [128, 8])
        exb = [kb.tile("exb%d" % i, [128, 512], BF16) for i in range(3)]
        tmp1 = kb.tile("atmp1", [128, 4, 128])
        ei_box = [0]
        si_box = [0]
        def attn_gen():
            for (s0, L) in seqs:
                QB = 512 if L >= 512 else L
                nqs = QB // 128
                ktiles = [("own", s0 // 128 + i) for i in range(L // 128)]
                if sample:
                    ktiles += [("cache", 0), ("cache", 1)]
                for h in range(4):
                    for qb in range(L // QB):
                        q0 = s0 + qb * QB
                        kb.fresh(ps[4], ps[5], ps[6])
                        def qk(ki, kind, kt, m):
                            sb = si_box[0] % 4
                            si_box[0] += 1
                            if kind == "own":
                                kl = kT.ap()[64 * m:64 * m + 64, h, kt * 128:(kt + 1) * 128]
                                kr_ = [kT]
                            else:
                                kl = kcT.ap()[64 * m:64 * m + 64, h, kt * 128:(kt + 1) * 128]
                                kr_ = [kcT]
                            kb.fresh(ps[sb])
                            kb.mm(ps[sb], PB[sb][:, 0:QB], kl, qT.ap()[64 * m:64 * m + 64, h, q0:q0 + QB], kr_ + [qT])
                            return sb

                        def qk2(i):
                            kind, kt = ktiles[i]
                            return [qk(i, kind, kt, 0), qk(i, kind, kt, 1)]

                        pend = [qk2(0)]
                        for ki, (kind, kt) in enumerate(ktiles):
                            if ki + 1 < len(ktiles):
                                pend.append(qk2(ki + 1))
                            sbs = pend.pop(0)
                            vt = kt if kind == "own" else NTT + kt
                            for m in range(2):
                                sb = sbs[m]
                                ex = exb[ei_box[0] % 3]
                                ei_box[0] += 1
                                kb.act(ex.ap()[:, 0:QB], PB[sb][:, 0:QB], AF.Exp, [ps[sb]], [ex])
                                for qs in range(nqs):
                                    a = m * 4 + qs
                                    ab = 4 + a // 3
                                    ao = (a % 3) * 130
                                    kb.mm(ps[ab], PB[ab][:, ao:ao + 129], ex.ap()[:, qs * 128:(qs + 1) * 128],
                                          v1.ap()[:, vt, h, 0:129], [ex, (v1, vt)])
                        A = accs.ap()
                        for bnk in range(3):
                            na = min(3, 8 - bnk * 3)
                            kb.act(A[:, bnk * 3:bnk * 3 + na, 0:129],
                                   PB[4 + bnk][:, 0:na * 130].rearrange("p (a e) -> p a e", a=na)[:, :, 0:129],
                                   AF.Copy, [ps[4 + bnk]], [accs])
                        kb.dve(RCP(rec.ap(), A[:, :, 128]), [accs], [rec])
                        kb.dve(TSM(rec.ap()[:, 4:8], rec.ap()[:, 4:8], lam.ap()[:, 1:2]), [rec, lam], [rec])
                        T1 = tmp1.ap()[:, 0:nqs, :]
                        kb.dve(TT(T1, A[:, 0:nqs, 0:128], bc(rec.ap()[:, 0:nqs].unsqueeze(2), [128, nqs, 128]), ALU.mult),
                               [accs, rec], [tmp1])
                        kb.pool(TT(A[:, 4:4 + nqs, 0:128], A[:, 4:4 + nqs, 0:128],
                                   bc(rec.ap()[:, 4:4 + nqs].unsqueeze(2), [128, nqs, 128]), ALU.mult), [accs, rec], [accs])
                        tq0 = q0 // 128
                        kb.dve(TT(otok.ap()[:, tq0:tq0 + nqs, h * 128:(h + 1) * 128], T1, A[:, 4:4 + nqs, 0:128], ALU.add),
                               [tmp1, accs], [(otok, range(tq0, tq0 + nqs))])
                        yield

        ATTN = attn_gen()
        rrb = {"i": 0}

        def nbank():
            i = rrb["i"]
            rrb["i"] = (i + 1) % 8
            kb.fresh(ps[i])
            return i

        def P3(b, a=4):
            return PB[b].rearrange("p (a e) -> p a e", a=a)

        def phc_gen():
            RW0 = 1536
            raw = [kb.tile("raw%d" % i, [128, nT]) for i in range(2)]
            wr = [kb.tile("wr%d" % i, [128, 8, 512], BF16) for i in range(2)]
            mixo = kb.tile("mixo", [128, nT])
            mixb = kb.tile("mixb", [128, nT], BF16)
            kb.pool(MSET(tw.ap()[64:65, :], 1.0), [], [tw])
            for c14 in range(14):
                wtile = wr[(c14 // 4) % 2]
                if c14 % 4 == 0:
                    ncol = min(512, 1792 - c14 * 128)
                    kb.dma("pool", wtile.ap()[:, :, 0:ncol], win_v[:, :, RW0 + c14 * 128:RW0 + c14 * 128 + ncol], writes=[wtile])

                class _W:
                    pass
                wt = _W()
                wt.t = wtile
                wt.v = wtile.ap()[:, :, (c14 % 4) * 128:(c14 % 4 + 1) * 128]
                rw_ = raw[c14 % 2]
                for n in range(NN):
                    b = nbank()
                    for k in range(8):
                        kb.mm(ps[b], PB[b], wt.v[:, k, :], hT.ap()[:, k, n * 512:(n + 1) * 512],
                              [wt.t, (hT, range(4 * n, 4 * n + 4))])
                    kb.act(rw_.ap()[:, n * 512:(n + 1) * 512], PB[b], AF.Copy, [ps[b]], [rw_])
                if c14 < 4:
                    dst, dt_ = rT.ap()[:, c14, :], rT
                elif c14 < 8:
                    dst, dt_ = krT.ap()[:, c14 - 4, :], krT
                else:
                    dst, dt_ = mixo.ap(), mixo
                kb.act(dst, rw_.ap(), AF.Copy, [rw_, pv], [dt_], scale=P[:, c0o + c14:c0o + c14 + 1])
                for (s0, L) in seqs:
                    kb.dve(STT(dst[:, s0 + 1:s0 + L], rw_.ap()[:, s0:s0 + L - 1], pvc("mu0", c14), dst[:, s0 + 1:s0 + L],
                               ALU.mult, ALU.add), [rw_, pv, dt_], [dt_])
                    kb.dve(STT(dst[:, s0:s0 + L - 1], rw_.ap()[:, s0 + 1:s0 + L], pvc("mu1", c14), dst[:, s0:s0 + L - 1],
                               ALU.mult, ALU.add), [rw_, pv, dt_], [dt_])
                if 8 <= c14 < 12:
                    vc = c14 - 8
                    kb.pool(CP(mixb.ap(), mixo.ap()), [mixo], [mixb])
                    for t8 in range(0, NTT, 8):
                        nt8 = min(8, NTT - t8)
                        b = nbank()
                        for i in range(nt8):
                            kb.transpose(ps[b], PBb[b][:, i * 128:(i + 1) * 128], mixb.ap()[:, (t8 + i) * 128:(t8 + i + 1) * 128],
                                         identB, [mixb, cstb])
                        kb.act(Vt.ap()[:, t8:t8 + nt8, vc * 128:(vc + 1) * 128],
                               PBb[b][:, 0:nt8 * 128].rearrange("p (a e) -> p a e", a=nt8), AF.Copy, [ps[b]],
                               [(Vt, range(t8, t8 + nt8))])
                elif c14 == 12:
                    kb.act(tw.ap()[0:64, :], mixo.ap()[0:64, :], AF.Tanh, [mixo], [tw])
                    kb.dve(CP(alo.ap()[64:128, :], mixo.ap()[64:128, :]), [mixo], [alo])
                elif c14 == 13:
                    kb.act(sg.ap(), mixo.ap(), AF.Sigmoid, [mixo], [sg])
                yield
            for c in range(4):
                kb.dve(TSM(kkT.ap()[:, c, :], krT.ap()[:, c, :], pvc("k_k", c)), [krT, pv], [kkT])
                kb.pool(TT(mixb.ap(), kkT.ap()[:, c, :], kkT.ap()[:, c, :], ALU.mult), [kkT], [mixb])
                for n in range(NN):
                    b = nbank()
                    kb.mm(ps[b], PB[b], blkB, mixb.ap()[:, n * 512:(n + 1) * 512], [cstb, mixb])
                    r_ = raw[0].ap()[:, n * 512:(n + 1) * 512]
                    kb.act(r_, PB[b], AF.Sqrt, [ps[b]], [raw[0]])
                    kb.dve(lambda e, r_=r_: e.tensor_scalar_max(r_, r_, 1e-12), [raw[0]], [raw[0]])
                    kb.dve(RCP(r_, r_), [raw[0]], [raw[0]])
                    kb.dve(TT(kkT.ap()[:, c, n * 512:(n + 1) * 512], kkT.ap()[:, c, n * 512:(n + 1) * 512], r_, ALU.mult),
                           [kkT, raw[0]], [kkT])
                yield

            kb.free(*raw, *wr, mixo, mixb)

        PHC = phc_gen()
        alive = {"a": ATTN, "c": PHC}
        n_att = sum((L // (512 if L >= 512 else L)) * 4 for (_s0, L) in seqs)
        per = -(-18 // n_att)
        while alive:
            for nm_, cnt_ in (("a", 1), ("c", per)):
                for _ in range(cnt_):
                    if nm_ in alive:
                        try:
                            next(alive[nm_])
                        except StopIteration:
                            del alive[nm_]
        kb.free(accs, rec, tmp1, *exb, qT, kT, v1)
        if kcT is not None:
            kb.free(kcT)
        sq = kb.tile("osq", [128, 512])
        ss = kb.tile("oss", [128, 4])
        for tt in range(NTT):
            O = otok.ap()[:, tt, :]
            O3 = O.rearrange("p (h e) -> p h e", h=4)
            kb.pool(TT(sq.ap(), O, O, ALU.mult), [(otok, tt)], [sq])
            kb.dve(RSUM(ss.ap(), sq.ap().rearrange("p (h e) -> p h e", h=4)), [sq], [ss])
            kb.dve(TS(ss.ap(), ss.ap(), 1.0 / 128.0, 1e-5, ALU.mult, ALU.add), [ss], [ss])
            kb.act(ss.ap(), ss.ap(), AF.Sqrt, [ss], [ss])
            kb.dve(RCP(ss.ap(), ss.ap()), [ss], [ss])
            kb.dve(TT(O3, O3, bc(ss.ap().unsqueeze(2), [128, 4, 128]), ALU.mult), [(otok, tt), ss], [(otok, tt)])
            ob = sq.ap().bitcast(BF16)[:, 0:512]
            kb.dve(TT(ob.rearrange("p (h e) -> p h e", h=4), O3, bc(R[:, 1024:1152].unsqueeze(1), [128, 4, 128]),
                      ALU.mult), [(otok, tt), rowp], [sq])
            b = tt % 2
            for h in range(4):
                kb.transpose(ps[b], PBb[b][:, h * 128:(h + 1) * 128], ob[:, h * 128:(h + 1) * 128], identB, [sq, cstb])
            kb.act(oattT.ap()[:, :, tt * 128:(tt + 1) * 128], PBb[b][:, 0:512].rearrange("p (h e) -> p h e", h=4),
                   AF.Copy, [ps[b]], [oattT])
        kb.free(sq, ss, otok)
        if stop_after == "B":
            for c_ in range(4):
                dump(oattT.ap()[:, c_, 0:512], [oattT], 512)
            dump(lam.ap(), [lam], 4)
            break

        if stop_after == "C":
            dump(rT.ap()[:, 0, 0:512], [rT], 512)
            dump(kkT.ap()[:, 1, 0:512], [kkT], 512)
            dump(Vt.ap()[:, 1, :], [Vt], 512)
            dump(tw.ap()[0:65, 0:512], [tw], 512)
            dump(sg.ap()[:, 0:512], [sg], 512)
            break

        kb.free(hT)

        def f3():
            return [128, 4, 128]

        def v3(t):
            return t.ap().rearrange("p (c e) -> p c e", c=4)

        def mkset(k):
            T_ = {}
            for nm in ["b0", "b1", "b2", "b3", "b4", "b5", "Wic", "Wlm", "Ah", "Bt", "Kt", "Rh", "Btok"]:
                T_[nm] = kb.tile("%s_%d" % (nm, k), [128, 512])
            for nm in ["h0", "h1", "rkp", "Ktok"]:
                T_[nm] = kb.tile("%s_%d" % (nm, k), [128, 512], BF16)
            T_["Gt"] = kb.tile("Gt_%d" % k, [128, 4])
            for nm in ["MrbT", "LabTf"]:
                T_[nm] = kb.tile("%s_%d" % (nm, k), [128, 2, 4, 128])
            for nm in ["LakT", "MrkT", "Pm0", "Pm1", "Ptm0", "Ptm1", "Inv0", "Inv1"]:
                T_[nm] = kb.tile("%s_%d" % (nm, k), [128, 2, 4, 128], BF16)
            return T_

        sets = [mkset(0), mkset(1)]
        ka_b = bc(pvc("k_a").unsqueeze(2), f3())
        omka_b = bc(P[:, omkao:omkao + 4].unsqueeze(2), f3())
        ydone = set()
        bdone = set()

        def unit(T_, S, d, tt):
            aT, keff, bq, Wex, Win, sig = T_["b0"], T_["b1"], T_["b2"], T_["b3"], T_["b4"], T_["b5"]
            Z0f, tmpS, Z0b, Xf, U0f, Ub = aT, keff, bq, Wex, Win, sig
            shi, slo = T_["h0"], T_["h1"]
            Xh, rb = shi, slo
            Wic, Wlm, Ah, Bt, Kt, Rh, Btok, Ktok, rkp, Gt = (T_[k_] for k_ in
                                                             ["Wic", "Wlm", "Ah", "Bt", "Kt", "Rh", "Btok", "Ktok", "rkp", "Gt"])
            MrbT, LabTf, LakT, MrkT = T_["MrbT"], T_["LabTf"], T_["LakT"], T_["MrkT"]
            Pm, Ptm, Inv = [T_["Pm0"], T_["Pm1"]], [T_["Ptm0"], T_["Ptm1"]], [T_["Inv0"], T_["Inv1"]]
            mk = dict(labT="lt", lab="gt", mr="le") if d == 0 else dict(labT="gt", lab="lt", mr="ge")
            tf, tl = (0, 127) if d == 0 else (127, 0)
            o = tt * 128
            sl = slice(o, o + 128)
            b0 = nbank()
            for c in range(4):
                kb.mm(ps[b0], PB[b0][:, c * 128:(c + 1) * 128],
                      aupt.ap()[64:128, d * 512 + c * 128:d * 512 + (c + 1) * 128], alo.ap()[64:128, sl], [aupt, alo])
            kb.dve(TT(v3(aT), P3(b0), bc(pvc("a0_%d" % d).unsqueeze(2), f3()), ALU.add), [ps[b0], pv], [aT])
            kb.act(aT.ap(), aT.ap(), AF.Sigmoid, [aT], [aT])
            bz = nbank()
            kb.mm(ps[bz], PB[bz], tw.ap()[0:65, sl], wupa.ap()[0:65, d * 512:(d + 1) * 512], [tw, wupa])
            kb.act(sig.ap(), PB[bz], AF.Sigmoid, [ps[bz]], [sig])
            yield
            kb.gp(TT(v3(keff), v3(aT), ka_b, ALU.mult), [aT, pv], [keff])
            kb.gp(TT(v3(keff), v3(keff), omka_b, ALU.add), [keff, pv], [keff])
            kb.gp(TT(v3(keff), v3(keff), krT.ap()[:, :, sl], ALU.mult), [keff, krT], [keff])
            kb.dve(TT(v3(bq), kkT.ap()[:, :, sl], v3(aT), ALU.mult), [kkT, aT], [bq])
            kb.gp(TT(v3(rkp), rT.ap()[:, :, sl], v3(keff), ALU.mult), [rT, keff], [rkp])
            bB = nbank()
            for c in range(4):
                kb.mm(ps[bB], PB[bB][:, 2 * c:2 * c + 2], v3(rkp)[:, c, :], rkblk.ap()[:, 2 * c:2 * c + 2], [rkp, rkblk])
            if tt not in bdone:
                bdone.add(tt)
                kb.act(bon.ap()[:, tt, :], PB[bB][:, 0:8], AF.Copy, [ps[bB]], [(bon, tt)])
            else:
                kb.dve(TT(bon.ap()[:, tt, :], PB[bB][:, 0:8], bon.ap()[:, tt, :], ALU.add), [ps[bB], (bon, tt)], [(bon, tt)])
            kb.act(shi.ap(), sig.ap(), AF.Copy, [sig], [shi])
            kb.dve(TT(slo.ap(), sig.ap(), shi.ap(), ALU.subtract), [sig, shi], [slo])
            bci = nbank()
            bce = nbank()
            for c in range(4):
                cs = slice(c * 128, (c + 1) * 128)
                kb.mm(ps[bci], PB[bci][:, cs], shi.ap()[:, cs], TIB[d], [shi, cstb])
                kb.mm(ps[bci], PB[bci][:, cs], slo.ap()[:, cs], TIB[d], [slo, cstb])
            for c in range(4):
                cs = slice(c * 128, (c + 1) * 128)
                kb.mm(ps[bce], PB[bce][:, cs], shi.ap()[:, cs], TEB[d], [shi, cstb])
                kb.mm(ps[bce], PB[bce][:, cs], slo.ap()[:, cs], TEB[d], [slo, cstb])
            kb.act(Wex.ap(), PB[bce], AF.Exp, [ps[bce]], [Wex], scale=CL)
            kb.act(Gt.ap(), P3(bce)[:, :, tf], AF.Exp, [ps[bce]], [Gt], scale=-CL)
            kb.act(Win.ap(), PB[bci], AF.Exp, [ps[bci]], [Win], scale=-CL)
            kb.act(Wic.ap(), PB[bci], AF.Exp, [ps[bci]], [Wic], scale=CL)
            yield
            kb.gp(TT(v3(Wlm), bc(C[:, 256:384].unsqueeze(1), f3()), bc(v3(Wic)[:, :, tl:tl + 1], f3()), ALU.mult),
                  [cst, Wic], [Wlm])
            kb.dve(STT(v3(Ah), kkT.ap()[:, :, sl], -1.0, v3(Wex), ALU.mult, ALU.mult), [kkT, Wex], [Ah])
            kb.dve(TT(Bt.ap(), bq.ap(), Win.ap(), ALU.mult), [bq, Win], [Bt])
            kb.dve(TT(Kt.ap(), keff.ap(), Win.ap(), ALU.mult), [keff, Win], [Kt])
            kb.dve(TT(v3(Rh), rT.ap()[:, :, sl], v3(Wic), ALU.mult), [rT, Wic], [Rh])
            bt_ = nbank()
            for c in range(4):
                kb.transpose(ps[bt_], PB[bt_][:, c * 128:(c + 1) * 128], v3(Bt)[:, c, :], identF, [Bt, cst])
            kb.act(Btok.ap(), PB[bt_], AF.Copy, [ps[bt_]], [Btok])
            bt_ = nbank()
            for c in range(4):
                kb.transpose(ps[bt_], PB[bt_][:, c * 128:(c + 1) * 128], v3(Kt)[:, c, :], identF, [Kt, cst])
            kb.act(Ktok.ap(), PB[bt_], AF.Copy, [ps[bt_]], [Ktok])
            yield

            def Lmat(dstt, lt, rt, mname, eng="dve"):
                for h2 in range(2):
                    b = nbank()
                    r0 = 64 * h2
                    for c in range(4):
                        kb.mm(ps[b], PB[b][:, c * 128:(c + 1) * 128], v3(lt)[r0:r0 + 64, c, :], v3(rt)[r0:r0 + 64, c, :], [lt, rt])
                    kb.dve(TT(dstt.ap()[:, h2, :, :], P3(b), bc(MASK[mname].unsqueeze(1), f3()), ALU.mult), [ps[b], cst], [dstt])
                    if FINE_YIELD:
                        yield

            yield from Lmat(LabTf, Bt, Ah, mk["labT"])
            kb.act(Ptm[0].ap(), LabTf.ap(), AF.Copy, [LabTf], [Ptm[0]])
            kb.dve(TT(LabTf.ap().rearrange("p a c e -> p (a c) e"), LabTf.ap().rearrange("p a c e -> p (a c) e"),
                      bc(identF.unsqueeze(1), [128, 8, 128]), ALU.subtract), [LabTf, cst], [LabTf])
            yield from Lmat(Pm[0], Ah, Bt, mk["lab"])
            yield
            yield from Lmat(LakT, Kt, Ah, mk["labT"])
            yield from Lmat(MrbT, Bt, Rh, mk["mr"])
            yield from Lmat(MrkT, Kt, Rh, mk["mr"])
            kb.dve(TT(Inv[0].ap().rearrange("p a c e -> p (a c) e"), Ptm[0].ap().rearrange("p a c e -> p (a c) e"),
                      bc(identB.unsqueeze(1), [128, 8, 128]), ALU.add), [Ptm[0], cstb], [Inv[0]])
            yield
            cur = 0
            for lvl in range(1, 7):
                nx = 1 - cur
                for hg in range(2):
                    b = nbank()
                    for hh in range(4):
                        kb.mm(ps[b], PB[b][:, hh * 128:(hh + 1) * 128], Ptm[cur].ap()[:, hg, hh, :], Pm[cur].ap()[:, hg, hh, :],
                              [Ptm[cur], Pm[cur]])
                    kb.act(Pm[nx].ap()[:, hg, :, :], P3(b), AF.Copy, [ps[b]], [Pm[nx]])
                    if FINE_YIELD:
                        yield
                if lvl < 6:
                    for hg in range(2):
                        b = nbank()
                        for hh in range(4):
                            kb.mm(ps[b], PB[b][:, hh * 128:(hh + 1) * 128], Pm[cur].ap()[:, hg, hh, :], Ptm[cur].ap()[:, hg, hh, :],
                                  [Ptm[cur], Pm[cur]])
                        kb.act(Ptm[nx].ap()[:, hg, :, :], P3(b), AF.Copy, [ps[b]], [Ptm[nx]])
                        if FINE_YIELD:
                            yield
                if not COARSE:
                    yield
                for hg in range(2):
                    b = nbank()
                    for hh in range(4):
                        kb.mm(ps[b], PB[b][:, hh * 128:(hh + 1) * 128], Pm[nx].ap()[:, hg, hh, :], Inv[cur].ap()[:, hg, hh, :],
                              [Pm[nx], Inv[cur]])
                    kb.dve(TT(Inv[nx].ap()[:, hg, :, :], P3(b), Inv[cur].ap()[:, hg, :, :], ALU.add), [ps[b], Inv[cur]], [Inv[nx]])
                    if FINE_YIELD:
                        yield
                cur = nx
                yield
            InvT = Inv[cur]
            kb.dve(TT(v3(Z0f), S.ap(), bc(Gt.ap().unsqueeze(2), f3()), ALU.mult), [S, Gt], [Z0f])
            Z0b = Z0f
            bx = nbank()
            for h in range(8):
                c, h2 = h // 2, h % 2
                hs = slice(h * 64, (h + 1) * 64)
                kb.mm(ps[bx], PB[bx][:, hs], LakT.ap()[:, h2, c, :], Vt.ap()[:, tt, hs], [LakT, (Vt, tt)])
            for c in range(4):
                cs = slice(c * 128, (c + 1) * 128)
                kb.mm(ps[bx], PB[bx][:, cs], v3(Ah)[:, c, :], v3(Z0b)[:, c, :], [Ah, Z0b])
            kb.act(Xf.ap(), PB[bx], AF.Copy, [ps[bx]], [Xf])
            kb.act(Xh.ap(), PB[bx], AF.Copy, [ps[bx]], [Xh])
            yield
            bu = nbank()
            for h in range(8):
                hs = slice(h * 64, (h + 1) * 64)
                kb.mm(ps[bu], PB[bu][:, hs], InvT.ap()[:, h % 2, h // 2, :], Xh.ap()[:, hs], [InvT, Xh])
            kb.act(U0f.ap(), PB[bu], AF.Copy, [ps[bu]], [U0f])
            yield
            bl = nbank()
            for h in range(8):
                hs = slice(h * 64, (h + 1) * 64)
                kb.mm(ps[bl], PB[bl][:, hs], LabTf.ap()[:, h % 2, h // 2, :], U0f.ap()[:, hs], [LabTf, U0f])
            kb.dve(TT(rb.ap(), PB[bl], Xf.ap(), ALU.add), [ps[bl], Xf], [rb])
            yield
            bd_ = nbank()
            for h in range(8):
                hs = slice(h * 64, (h + 1) * 64)
                kb.mm(ps[bd_], PB[bd_][:, hs], InvT.ap()[:, h % 2, h // 2, :], rb.ap()[:, hs], [InvT, rb])
            kb.dve(TT(Ub.ap(), PB[bd_], U0f.ap(), ALU.add), [ps[bd_], U0f], [Ub])
            yield
            by = nbank()
            for h in range(8):
                c, h2 = h // 2, h % 2
                hs = slice(h * 64, (h + 1) * 64)
                kb.mm(ps[by], PB[by][:, hs], MrkT.ap()[:, h2, c, :], Vt.ap()[:, tt, hs], [MrkT, (Vt, tt)])
            for c in range(4):
                cs = slice(c * 128, (c + 1) * 128)
                kb.mm(ps[by], PB[by][:, cs], v3(Rh)[:, c, :], v3(Z0b)[:, c, :], [Rh, Z0b])
            for h in range(8):
                c, h2 = h // 2, h % 2
                hs = slice(h * 64, (h + 1) * 64)
                kb.mm(ps[by], PB[by][:, hs], MrbT.ap()[:, h2, c, :], Ub.ap()[:, hs], [MrbT, Ub])
            if tt not in ydone:
                ydone.add(tt)
                kb.act(yac.ap()[:, tt, :], PB[by], AF.Copy, [ps[by]], [(yac, tt)])
            else:
                kb.dve(TT(yac.ap()[:, tt, :], PB[by], yac.ap()[:, tt, :], ALU.add), [ps[by], (yac, tt)], [(yac, tt)])
            bs = nbank()
            for c in range(4):
                cs = slice(c * 128, (c + 1) * 128)
                kb.mm(ps[bs], PB[bs][:, cs], Ktok.ap()[:, cs], Vt.ap()[:, tt, cs], [Ktok, (Vt, tt)])
            for c in range(4):
                cs = slice(c * 128, (c + 1) * 128)
                kb.mm(ps[bs], PB[bs][:, cs], Btok.ap()[:, cs], Ub.ap()[:, cs], [Btok, Ub])
            kb.dve(TT(tmpS.ap(), PB[bs], Z0f.ap(), ALU.add), [ps[bs], Z0f], [tmpS])
            kb.dve(TT(S.ap(), v3(tmpS), v3(Wlm), ALU.mult), [tmpS, Wlm], [S])
            yield

        def chain(T_, si, s0, L, d):
            S = kb.tile("S_%d_%d" % (si, d), f3())
            if sample:
                stn = kb.tile("stn%d" % d, [128, 4, 64])
                bd = kb.tile("bd%d" % d, f3())
                kb.dma("sp", stn.ap(), st_d[d].rearrange("(c p) k -> p c k", p=128), writes=[stn])
                kb.dve(MSET(bd.ap(), 0.0), [], [bd])
                kb.dve(CP(bd.ap()[0:64, :, 0:64], stn.ap()[0:64, :, :]), [stn], [bd])
                kb.dve(CP(bd.ap()[64:128, :, 64:128], stn.ap()[64:128, :, :]), [stn], [bd])
                b = nbank()
                for c in range(4):
                    kb.transpose(ps[b], PB[b][:, c * 128:(c + 1) * 128], bd.ap()[:, c, :], identF, [bd, cst])
                kb.dve(CP(S.ap(), P3(b)), [ps[b]], [S])
                kb.free(stn, bd)
            else:
                kb.dve(MSET(S.ap(), 0.0), [], [S])
            tts = list(range(s0 // 128, (s0 + L) // 128))
            if d == 1:
                tts = tts[::-1]
            for tt in tts:
                yield from unit(T_, S, d, tt)
            if not sample:
                bo = nbank()
                for c in range(4):
                    kb.transpose(ps[bo], PB[bo][:, c * 128:(c + 1) * 128], S.ap()[:, c, :], identF, [S, cst])
                so = kb.tile("so%d" % d, [128, 4, 64])
                kb.dve(CP(so.ap()[0:64], P3(bo)[0:64, :, 0:64]), [ps[bo]], [so])
                kb.dve(CP(so.ap()[64:128], P3(bo)[64:128, :, 64:128]), [ps[bo]], [so])
                kb.dma("sp", ns_d[si, d].rearrange("(c p) k -> p c k", p=128), so.ap(), reads=[so], is_output=True)
                kb.free(so)
            kb.free(S)

        for si, (s0, L) in enumerate(seqs):
            gens = [chain(sets[0], si, s0, L, 0), chain(sets[1], si, s0, L, 1)]
            alive = list(gens)
            for _ in range(SCAN_OFFSET):
                next(gens[0])
            while alive:
                for g_ in list(alive):
                    try:
                        next(g_)
                    except StopIteration:
                        alive.remove(g_)
        for T_ in sets:
            kb.free(*T_.values())
        if stop_after == "scan":
            dump(yac.ap()[:, 0, :], [yac], 512)
            dump(yac.ap()[:, NTT - 1, :], [yac], 512)
            dump(bon.ap()[:, 0, :], [bon], 8)
            break

        yrwT = kb.tile("yrwT", [128, 4, nT], BF16)
        st1 = kb.tile("st1", [128, 8])
        st2 = kb.tile("st2", [128, 8])
        msq = kb.tile("msq", [128, 8])
        ysq = kb.tile("ysq", [128, 512])
        ygb = kb.tile("ygb", [128, 512], BF16)
        xT = kb.tile("xT", [128, 8, nT], F32, nslots=NN)
        hT = kb.tile("hT2", [128, 8, nT], BF16, nslots=NTT)
        ecx = {"i": 0}

        def emit_x(c, tt, pap, pst):
            out = xT.ap()[:, c, tt * 128:(tt + 1) * 128]
            outh = hT.ap()[:, c, tt * 128:(tt + 1) * 128]
            kb.act(out, pap, AF.Copy, [pst], [(xT, tt // 4)], scale=ALPHA_C)
            kb.act(outh, pap, AF.Identity, [pst, modT], [(hT, tt)], bias=mod("sh1", c, g), scale=mod("sc1", c, g))

        woa = kb.tile("woa", [128, 4, D], BF16)
        wor = kb.tile("wor", [128, 4, D], BF16)
        wout = kb.tile("wout", [128, 8, D], BF16)
        kb.dma("pool", woa.ap(), w_oa.rearrange("(c p) n -> p c n", p=128), writes=[woa])
        kb.dma("pool", wor.ap(), w_or.rearrange("(c p) n -> p c n", p=128), writes=[wor])
        kb.dma("pool", wout.ap(), w_out.rearrange("(c p) n -> p c n", p=128), writes=[wout])
        def phd_gen():
            for tt in range(NTT):
                o = tt * 128
                Y = yac.ap()[:, tt, :]
                Y3 = Y.rearrange("p (h e) -> p h e", h=8)
                yt = [(yac, tt)]
                kb.dve(RSUM(st1.ap(), Y3), yt, [st1])
                kb.pool(TT(ysq.ap(), Y, Y, ALU.mult), yt, [ysq])
                kb.dve(RSUM(st2.ap(), ysq.ap().rearrange("p (h e) -> p h e", h=8)), [ysq], [st2])
                kb.dve(TSM(st1.ap(), st1.ap(), 1.0 / 64.0), [st1], [st1])
                kb.dve(TT(msq.ap(), st1.ap(), st1.ap(), ALU.mult), [st1], [msq])
                kb.dve(STT(st2.ap(), st2.ap(), 1.0 / 64.0, msq.ap(), ALU.mult, ALU.subtract), [st2, msq], [st2])
                kb.dve(TSA(st2.ap(), st2.ap(), 64e-5), [st2], [st2])
                kb.act(st2.ap(), st2.ap(), AF.Sqrt, [st2], [st2])
                kb.dve(RCP(st2.ap(), st2.ap()), [st2], [st2])
                kb.dve(TT(Y3, Y3, bc(st1.ap().unsqueeze(2), [128, 8, 64]), ALU.subtract), yt + [st1], yt)
                kb.dve(TT(Y3, Y3, bc(st2.ap().unsqueeze(2), [128, 8, 64]), ALU.mult), yt + [st2], yt)
                kb.pool(TT(Y, Y, R[:, 0:512], ALU.mult), yt + [rowp], yt)
                kb.pool(TT(Y, Y, R[:, 512:1024], ALU.add), yt + [rowp], yt)
                kb.dve(TT(ysq.ap().rearrange("p (h e) -> p h e", h=8), Vt.ap()[:, tt, :].rearrange("p (h e) -> p h e", h=8),
                          bc(bon.ap()[:, tt, :].unsqueeze(2), [128, 8, 64]), ALU.mult), [(Vt, tt), (bon, tt)], [ysq])
                kb.pool(TT(Y, Y, ysq.ap(), ALU.add), yt + [ysq], yt)
                bg = nbank()
                kb.mm(ps[bg], PB[bg], sg.ap()[:, o:o + 128], gup.ap(), [sg, gup])
                kb.dve(TT(ygb.ap(), Y, PB[bg], ALU.mult), yt + [ps[bg]], [ygb])
                bt_ = nbank()
                for c in range(4):
                    kb.transpose(ps[bt_], PBb[bt_][:, c * 128:(c + 1) * 128], ygb.ap()[:, c * 128:(c + 1) * 128], identB, [ygb, cstb])
                kb.act(yrwT.ap()[:, :, o:o + 128], PBb[bt_][:, 0:512].rearrange("p (c e) -> p c e", c=4), AF.Copy, [ps[bt_]], [yrwT])
                yield

        XLD = load_x_T_gen(ps, PB, t0, nT, emit_x)
        alive = [phd_gen(), XLD]
        while alive:
            for g_ in list(alive):
                try:
                    next(g_)
                except StopIteration:
                    alive.remove(g_)
        kb.free(st1, st2, msq, ysq, ygb, rT, krT, kkT, Vt, tw, alo, sg, yac, bon)
        if stop_after == "D":
            for c_ in range(4):
                dump(yrwT.ap()[:, c_, 0:512], [yrwT], 512)
            break

        mixpre = kb.tile("mixpre", [128, 8, nT], BF16, nslots=NN)
        wg = [kb.tile("wg%d" % i, [128, 8, 1024], BF16) for i in range(2)]
        gat = [kb.tile("gat%d" % i, [128, 512]) for i in range(4)]
        G0 = 3328
        for j in range(8):
            w_ = wg[(j // 4) % 2]
            jj = j % 4
            if jj == 0:
                kb.dma("pool", w_.ap()[:, :, 0:512], win_v[:, :, G0 + j * 128:G0 + j * 128 + 512], writes=[w_])
                kb.dma("pool", w_.ap()[:, :, 512:1024], win_v[:, :, G0 + 1024 + j * 128:G0 + 1024 + j * 128 + 512], writes=[w_])
            for n in range(NN):
                ns_ = slice(n * 512, (n + 1) * 512)
                hs_ = (hT, range(4 * n, 4 * n + 4))
                b1 = nbank()
                for c in range(4):
                    kb.mm(ps[b1], PB[b1], woa.ap()[:, c, j * 128:(j + 1) * 128], oattT.ap()[:, c, ns_], [woa, oattT])
                b2 = nbank()
                for c in range(4):
                    kb.mm(ps[b2], PB[b2], wor.ap()[:, c, j * 128:(j + 1) * 128], yrwT.ap()[:, c, ns_], [wor, yrwT])
                b3 = nbank()
                for k in range(8):
                    kb.mm(ps[b3], PB[b3], w_.ap()[:, k, jj * 128:(jj + 1) * 128], hT.ap()[:, k, ns_], [w_, hs_])
                b4 = nbank()
                for k in range(8):
                    kb.mm(ps[b4], PB[b4], w_.ap()[:, k, 512 + jj * 128:512 + (jj + 1) * 128], hT.ap()[:, k, ns_], [w_, hs_])
                ga, gb_, m1, m2 = gat
                kb.act(ga.ap(), PB[b3], AF.Sigmoid, [ps[b3]], [ga])
                kb.act(gb_.ap(), PB[b4], AF.Sigmoid, [ps[b4]], [gb_])
                kb.dve(TT(m1.ap(), ga.ap(), PB[b1], ALU.mult), [ga, ps[b1]], [m1])
                kb.dve(TT(m2.ap(), gb_.ap(), PB[b2], ALU.mult), [gb_, ps[b2]], [m2])
                kb.pool(TT(mixpre.ap()[:, j, ns_], m1.ap(), m2.ap(), ALU.add), [m1, m2], [(mixpre, n)])
        kb.free(woa, wor, *wg, *gat, oattT, yrwT)
        for j2 in range(8):
            for n in range(NN):
                ns_ = slice(n * 512, (n + 1) * 512)
                b = nbank()
                for j in range(8):
                    kb.mm(ps[b], PB[b], wout.ap()[:, j, j2 * 128:(j2 + 1) * 128], mixpre.ap()[:, j, ns_], [wout, (mixpre, n)])
                kb.dve(STT(xT.ap()[:, j2, ns_], PB[b], mod("g1", j2, g), xT.ap()[:, j2, ns_], ALU.mult, ALU.add),
                       [ps[b], modT, (xT, n)], [(xT, n)])
        kb.free(wout, mixpre)

        def layer_norm_T(gname, bname):
            means = [kb.tile("lnmean%d" % i, [128, 512]) for i in range(NN)]
            vars_ = [kb.tile("lnvar%d" % i, [128, 512]) for i in range(NN)]
            sqt = [kb.tile("lnsq%d" % i, [128, 512]) for i in range(2)]
            tmp = [kb.tile("lntmp%d" % i, [128, 512]) for i in range(2)]
            for n in range(NN):
                ns_ = slice(n * 512, (n + 1) * 512)
                xs = [(xT, n)]
                mean, var = means[n], vars_[n]
                bm = nbank()
                for j in range(8):
                    kb.mm(ps[bm], PB[bm], onesd, xT.ap()[:, j, ns_], [cst] + xs)
                bq_ = nbank()
                for j in range(8):
                    sq_ = sqt[j % 2]
                    kb.act(sq_.ap(), xT.ap()[:, j, ns_], AF.Square, xs, [sq_])
                    kb.mm(ps[bq_], PB[bq_], onesd, sq_.ap(), [cst, sq_])
                kb.act(mean.ap(), PB[bm], AF.Copy, [ps[bm]], [mean])
                kb.dve(TT(var.ap(), mean.ap(), mean.ap(), ALU.mult), [mean], [var])
                kb.dve(TT(var.ap(), PB[bq_], var.ap(), ALU.subtract), [ps[bq_], var], [var])
                kb.dve(TSA(var.ap(), var.ap(), 1e-5), [var], [var])
                kb.act(var.ap(), var.ap(), AF.Sqrt, [var], [var])
                kb.dve(RCP(var.ap(), var.ap()), [var], [var])
            for n in range(NN):
                ns_ = slice(n * 512, (n + 1) * 512)
                xs = [(xT, n)]
                mean, var = means[n], vars_[n]
                for j in range(8):
                    t_ = tmp[j % 2]
                    kb.dve(TT(t_.ap(), xT.ap()[:, j, ns_], mean.ap(), ALU.subtract), xs + [mean], [t_])
                    kb.dve(TT(t_.ap(), t_.ap(), var.ap(), ALU.mult), [t_, var], [t_])
                    kb.act(xT.ap()[:, j, ns_], t_.ap(), AF.Identity, [t_, pv], xs, bias=pvc(bname, j), scale=pvc(gname, j))
            kb.free(*means, *vars_, *sqt, *tmp)

        wu = [kb.tile("wu%d" % i, [128, 8, 1024], BF16) for i in range(2)]
        wup_v = w_up.rearrange("(k p) n -> p k n", p=128)
        kb.dma("pool", wu[0].ap()[:, :, 0:512], wup_v[:, :, 0:512], writes=[wu[0]])
        kb.dma("pool", wu[0].ap()[:, :, 512:1024], wup_v[:, :, D_FF:D_FF + 512], writes=[wu[0]])
        layer_norm_T("ln1_g", "ln1_b")
        for c in range(8):
            for n in range(NN):
                ns_ = slice(n * 512, (n + 1) * 512)
                kb.act(hT.ap()[:, c, ns_], xT.ap()[:, c, ns_], AF.Identity, [(xT, n), modT], [(hT, range(4 * n, 4 * n + 4))],
                       bias=mod("sh2", c, g), scale=mod("sc2", c, g))
        for c in range(8):
            for n in range(NN):
                ns_ = slice(n * 512, (n + 1) * 512)
                kb.pool(TSM(xT.ap()[:, c, ns_], xT.ap()[:, c, ns_], ALPHA_C), [(xT, n)], [(xT, n)])
        if stop_after == "E":
            dump(xT.ap()[:, 0, 0:512], [xT], 512)
            dump(xT.ap()[:, 7, 0:512], [xT], 512)
            dump(xT.ap()[:, 3, nT - 512:nT], [xT], 512)
            dump(hT.ap()[:, 3, nT - 512:nT], [hT], 512)
            break

        fT = kb.tile("fT", [128, 22, nT], BF16, nslots=NN)
        uraw = [kb.tile("uraw%d" % i, [128, nT]) for i in range(2)]
        uacc = [kb.tile("uacc%d" % i, [128, nT]) for i in range(2)]
        wd0 = kb.tile("wd0", [128, 22, 512], BF16)
        wdn_v = w_down.rearrange("(f p) n -> p f n", p=128)
        kb.dma("pool", wd0.ap()[:, 0:11, :], wdn_v[:, 0:11, 0:512], writes=[wd0])
        kb.dma("pool", wd0.ap()[:, 11:22, :], wdn_v[:, 11:22, 0:512], writes=[wd0])
        for f in range(22):
            w_ = wu[(f // 4) % 2]
            fj = f % 4
            if fj == 0 and f > 0:
                ncol = min(512, D_FF - f * 128)
                kb.dma("pool", w_.ap()[:, :, 0:ncol], wup_v[:, :, f * 128:f * 128 + ncol], writes=[w_])
                kb.dma("pool", w_.ap()[:, :, 512:512 + ncol], wup_v[:, :, D_FF + f * 128:D_FF + f * 128 + ncol], writes=[w_])
            ur, ua = uraw[f % 2], uacc[f % 2]
            bv = []
            for n in range(NN):
                ns_ = slice(n * 512, (n + 1) * 512)
                hs_ = (hT, range(4 * n, 4 * n + 4))
                b = nbank()
                for k in range(8):
                    kb.mm(ps[b], PB[b], w_.ap()[:, k, fj * 128:(fj + 1) * 128], hT.ap()[:, k, ns_], [w_, hs_])
                kb.act(ur.ap()[:, ns_], PB[b], AF.Copy, [ps[b]], [ur])
                b = nbank()
                for k in range(8):
                    kb.mm(ps[b], PB[b], w_.ap()[:, k, 512 + fj * 128:512 + (fj + 1) * 128], hT.ap()[:, k, ns_], [w_, hs_])
                bv.append(b)
            kb.dve(TS(ua.ap(), ur.ap(), pvc("cw1", f), pvc("cb", f), ALU.mult, ALU.add), [ur, pv], [ua])
            for (s0, L) in seqs:
                kb.dve(STT(ua.ap()[:, s0 + 1:s0 + L], ur.ap()[:, s0:s0 + L - 1], pvc("cw0", f), ua.ap()[:, s0 + 1:s0 + L],
                           ALU.mult, ALU.add), [ur, pv, ua], [ua])
                kb.dve(STT(ua.ap()[:, s0:s0 + L - 1], ur.ap()[:, s0 + 1:s0 + L], pvc("cw2", f), ua.ap()[:, s0:s0 + L - 1],
                           ALU.mult, ALU.add), [ur, pv, ua], [ua])
            kb.act(ua.ap(), ua.ap(), AF.Gelu_apprx_tanh, [ua], [ua])
            for n in range(NN):
                ns_ = slice(n * 512, (n + 1) * 512)
                kb.dve(TT(fT.ap()[:, f, ns_], ua.ap()[:, ns_], PB[bv[n]], ALU.mult), [ua, ps[bv[n]]], [(fT, n)])
        if stop_after == "F1":
            dump(fT.ap()[:, 0, 0:512], [fT], 512)
            dump(fT.ap()[:, 21, nT - 512:nT], [fT], 512)
            dump(fT.ap()[:, 10, nT - 512:nT], [fT], 512)
            dump(uacc[1].ap()[:, nT - 512:nT], [uacc[1]], 512)
            break
        kb.free(*wu, *uraw, *uacc, hT)
        wd = [wd0, kb.tile("wd1", [128, 22, 512], BF16)]
        for j in range(8):
            w_ = wd[(j // 4) % 2]
            jj = j % 4
            if jj == 0 and j > 0:
                kb.dma("pool", w_.ap()[:, 0:11, :], wdn_v[:, 0:11, j * 128:j * 128 + 512], writes=[w_])
                kb.dma("pool", w_.ap()[:, 11:22, :], wdn_v[:, 11:22, j * 128:j * 128 + 512], writes=[w_])
            for n in range(NN):
                ns_ = slice(n * 512, (n + 1) * 512)
                b = nbank()
                for f in range(22):
                    kb.mm(ps[b], PB[b], w_.ap()[:, f, jj * 128:(jj + 1) * 128], fT.ap()[:, f, ns_], [w_, (fT, n)])
                kb.dve(STT(xT.ap()[:, j, ns_], PB[b], mod("g2", j, g), xT.ap()[:, j, ns_], ALU.mult, ALU.add),
                       [ps[b], modT, (xT, n)], [(xT, n)])
        kb.free(*wd, fT)
        layer_norm_T("ln2_g", "ln2_b")
        ytok = [kb.tile("ytok%d" % i, [128, D]) for i in range(2)]
        for tt in range(NTT):
            yt_ = ytok[tt % 2]
            for half in range(2):
                b = nbank()
                for cc in range(4):
                    c = half * 4 + cc
                    kb.transpose(ps[b], PB[b][:, cc * 128:(cc + 1) * 128], xT.ap()[:, c, tt * 128:(tt + 1) * 128], identF,
                                 [(xT, tt // 4), cst])
                evac_copy(yt_.ap()[:, half * 512:(half + 1) * 512], PB[b], [ps[b]], [yt_])
            kb.dma("sp", y_d[t0 + tt * 128:t0 + (tt + 1) * 128, :], yt_.ap(), reads=[yt_], is_output=True)
        kb.free(*ytok, xT)
    dbg = {}
    kb.finish()
    return nc, kb


def prep_inputs(inp):
    f = lambda a: np.ascontiguousarray(np.asarray(a, np.float32))
    shared = {}
    for k_, n_ in [("w_ada", "w_ada"), ("w_in", "w_in"), ("w_o_attn", "w_oa"), ("w_o_rwkv", "w_or"), ("w_out", "w_out"),
                   ("w_up", "w_up"), ("w_down", "w_down")]:
        shared[n_] = f(inp[k_][0])
    pvec = np.zeros((128, NPV), np.float32)

    def put(name, v):
        o, n = PV[name]
        pvec[:, o:o + n] = fm(v)

    put("b_ada", inp["b_ada"][0])
    put("mu0", inp["rw_mu"][0, 0])
    put("mu1", inp["rw_mu"][0, 1])
    put("a0_0", inp["rw_a0"][0, 0])
    put("a0_1", inp["rw_a0"][0, 1])
    put("k_k", inp["rw_k_k"][0])
    put("k_a", inp["rw_k_a"][0])
    put("ln1_g", inp["ln1_g"][0])
    put("ln1_b", inp["ln1_b"][0])
    put("ln2_g", inp["ln2_g"][0])
    put("ln2_b", inp["ln2_b"][0])
    put("cw0", inp["conv_w"][0, 0])
    put("cw1", inp["conv_w"][0, 1])
    put("cw2", inp["conv_w"][0, 2])
    put("cb", inp["conv_b"][0])
    shared["pvec"] = pvec
    rowv = np.concatenate([f(inp["rw_lnx_g"][0]), f(inp["rw_lnx_b"][0]), f(inp["da_subln_g"][0]),
                           f(inp["da_lambda"][0]).reshape(-1)])
    shared["rowp"] = np.ascontiguousarray(np.broadcast_to(rowv[None, :], (128, 1408)))
    rk = f(inp["rw_r_k"][0])
    rkblk = np.zeros((128, 4, 2), np.float32)
    for c in range(4):
        for j in range(2):
            rkblk[64 * j:64 * j + 64, c, j] = rk[2 * c + j]
    shared["rkblk"] = rkblk.reshape(128, 8)
    wupa = np.zeros((65, 2, 512), np.float32)
    wupa[0:64] = np.transpose(f(inp["rw_w_up"][0]), (1, 0, 2))
    wupa[64] = f(inp["rw_w0"][0])
    shared["wupa"] = wupa.reshape(65, 1024)
    aupt = np.zeros((128, 2, 512), np.float32)
    aupt[64:128] = np.transpose(f(inp["rw_a_up"][0]), (1, 0, 2))
    shared["aupt"] = aupt.reshape(128, 1024)
    shared["gup"] = f(inp["rw_g_up"][0])
    shared["cst"] = make_consts()
    xs = f(inp["x_sample"])
    xp = f(inp["x_prompt"])
    cctx = f(inp["c_ctx"])
    cc = f(inp["c"])
    maps = []
    for i in range(8):
        m = dict(shared)
        m["xin"] = np.concatenate([xs[i], xp[2 * i], xp[2 * i + 1]], axis=0)
        cvec = np.zeros((128, 8, 2), np.float32)
        cvec[:, :, 0] = fm(cc[i])
        cvec[:, :, 1] = fm(cctx)
        m["cvec"] = cvec.reshape(128, 16)
        m["ck"] = f(inp["cache_k"][i, 0]).reshape(256, 512)
        m["cv"] = f(inp["cache_v"][i, 0]).reshape(256, 512)
        m["st"] = f(inp["state_rwkv"][i, 0]).reshape(2, 512, 64)
        maps.append(m)
    return maps


def assemble(results):
    y_p = np.zeros((16, 256, D), np.float32)
    y_s = np.zeros((8, 1024, D), np.float32)
    nk = np.zeros((16, 1, 256, 4, 2, 64), np.float32)
    nv = np.zeros((16, 1, 256, 4, 128), np.float32)
    ns = np.zeros((16, 1, 2, 8, 64, 64), np.float32)
    for i, r in enumerate(results):
        y = r["y"]
        y_s[i] = y[0:1024]
        y_p[2 * i] = y[1024:1280]
        y_p[2 * i + 1] = y[1280:1536]
        nk[2 * i, 0] = r["nk"][0:256].reshape(256, 4, 2, 64)
        nk[2 * i + 1, 0] = r["nk"][256:512].reshape(256, 4, 2, 64)
        nv[2 * i, 0] = r["nv"][0:256].reshape(256, 4, 128)
        nv[2 * i + 1, 0] = r["nv"][256:512].reshape(256, 4, 128)
        ns[2 * i, 0] = r["ns"][0].reshape(2, 8, 64, 64)
        ns[2 * i + 1, 0] = r["ns"][1].reshape(2, 8, 64, 64)
    return y_p, y_s, nk, nv, ns


_CACHE = {}


def kernel(**inputs):
    if "nc" not in _CACHE:
        _CACHE["nc"] = build_program()[0]
    maps = prep_inputs(inputs)
    res = run_bass_kernel_spmd(_CACHE["nc"], maps, core_ids=list(range(8)))
    return assemble(res.results)
```
